# Optimizing a Trainium2 kernel written in Bass

```python
import math
import jax, jax.numpy as jnp
from jax import lax
import numpy as np

D_MODEL = 1024
BATCH = 4
SEQ = 4096
DEPTH = 4

HEAD_DIM = 64
D_MIX = D_MODEL
W_FNET = D_MIX // 4
FNET_GROUPS = W_FNET // HEAD_DIM
N_HEADS_GQA = D_MIX // 4 // HEAD_DIM
N_KV_GQA = 2
W_CONV = D_MIX // 4
CONV_WIDTH = 31
N_HEADS_DIL = D_MIX // 4 // HEAD_DIM
DIL_CONFIGS = ((128, 1), (512, 4), (2048, 16))
D_FF = 4 * D_MODEL
GRID_W = 64
ROPE_THETA = 10000.0
Q_BLOCK = 128
REL_BUCKETS = 32
REL_MAX_DIST = 1024
ALPHA = (2 * DEPTH) ** 0.25
BETA = (8 * DEPTH) ** -0.25
LN_EPS = 1e-5
RMS_EPS = 1e-6
NEG = -1e30
IN_WIDTHS = (W_FNET, N_HEADS_GQA * HEAD_DIM, N_KV_GQA * HEAD_DIM, N_KV_GQA * HEAD_DIM,
             2 * W_CONV, N_HEADS_DIL * HEAD_DIM, N_HEADS_DIL * HEAD_DIM, N_HEADS_DIL * HEAD_DIM)
D_IN = sum(IN_WIDTHS)

kernel_name = "hybrid_parallel_fnet_gqa_conv_dilated_encoder"


def split_points():
    pts, acc = [], 0
    for w in IN_WIDTHS[:-1]:
        acc += w
        pts.append(acc)
    return pts


def layer_norm(x, g, b):
    xf = x.astype(jnp.float32)
    mu = xf.mean(-1, keepdims=True)
    var = jnp.square(xf - mu).mean(-1, keepdims=True)
    return ((xf - mu) * lax.rsqrt(var + LN_EPS) * g.astype(jnp.float32) + b.astype(jnp.float32)).astype(x.dtype)


def rms_norm(x, g):
    xf = x.astype(jnp.float32)
    return (xf * lax.rsqrt(jnp.mean(xf * xf, -1, keepdims=True) + RMS_EPS) * g.astype(jnp.float32)).astype(x.dtype)


def axial_rope_tables(S):
    rows = S // GRID_W
    row = jnp.repeat(jnp.arange(rows), GRID_W).astype(jnp.float32)
    col = jnp.tile(jnp.arange(GRID_W), rows).astype(jnp.float32)
    nf = HEAD_DIM // 4
    inv = ROPE_THETA ** (-jnp.arange(nf, dtype=jnp.float32) / nf)
    ang = jnp.concatenate([row[:, None] * inv, col[:, None] * inv], -1)
    return jnp.cos(ang), jnp.sin(ang)


def apply_axial_rope(x, cos, sin):
    B, S, H, hd = x.shape
    nf = hd // 4
    xs = x.astype(jnp.float32).reshape(B, S, H, 2, 2, nf)
    x1, x2 = xs[..., 0, :], xs[..., 1, :]
    c = cos.reshape(S, 1, 2, nf)
    s = sin.reshape(S, 1, 2, nf)
    out = jnp.stack([x1 * c - x2 * s, x1 * s + x2 * c], axis=-2).reshape(B, S, H, hd)
    return out.astype(x.dtype)


def t5_bucket(rel):
    nb = REL_BUCKETS // 2
    max_exact = nb // 2
    ret = jnp.where(rel > 0, nb, 0)
    n = jnp.abs(rel)
    nf = jnp.maximum(n, 1).astype(jnp.float32)
    large = max_exact + (jnp.log(nf / max_exact) / math.log(REL_MAX_DIST / max_exact)
                         * (nb - max_exact)).astype(jnp.int32)
    large = jnp.minimum(large, nb - 1)
    return ret + jnp.where(n < max_exact, n, large)


def fourier_mix(u, w):
    B, S, _ = u.shape
    a = u.astype(jnp.float32).reshape(B, S, FNET_GROUPS, HEAD_DIM)
    f = jnp.fft.fft2(a, axes=(1, 3), norm='ortho').real
    return f.reshape(B, S, W_FNET).astype(u.dtype) @ w


def gqa_attention(q, k, v):
    B, S, H, hd = q.shape
    G = k.shape[2]
    R = H // G
    nqb = S // Q_BLOCK
    qb = q.reshape(B, nqb, Q_BLOCK, G, R, hd).transpose(1, 0, 2, 3, 4, 5)
    scale = hd ** -0.5

    def block(qblk):
        s = jnp.einsum('bqgrd,bkgd->bgrqk', qblk, k).astype(jnp.float32) * scale
        p = jax.nn.softmax(s, axis=-1).astype(v.dtype)
        return jnp.einsum('bgrqk,bkgd->bqgrd', p, v)

    o = lax.map(block, qb)
    return o.transpose(1, 0, 2, 3, 4, 5).reshape(B, S, H * hd)


def conformer_conv(u, dw, b, g, beta, w_pw):
    a, gate = jnp.split(u, 2, axis=-1)
    h = a * jax.nn.sigmoid(gate)
    pad = CONV_WIDTH // 2
    h = lax.conv_general_dilated(h, dw[:, None, :], window_strides=(1,), padding=[(pad, pad)],
                                 dimension_numbers=('NWC', 'WIO', 'NWC'),
                                 feature_group_count=W_CONV) + b
    h = jax.nn.silu(layer_norm(h, g, beta))
    return h @ w_pw


def dilated_branch(q, k, v, dil, n, rel_bias):
    B, S, H, hd = q.shape
    L = S // dil

    def to_res(t):
        return t.reshape(B, L, dil, H, hd).transpose(0, 2, 1, 3, 4).reshape(B * dil, L, H, hd)

    qr, kr, vr = to_res(q), to_res(k), to_res(v)
    nb = -(-L // n)
    Lp = nb * n
    qr = jnp.pad(qr, ((0, 0), (0, Lp - L), (0, 0), (0, 0)))
    kp = jnp.pad(kr, ((0, 0), (n, Lp - L + n), (0, 0), (0, 0)))
    vp = jnp.pad(vr, ((0, 0), (n, Lp - L + n), (0, 0), (0, 0)))

    def band(t):
        tb = t.reshape(B * dil, nb + 2, n, H, hd)
        return jnp.concatenate([tb[:, :-2], tb[:, 1:-1], tb[:, 2:]], axis=2)

    kw, vw = band(kp), band(vp)
    qb = qr.reshape(B * dil, nb, n, H, hd)
    s = jnp.einsum('bnqhd,bnkhd->bnhqk', qb, kw).astype(jnp.float32) * (hd ** -0.5)
    qi = jnp.arange(n)
    ki = jnp.arange(3 * n)
    rel = ki[None, :] - n - qi[:, None]
    kpos = jnp.arange(nb)[:, None] * n - n + ki[None, :]
    valid = (jnp.abs(rel) <= n)[None] & ((kpos >= 0) & (kpos < L))[:, None, :]
    bias = rel_bias[t5_bucket(rel * dil)].astype(jnp.float32).transpose(2, 0, 1)
    logits = jnp.where(valid[None, :, None], s + bias[None, None], NEG)
    m = logits.max(-1, keepdims=True)
    p = jnp.exp(logits - m)
    den = p.sum(-1, keepdims=True)
    o = jnp.einsum('bnhqk,bnkhd->bnqhd', (p / den).astype(v.dtype), vw)
    lse = (m + jnp.log(den))[..., 0].transpose(0, 1, 3, 2)
    o = o.reshape(B * dil, Lp, H, hd)[:, :L]
    lse = lse.reshape(B * dil, Lp, H)[:, :L]

    def from_res(t):
        return t.reshape(B, dil, L, *t.shape[2:]).swapaxes(1, 2).reshape(B, S, *t.shape[2:])

    return from_res(o), from_res(lse)


def dilated_mixture(q, k, v, rel_bias):
    outs, lses = [], []
    for window, dil in DIL_CONFIGS:
        o, lse = dilated_branch(q, k, v, dil, window // (2 * dil), rel_bias)
        outs.append(o)
        lses.append(lse)
    w = jax.nn.softmax(jnp.stack(lses, 0), axis=0)
    out = jnp.sum(w[..., None].astype(q.dtype) * jnp.stack(outs, 0), axis=0)
    B, S, H, hd = q.shape
    return out.reshape(B, S, H * hd)


def setup_inputs(seed: int = 0) -> dict:
    key = jax.random.key(seed)
    ks = jax.random.split(key, 24)
    f32 = jnp.float32
    nrm = lambda k, shape: jax.random.normal(k, shape, f32)
    return {
        'x': nrm(ks[0], (BATCH, SEQ, D_MODEL)),
        'emb_ln_g': 1.0 + 0.02 * nrm(ks[1], (D_MODEL,)),
        'emb_ln_b': 0.02 * nrm(ks[2], (D_MODEL,)),
        'w_in': nrm(ks[3], (DEPTH, D_MODEL, D_IN)) * D_MODEL ** -0.5,
        'w_fnet': nrm(ks[4], (DEPTH, W_FNET, W_FNET)) * W_FNET ** -0.5,
        'q_norm_g': 1.0 + 0.02 * nrm(ks[5], (DEPTH, HEAD_DIM)),
        'k_norm_g': 1.0 + 0.02 * nrm(ks[6], (DEPTH, HEAD_DIM)),
        'conv_dw': nrm(ks[7], (DEPTH, CONV_WIDTH, W_CONV)) * CONV_WIDTH ** -0.5,
        'conv_b': 0.02 * nrm(ks[8], (DEPTH, W_CONV)),
        'conv_ln_g': 1.0 + 0.02 * nrm(ks[9], (DEPTH, W_CONV)),
        'conv_ln_b': 0.02 * nrm(ks[10], (DEPTH, W_CONV)),
        'w_conv_out': nrm(ks[11], (DEPTH, W_CONV, W_CONV)) * W_CONV ** -0.5,
        'w_out': nrm(ks[12], (DEPTH, D_MIX, D_MODEL)) * (D_MIX ** -0.5 * BETA),
        'ln1_g': 1.0 + 0.02 * nrm(ks[13], (DEPTH, D_MODEL)),
        'ln1_b': 0.02 * nrm(ks[14], (DEPTH, D_MODEL)),
        'w_ff1': nrm(ks[15], (DEPTH, D_MODEL, D_FF)) * D_MODEL ** -0.5,
        'w_ff2': nrm(ks[16], (DEPTH, D_FF, D_MODEL)) * (D_FF ** -0.5 * BETA),
        'ln2_g': 1.0 + 0.02 * nrm(ks[17], (DEPTH, D_MODEL)),
        'ln2_b': 0.02 * nrm(ks[18], (DEPTH, D_MODEL)),
        'rel_bias': 0.2 * nrm(ks[19], (REL_BUCKETS, N_HEADS_DIL)),
    }


def reference(x, emb_ln_g, emb_ln_b, w_in, w_fnet, q_norm_g, k_norm_g, conv_dw, conv_b,
              conv_ln_g, conv_ln_b, w_conv_out, w_out, ln1_g, ln1_b, w_ff1, w_ff2,
              ln2_g, ln2_b, rel_bias):
    B, S, _ = x.shape
    cos, sin = axial_rope_tables(S)
    pts = split_points()
    x = layer_norm(x, emb_ln_g, emb_ln_b)
    for l in range(DEPTH):
        u = x @ w_in[l]
        u_f, q_b, k_b, v_b, u_c, q_d, k_d, v_d = jnp.split(u, pts, axis=-1)
        y_a = fourier_mix(u_f, w_fnet[l])
        qh = apply_axial_rope(rms_norm(q_b.reshape(B, S, N_HEADS_GQA, HEAD_DIM), q_norm_g[l]), cos, sin)
        kh = apply_axial_rope(rms_norm(k_b.reshape(B, S, N_KV_GQA, HEAD_DIM), k_norm_g[l]), cos, sin)
        vh = v_b.reshape(B, S, N_KV_GQA, HEAD_DIM)
        y_b = gqa_attention(qh, kh, vh)
        y_c = conformer_conv(u_c, conv_dw[l], conv_b[l], conv_ln_g[l], conv_ln_b[l], w_conv_out[l])
        y_d = dilated_mixture(q_d.reshape(B, S, N_HEADS_DIL, HEAD_DIM),
                              k_d.reshape(B, S, N_HEADS_DIL, HEAD_DIM),
                              v_d.reshape(B, S, N_HEADS_DIL, HEAD_DIM), rel_bias)
        y = jnp.concatenate([y_a, y_b, y_c, y_d], axis=-1) @ w_out[l]
        x = layer_norm(ALPHA * x + y, ln1_g[l], ln1_b[l])
        h = jnp.square(jax.nn.relu(x @ w_ff1[l]))
        x = layer_norm(ALPHA * x + h @ w_ff2[l], ln2_g[l], ln2_b[l])
    return x
```

```python
import math
import os
import numpy as np
BSTEP = int(os.environ.get("BSTEP", "99"))
import ml_dtypes
import concourse.bass as bass
import concourse.mybir as mybir
from concourse.bass_utils import run_bass_kernel_spmd

F32 = mybir.dt.float32
BF16 = mybir.dt.bfloat16
AF = mybir.ActivationFunctionType
ALU = mybir.AluOpType
AX = mybir.AxisListType

S = 4096
D = 1024
NT = 32
DEPTH = 4
DFF = 4096
ALPHA = (2 * DEPTH) ** 0.25
LN_EPS = 1e-5
RMS_EPS = 1e-6
DILS = (1, 4, 16)
MASKV = -240000.0

ENGS = ("pe", "act", "dve", "pool", "sp")


class Prog:
    def __init__(self, nc, n_dma_slots=48):
        self.nc = nc
        self.ops = []
        self.n_dma_slots = n_dma_slots
        self.eng_obj = {"pe": nc.tensor, "act": nc.scalar, "dve": nc.vector,
                        "pool": nc.gpsimd, "sp": nc.sync}

    skip = False

    def op(self, eng, fn, reads=(), writes=(), dma=False):
        if self.skip:
            return
        ps_r = tuple(k for k in reads if isinstance(k, tuple) and k[0] == "ps" and k not in writes)
        self.ops.append((eng, fn, tuple(reads), tuple(writes) + ps_r, dma))

    def dma(self, out, in_, reads, writes, q="sp", **kw):
        e = self.eng_obj[q]
        self.op(q, lambda: e.dma_start(out=out, in_=in_, **kw), reads, writes, dma=True)

    def barrier(self, fn):
        if self.skip:
            return
        self.ops.append(("pool", fn, "BARRIER", (), False))

    def emit(self, final_wait_keys=()):
        nc = self.nc
        ops = self.ops
        n = len(ops)
        last_w = {}
        readers = {}
        deps = [None] * n
        bar = None
        last_eng = {}
        dma_since = []
        for i, (eng, fn, reads, writes, is_dma) in enumerate(ops):
            if reads == "BARRIER":
                d = set(last_eng.values()) | set(dma_since)
                if bar is not None:
                    d.add(bar)
                deps[i] = d
                last_w = {}
                readers = {}
                dma_since = []
                last_eng = {}
                bar = i
                ops[i] = (eng, fn, (), (), False)
                continue
            if is_dma:
                dma_since.append(i)
            else:
                last_eng[eng] = i
            d = set()
            if bar is not None:
                d.add(bar)
            for k in reads:
                if k in last_w:
                    d.add(last_w[k])
            for k in writes:
                if k in last_w:
                    d.add(last_w[k])
                for r in readers.get(k, ()):
                    d.add(r)
            d.discard(i)
            deps[i] = d
            for k in reads:
                readers.setdefault(k, []).append(i)
            for k in writes:
                last_w[k] = i
                readers[k] = []
        final_deps = set()
        for k in final_wait_keys:
            if k in last_w:
                final_deps.add(last_w[k])
        needed = set()
        for i in range(n):
            eng_i, _, _, _, dma_i = ops[i]
            keep = set()
            for j in deps[i]:
                eng_j, _, _, _, dma_j = ops[j]
                if (not dma_j) and (not dma_i) and eng_i == "pe" and eng_j == "pe":
                    continue
                keep.add(j)
            deps[i] = keep
            needed |= keep
        needed |= final_deps
        sems = {e: nc.alloc_semaphore(name=f"s_{e}") for e in ENGS}
        slots = [nc.alloc_semaphore(name=f"s_dma{k}") for k in range(self.n_dma_slots)]
        cnt = {e: 0 for e in ENGS}
        slot_use = [0] * self.n_dma_slots
        half = self.n_dma_slots // 2
        slot_rr = {"sw": 0, "hw": 0}
        done_tok = [None] * n
        prev_slot_tok = [None] * n
        for i, (eng, fn, reads, writes, is_dma) in enumerate(ops):
            if is_dma:
                kind = "sw" if eng == "pool" else "hw"
                s = slot_rr[kind] + (half if kind == "sw" else 0)
                slot_rr[kind] = (slot_rr[kind] + 1) % half
                if slot_use[s] > 0:
                    prev_slot_tok[i] = (("slot", s), 16 * slot_use[s])
                slot_use[s] += 1
                done_tok[i] = (("slot", s), 16 * slot_use[s])
            elif i in needed:
                cnt[eng] += 1
                done_tok[i] = (("eng", eng), cnt[eng])
        seen = {e: {} for e in ENGS}

        def semh(key):
            return sems[key[1]] if key[0] == "eng" else slots[key[1]]

        n_wait = 0
        for i, (eng, fn, reads, writes, is_dma) in enumerate(ops):
            e = self.eng_obj[eng]
            want = {}
            for j in deps[i]:
                key, val = done_tok[j]
                if want.get(key, 0) < val:
                    want[key] = val
            if prev_slot_tok[i] is not None:
                key, val = prev_slot_tok[i]
                if want.get(key, 0) < val:
                    want[key] = val
            for key, val in want.items():
                if seen[eng].get(key, 0) >= val:
                    continue
                e.wait_ge(semh(key), val)
                seen[eng][key] = val
                n_wait += 1
            ins = fn()
            if is_dma:
                ins.then_inc(semh(done_tok[i][0]), 16)
            elif done_tok[i] is not None:
                ins.then_inc(semh(done_tok[i][0]), 1)
        e = self.eng_obj["sp"]
        want = {}
        for j in final_deps:
            key, val = done_tok[j]
            if want.get(key, 0) < val:
                want[key] = val
        for key, val in want.items():
            e.wait_ge(semh(key), val)
        self.stats = dict(n_ops=n, n_wait=n_wait, cnt=dict(cnt))
        return self.stats


def fv(ap, off, dims):
    return bass.AP(ap.tensor, ap.offset + off, [list(ap.ap[0])] + [list(d) for d in dims])


def build_program(n_layers=DEPTH, debug=False, stop_after=None):
    nc = bass.Bass("TRN2", target_bir_lowering=False)
    P = Prog(nc)

    def din(name, shape, dt=F32):
        return nc.dram_tensor(name, list(shape), dt, kind="ExternalInput").ap()

    def dscr(name, shape, dt):
        return nc.dram_tensor(name, list(shape), dt, kind="Internal").ap()

    x_in = din("x", [S, D])
    emb_g = din("emb_ln_g", [D]); emb_b = din("emb_ln_b", [D])
    w_in = din("w_in", [DEPTH, D, 2048])
    w_fnet = din("w_fnet", [DEPTH, 256, 256])
    qg = din("q_norm_g", [DEPTH, 64]); kg = din("k_norm_g", [DEPTH, 64])
    conv_dw = din("conv_dw", [DEPTH, 31, 256]); conv_b = din("conv_b", [DEPTH, 256])
    cln_g = din("conv_ln_g", [DEPTH, 256]); cln_b = din("conv_ln_b", [DEPTH, 256])
    w_pw = din("w_conv_out", [DEPTH, 256, 256])
    w_out = din("w_out", [DEPTH, D, D])
    ln1_g = din("ln1_g", [DEPTH, D]); ln1_b = din("ln1_b", [DEPTH, D])
    w_ff1 = din("w_ff1", [DEPTH, D, DFF]); w_ff2 = din("w_ff2", [DEPTH, DFF, D])
    ln2_g = din("ln2_g", [DEPTH, D]); ln2_b = din("ln2_b", [DEPTH, D])
    dbias = din("dbias", [4, 3, 2, 128, 128])
    c_ident = din("c_ident", [128, 128])
    c_rope = din("c_rope", [S, 64])
    c_bcs = din("c_bcs", [256, 512])
    c_r = din("c_r", [128, 128])
    c_t3 = din("c_t3", [128, 4096])
    c_dmask = din("c_dmask", [2, 128, 128])
    out = nc.dram_tensor("out", [S, D], F32, kind="ExternalOutput").ap()

    X = dscr("X", [S, D], F32)
    XT = dscr("XT", [8, 128, S], BF16)
    YT = dscr("YT", [8, 128, S], BF16)
    VD = dscr("VD", [S, 128], BF16)
    B8d = dscr("B8d", [24, 128, 128], BF16)
    if debug:
        dbg_yt = nc.dram_tensor("dbg_yt", [8, 128, S], BF16, kind="ExternalOutput").ap()
        dbg_x1 = nc.dram_tensor("dbg_x1", [S, D], F32, kind="ExternalOutput").ap()

    def sb(name, shape, dt):
        return nc.alloc_sbuf_tensor(name, list(shape), dt)

    PS = nc.alloc_psum_tensor("PS", [128, 4096], F32)

    def bank(k, w=512):
        return PS[:, k * 512:k * 512 + w]

    def pk(k, nb=1):
        return [("ps", k + i) for i in range(nb)]

    ident = sb("ident", [128, 128], BF16)
    identf = sb("identf", [128, 128], F32)
    onesF = sb("onesF", [128, 128], F32)
    ones64 = sb("ones64", [128, 64], BF16)
    mh = sb("mh", [128, 8], F32)
    xTc = [sb(f"xTc{i}", [128, 8, 512], BF16) for i in range(2)]
    WA = sb("WA", [128, 8, 1024], BF16)
    big1 = sb("big1", [128, 16384], F32)
    big2 = sb("big2", [128, 16384], F32)
    lng = sb("lng", [128, 1024], F32); lnb = sb("lnb", [128, 1024], F32)
    rt = [sb(f"rt{i}", [128, 1024], F32) for i in range(2)]
    xt_in = [sb(f"xin{i}", [128, 1024], F32) for i in range(2)]
    xnb = [sb(f"xnb{i}", [128, 1024], BF16) for i in range(2)]
    xTs = [sb("xTs0", [128, 8, 256], BF16)] * 2
    st6 = sb("st6", [128, 2, 2, 6], F32)
    mv2 = sb("mv2", [128, 2, 2], F32)
    rstd2 = sb("rstd2", [128, 2], F32)
    small = sb("small", [128, 2048], F32)
    bar_t = sb("bar_t", [128, 8], F32)

    def mm(o, l, r, st, sp, R, W):
        P.op("pe", lambda: nc.tensor.matmul(o, lhsT=l, rhs=r, start=st, stop=sp), R, W)

    def tr(o, i, idn, R, W):
        P.op("pe", lambda: nc.tensor.transpose(o, i, idn), R, W)

    def act(o, i, f, R, W, scale=None, bias=None):
        kw = {}
        if scale is not None:
            kw["scale"] = scale
        if bias is not None:
            kw["bias"] = bias
        P.op("act", lambda: nc.scalar.activation(out=o, in_=i, func=f, **kw), R, W)

    def cp(eng, o, i, R, W):
        e = P.eng_obj[eng]
        if eng == "act":
            P.op("act", lambda: nc.scalar.copy(out=o, in_=i), R, W)
        else:
            P.op(eng, lambda: e.tensor_copy(out=o, in_=i), R, W)

    def tt(eng, o, a, b, op, R, W):
        e = P.eng_obj[eng]
        P.op(eng, lambda: e.tensor_tensor(out=o, in0=a, in1=b, op=op), R, W)

    def ts(eng, o, a, s1, s2, op0, op1, R, W):
        e = P.eng_obj[eng]
        if op1 is None:
            P.op(eng, lambda: e.tensor_scalar(out=o, in0=a, scalar1=s1, scalar2=None, op0=op0), R, W)
        else:
            P.op(eng, lambda: e.tensor_scalar(out=o, in0=a, scalar1=s1, scalar2=s2, op0=op0, op1=op1), R, W)

    def stt(o, a, s, b, op0, op1, R, W):
        P.op("dve", lambda: nc.vector.scalar_tensor_tensor(out=o, in0=a, scalar=s, in1=b, op0=op0, op1=op1), R, W)

    def memset(eng, o, v, W):
        e = P.eng_obj[eng]
        P.op(eng, lambda: e.memset(o, v), [], W)

    def wload(dst, src, R, W):
        P.dma(dst, src, R, W, q="pool")

    wload(ident[:], c_ident, [], ["ident"])
    P.dma(identf[:], c_ident, [], ["identf"])
    memset("pool", onesF[:], 1.0 / 256.0, ["onesF"])
    memset("pool", ones64[:], 1.0, ["ones64"])
    memset("pool", mh[:], -0.5, ["mh"])
    for h in range(4):
        for di in range(3):
            for ab in range(2):
                k = (h * 3 + di) * 2 + ab
                P.dma(small[:, 0:128], dbias[h, di, ab], [], ["small"])
                P.dma(small[:, 128:256], c_dmask[ab], [], ["small"])
                b8s = fv(small[:].bitcast(BF16), 1024, [[1, 128]])
                stt(small[:, 256:384], small[:, 0:128], 8.0, small[:, 128:256], ALU.mult, ALU.add, ["small"], ["b8f"])
                act(b8s, small[:, 256:384], AF.Exp, ["b8f"], ["b8s"], scale=0.125)
                P.dma(B8d[k], b8s, ["b8s"], ["B8d"])

    def ln_pair(r_aps, rkeys, xn_aps, xnkeys, rstd_on_pool=False):
        for k in range(2):
            r_ap = r_aps[k]
            P.op("dve", lambda r_ap=r_ap, k=k: nc.vector.bn_stats(out=st6[:, k, 0, :], in_=r_ap[:, 0:512]), [rkeys[k]], [f"st6_{k}"])
            P.op("dve", lambda r_ap=r_ap, k=k: nc.vector.bn_stats(out=st6[:, k, 1, :], in_=r_ap[:, 512:1024]), [rkeys[k]], [f"st6_{k}"])
            P.op("dve", lambda k=k: nc.vector.bn_aggr(out=mv2[:, k, :], in_=st6[:, k].rearrange("p a b -> p (a b)")), [f"st6_{k}"], ["mv2"])
        if rstd_on_pool:
            ts("pool", rstd2[:], fv(mv2[:], 1, [[2, 2]]), LN_EPS, None, ALU.add, None, ["mv2"], ["rstd2"])
            tt("pool", rstd2[:], rstd2[:], mh[:, 0:2], ALU.pow, ["rstd2", "mh"], ["rstd2"])
        else:
            ts("dve", rstd2[:], fv(mv2[:], 1, [[2, 2]]), LN_EPS, None, ALU.add, None, ["mv2"], ["rstd2"])
            act(rstd2[:], rstd2[:], AF.Sqrt, ["rstd2"], ["rstd2"])
            P.op("dve", lambda: nc.vector.reciprocal(out=rstd2[:], in_=rstd2[:]), ["rstd2"], ["rstd2"])
        for k in range(2):
            ts("dve", xn_aps[k], r_aps[k], mv2[:, k, 0:1], rstd2[:, k:k + 1], ALU.subtract, ALU.mult, [rkeys[k], "mv2", "rstd2"], [xnkeys[k]])
            tt("dve", xn_aps[k], xn_aps[k], lng[:], ALU.mult, [xnkeys[k], "lng"], [xnkeys[k]])
            tt("dve", xn_aps[k], xn_aps[k], lnb[:], ALU.add, [xnkeys[k], "lng"], [xnkeys[k]])

    def transpose_tile(xn_ap, xnkey, dst_ap, dstkey, i):
        cp("act", xnb[i][:], xn_ap, [xnkey], [f"xnb{i}"])
        pst = bank(7).bitcast(BF16)
        for c in range(8):
            tr(pst[:, c * 128:(c + 1) * 128], xnb[i][:, c * 128:(c + 1) * 128], ident[:], [f"xnb{i}", "ident"], pk(7))
        cp("dve", dst_ap, pst.rearrange("p (c t) -> p c t", c=8), pk(7), [dstkey])

    def load_ln_params(g_ap, b_ap):
        P.dma(lng[:], g_ap.partition_broadcast(128), [], ["lng"])
        P.dma(lnb[:], b_ap.partition_broadcast(128), [], ["lng"])

    XTv = XT.rearrange("c p s -> p c s")
    YTv = YT.rearrange("c p s -> p c s")

    def ln_store(xn_tiles, t0, final=False, write_xt=True):
        pass

    load_ln_params(emb_g, emb_b)
    for tp in range(NT // 2):
        for k in range(2):
            t = 2 * tp + k
            P.dma(xt_in[k][:], x_in[t * 128:(t + 1) * 128, :], [], [f"xin{k}"])
        ln_pair([xt_in[0][:], xt_in[1][:]], ["xin0", "xin1"], [rt[0][:], rt[1][:]], ["rt0", "rt1"])
        for k in range(2):
            t = 2 * tp + k
            P.dma(X[t * 128:(t + 1) * 128, :], rt[k][:], [f"rt{k}"], [("X", t)], q="pool")
            transpose_tile(rt[k][:], f"rt{k}", xTs[0][:, :, k * 128:(k + 1) * 128], "xTs0", k)
        P.dma(XTv[:, :, tp * 256:(tp + 1) * 256], xTs[0][:], ["xTs0"], [("XT", tp // 2)], q="pool")

    P.barrier(lambda: nc.gpsimd.memset(bar_t[:], 0.0))

    def load_xT(j, buf):
        P.dma(xTc[buf][:], XTv[:, :, j * 512:(j + 1) * 512], [("XT", j)], [f"xTc{buf}"])

    b1 = big1[:].bitcast(BF16)
    b2 = big2[:].bitcast(BF16)
    smb = small[:].bitcast(BF16)

    order = ["p0", "A", "B0", "B1", "B2", "B", "C", "D", "O", "F"]
    def run_phase(ph):
        return stop_after is None or order.index(ph) <= order.index(stop_after)

    for l in range(n_layers):
        win_v = w_in[l].rearrange("(c p) n -> p c n", p=128)

        P.skip = not run_phase("A")
        wload(WA[:, :, 0:256], win_v[:, :, 0:256], [], ["WA"])
        BCS = fv(smb, 0, [[512, 2], [1, 512]])
        Wf = fv(smb, 1024, [[256, 2], [1, 256]])
        Wcs = fv(smb, 1536, [[512, 2], [1, 512]])
        Rr = fv(smb, 2560, [[1, 128]])
        wload(BCS, c_bcs.rearrange("(c p) n -> p c n", p=128), [], ["small"])
        wload(Wf, w_fnet[l].rearrange("(c p) n -> p c n", p=128), [], ["small"])
        wload(Rr, c_r, [], ["small"])
        T3 = fv(b2, 16384, [[1, 4096]])
        wload(T3, c_t3, [], ["big2"])
        for m in range(2):
            mm(bank(m)[:, 0:256], BCS[:, m, m * 128:(m + 1) * 128], Wf[:, m, :], True, True, ["small"], pk(m))
            mm(bank(m)[:, 256:512], BCS[:, m, 256 + m * 128:256 + (m + 1) * 128], Wf[:, m, :], True, True, ["small"], pk(m))
            cp("dve", Wcs[:, m, :], bank(m), pk(m), ["small"])
        ufT = fv(b1, 0, [[4096, 2], [1, 4096]])
        load_xT(0, 0)
        for j in range(8):
            if j + 1 < 8:
                load_xT(j + 1, (j + 1) % 2)
            xc = xTc[j % 2]
            for m in range(2):
                for c in range(8):
                    mm(bank(m), WA[:, c, m * 128:(m + 1) * 128], xc[:, c, :], c == 0, c == 7, ["WA", f"xTc{j % 2}"], pk(m))
                cp("act" if m == 0 else "dve", ufT[:, m, j * 512:(j + 1) * 512], bank(m), pk(m), [("ufT", j)])
        sk = P.skip
        P.skip = not run_phase("B0")
        wload(WA[:, :, 0:512], win_v[:, :, 256:768], [], ["WA"])
        P.skip = sk
        A_sb = fv(b1, 8192, [[256, 64], [1, 256]])
        ufkeys = [("ufT", j) for j in range(8)]
        for s1 in range(64):
            pb = s1 % 2
            for c in range(2):
                l_ap = fv(b1, c * 4096 + s1, [[64, 64]])
                mm(bank(pb)[0:64, 0:256], l_ap, Wcs[:, c, 0:256], c == 0, c == 1, ufkeys + ["small"], pk(pb))
            for c in range(2):
                l_ap = fv(b1, c * 4096 + s1, [[64, 64]])
                mm(bank(pb)[64:128, 0:256], l_ap, Wcs[:, c, 256:512], c == 0, c == 1, ufkeys + ["small"], pk(pb))
            cp("act" if pb == 0 else "dve", A_sb[:, s1, :], bank(pb)[:, 0:256], pk(pb), ["A_sb"])
        Y_sb = fv(b2, 0, [[64, 256], [1, 64]])
        for cb in range(32):
            pb = cb % 2
            for cc in range(8):
                ch = cb * 8 + cc
                l_ap = fv(b1, 8192 + ch, [[256, 64]])
                mm(bank(pb)[0:64, cc * 64:(cc + 1) * 64], l_ap, Rr[:, 0:64], True, True, ["A_sb", "small"], pk(pb))
                mm(bank(pb)[64:128, cc * 64:(cc + 1) * 64], l_ap, Rr[:, 64:128], True, True, ["A_sb", "small"], pk(pb))
            cp("act" if pb == 0 else "dve", fv(b2, cb * 512, [[1, 512]]), bank(pb), pk(pb), ["Y_sb"])
        yA = fv(b1, 24576, [[4096, 2], [1, 4096]])
        for m in range(2):
            for kb in range(8):
                pb = kb % 2
                for ks in range(8):
                    k2 = kb * 8 + ks
                    l_ap = fv(b2, m * 128 * 64 + k2, [[64, 128]])
                    o_ap = fv(bank(pb), ks, [[8, 64]])
                    mm(o_ap, l_ap, T3[:, k2 * 64:(k2 + 1) * 64], True, True, ["Y_sb", "big2"], pk(pb))
                o_sb = fv(b1, 24576 + m * 4096 + kb * 8, [[64, 64], [1, 8]])
                i_ps = fv(bank(pb), 0, [[8, 64], [1, 8]])
                cp("act" if pb == 0 else "dve", o_sb, i_ps, pk(pb), [("yA", m)])
            P.dma(YTv[:, m, :], yA[:, m, :], [("yA", m)], [("YT", m)], q="pool")

        P.barrier(lambda: nc.gpsimd.memset(bar_t[:], 0.0))
        P.skip = not run_phase("B0")
        QKT = fv(b1, 0, [[4096, 3], [1, 4096]])
        Vaug = fv(b1, 12288, [[256, 32], [128, 2], [1, 128]])
        rope = fv(big2[:], 0, [[64, 32], [1, 64]])
        P.dma(rope, c_rope.rearrange("(t p) f -> p t f", p=128), [], ["rope"])
        for hh in range(4):
            P.dma(fv(big2[:], 2048 + hh * 64, [[1, 64]]), qg[l].partition_broadcast(128), [], ["gqk"])
        for hh in range(2):
            P.dma(fv(big2[:], 2048 + 256 + hh * 64, [[1, 64]]), kg[l].partition_broadcast(128), [], ["gqk"])
        memset("dve", fv(b1, 12288 + 64, [[128, 64], [1, 64]]), 1.0, ["Vaug_ones"])
        QKo, TMPo, SSo, RSo, QBo = 2560, 8704, 14848, 14944, 20480
        P.skip = not run_phase("B1")
        load_xT(0, 0)
        for half in range(2):
            for k in range(16):
                t = half * 16 + k
                j = t // 4
                if t % 4 == 0 and j + 1 < 8:
                    load_xT(j + 1, (j + 1) % 2)
                xc = xTc[j % 2]
                tl = (t % 4) * 128
                pb = t % 2
                for c in range(8):
                    mm(bank(pb), xc[:, c, tl:tl + 128], WA[:, c, 0:512], c == 0, c == 7, ["WA", f"xTc{j % 2}"], pk(pb))
                cp("act", fv(big2[:], QKo + k * 384, [[1, 384]]), bank(pb)[:, 0:384], pk(pb), ["QKh"])
                cp("dve", fv(b1, 12288 + t * 256, [[128, 2], [1, 64]]), fv(bank(pb), 384, [[64, 2], [1, 64]]), pk(pb), [("Vaug", t)])
            QKf = fv(big2[:], QKo, [[1, 6144]])
            TMPf = fv(big2[:], TMPo, [[1, 6144]])
            tt("dve", TMPf, QKf, QKf, ALU.mult, ["QKh"], ["TMP"])
            P.op("dve", lambda: nc.vector.tensor_reduce(out=fv(big2[:], SSo, [[1, 96]]), in_=fv(big2[:], TMPo, [[64, 96], [1, 64]]),
                                                        axis=AX.X, op=ALU.add), ["TMP"], ["ss"])
            ts("pool", fv(big2[:], RSo, [[1, 96]]), fv(big2[:], SSo, [[1, 96]]), 1.0 / 64.0, RMS_EPS, ALU.mult, ALU.add, ["ss"], ["rs"])
            tt("pool", fv(big2[:], RSo, [[1, 96]]), fv(big2[:], RSo, [[1, 96]]), fv(mh[:], 0, [[0, 96]]), ALU.pow, ["rs", "mh"], ["rs"])
            tt("dve", fv(big2[:], QKo, [[64, 96], [1, 64]]), fv(big2[:], QKo, [[64, 96], [1, 64]]), fv(big2[:], RSo, [[1, 96], [0, 64]]),
               ALU.mult, ["QKh", "rs"], ["QKh"])
            tt("dve", fv(big2[:], QKo, [[384, 16], [1, 384]]), fv(big2[:], QKo, [[384, 16], [1, 384]]), fv(big2[:], 2048, [[0, 16], [1, 384]]),
               ALU.mult, ["QKh", "gqk"], ["QKh"])
            for h6 in range(6):
                eng = "dve" if h6 < 3 else "pool"
                x1 = fv(big2[:], QKo + h6 * 64, [[384, 16], [32, 2], [1, 16]])
                x2 = fv(big2[:], QKo + h6 * 64 + 16, [[384, 16], [32, 2], [1, 16]])
                cosb = fv(big2[:], half * 16 * 64, [[64, 16], [16, 2], [1, 16]])
                sinb = fv(big2[:], half * 16 * 64 + 32, [[64, 16], [16, 2], [1, 16]])
                t1 = fv(big2[:], TMPo + h6 * 1024, [[32, 16], [16, 2], [1, 16]])
                t2 = fv(big2[:], TMPo + h6 * 1024 + 512, [[32, 16], [16, 2], [1, 16]])
                slot = [0, 2, 1, 3, 4, 5][h6]
                o1 = fv(b1, QBo + slot * 64, [[384, 16], [32, 2], [1, 16]])
                o2 = fv(b1, QBo + slot * 64 + 16, [[384, 16], [32, 2], [1, 16]])
                tt(eng, t1, x1, cosb, ALU.mult, ["QKh", "rope", "TMP"], [("t1", h6)])
                tt(eng, t2, x2, sinb, ALU.mult, ["QKh", "rope", "TMP"], [("t2", h6)])
                tt(eng, o1, t1, t2, ALU.subtract, [("t1", h6), ("t2", h6)], [("qb", h6)])
                tt(eng, t1, x1, sinb, ALU.mult, ["QKh", "rope", ("qb", h6)], [("t1", h6)])
                tt(eng, t2, x2, cosb, ALU.mult, ["QKh", "rope", ("qb", h6)], [("t2", h6)])
                tt(eng, o2, t1, t2, ALU.add, [("t1", h6), ("t2", h6), "QKh"], [("qb", h6)])
            qbk = [("qb", h6) for h6 in range(6)]
            for k in range(16):
                t = half * 16 + k
                pst = bank(6 + k % 2).bitcast(BF16)
                for k3 in range(3):
                    tr(pst[:, k3 * 128:(k3 + 1) * 128], fv(b1, QBo + k * 384 + k3 * 128, [[1, 128]]), ident[:], qbk + ["ident"], pk(6 + k % 2))
                cp("act" if k % 2 == 0 else "dve", fv(b1, t * 128, [[4096, 3], [1, 128]]), fv(pst, 0, [[128, 3], [1, 128]]), pk(6 + k % 2), [("QKT", t)])
        sk = P.skip
        P.skip = not run_phase("C")
        wload(WA[:, :, 0:512], win_v[:, :, 768:1280], [], ["WA"])
        P.skip = sk
        P.skip = not run_phase("B")
        qkt_keys = [("QKT", t) for t in range(NT)]
        yB = fv(b2, 16384, [[4096, 2], [1, 4096]])
        PT = [fv(smb, i * 1024, [[1, 1024]]) for i in range(2)]
        rec = fv(big2[:], 12288, [[1, 2048]])
        osb = fv(big2[:], 14336, [[1, 2048]])
        iters = [(qc, kt) for qc in range(8) for kt in range(NT)]

        def b_S(it, hb):
            qc, kt = iters[it]
            for g in range(2):
                pr = slice(g * 64, (g + 1) * 64)
                mm(bank(hb * 2 + g), QKT[pr, 2, kt * 128:(kt + 1) * 128], QKT[pr, hb, qc * 512:(qc + 1) * 512], True, True,
                   qkt_keys, pk(hb * 2 + g))

        def b_E(it, hb):
            act(PT[hb], PS[:, hb * 1024:hb * 1024 + 1024], AF.Exp, pk(hb * 2, 2), [f"PT{hb}"], scale=0.125)

        def b_P(it, hb):
            qc, kt = iters[it]
            for g in range(2):
                mm(bank(4 + hb * 2 + g), Vaug[:, kt, g, :], PT[hb][:, g * 512:(g + 1) * 512], kt == 0, kt == NT - 1,
                   [f"PT{hb}", ("Vaug", kt), "Vaug_ones"], pk(4 + hb * 2 + g))
            if kt == NT - 1 and hb == 1:
                for hf in range(2):
                    cp("dve", rec[0:64, hf * 1024:(hf + 1) * 1024], PS[64:128, 2048 + hf * 1024:2048 + (hf + 1) * 1024], pk(4 + 2 * hf, 2), ["rec"])
                    cp("act", osb[0:64, hf * 1024:(hf + 1) * 1024], PS[0:64, 2048 + hf * 1024:2048 + (hf + 1) * 1024], pk(4 + 2 * hf, 2), ["osb"])
                P.op("dve", lambda: nc.vector.reciprocal(out=rec[0:64, :], in_=rec[0:64, :]), ["rec"], ["rec"])
                for hb2 in range(2):
                    for g in range(2):
                        bk = hb2 * 2 + g
                        tt("dve", yB[hb2 * 64:(hb2 + 1) * 64, g, qc * 512:(qc + 1) * 512], osb[0:64, bk * 512:(bk + 1) * 512],
                           rec[0:64, bk * 512:(bk + 1) * 512], ALU.mult, ["osb", "rec"], [("yB", g)])
                if qc == 7:
                    for g in range(2):
                        P.dma(YTv[:, 2 + g, :], yB[:, g, :], [("yB", g)], [("YT", 2 + g)], q="pool")

        b_S(0, 0)
        b_S(0, 1)
        for it in range(len(iters)):
            for hb in range(2):
                b_E(it, hb)
                b_P(it, hb)
                if it + 1 < len(iters):
                    b_S(it + 1, hb)

        P.barrier(lambda: nc.gpsimd.memset(bar_t[:], 0.0))
        P.skip = not run_phase("C")
        HW_ = 4128
        hbuf = fv(big1[:], 0, [[HW_, 2], [1, HW_]])
        accC = fv(big2[:], 0, [[4096, 2], [1, 4096]])
        dwT = fv(big2[:], 8192, [[31, 2], [1, 31]])
        cb_sb = fv(big2[:], 8256, [[1, 2]])
        cg_sb = fv(big2[:], 8258, [[1, 2]])
        cbe_sb = fv(big2[:], 8260, [[1, 2]])
        Wpw = fv(b2, 16640, [[256, 2], [1, 256]])
        dwraw = fv(big2[:], 8704, [[1, 256]])
        P.dma(dwraw[0:31, :], conv_dw[l], [], ["dwraw"])
        for m in range(2):
            tr(bank(6)[:, m * 32:m * 32 + 31], dwraw[0:31, m * 128:(m + 1) * 128], identf[0:31, 0:31], ["dwraw", "identf"], pk(6))
        cp("dve", dwT, fv(bank(6), 0, [[32, 2], [1, 31]]), pk(6), ["cpar"])
        P.dma(cb_sb, conv_b[l].rearrange("(m p) -> p m", p=128), [], ["cpar"], allow_slow_non_contiguous=True)
        P.dma(cg_sb, cln_g[l].rearrange("(m p) -> p m", p=128), [], ["cpar"], allow_slow_non_contiguous=True)
        P.dma(cbe_sb, cln_b[l].rearrange("(m p) -> p m", p=128), [], ["cpar"], allow_slow_non_contiguous=True)
        wload(Wpw, w_pw[l].rearrange("(c p) n -> p c n", p=128), [], ["Wpw"])
        for m in range(2):
            memset("pool", fv(big1[:], m * HW_, [[1, 16]]), 0.0, [("hbuf", "padl")])
            memset("pool", fv(big1[:], m * HW_ + 16 + 4096, [[1, 16]]), 0.0, [("hbuf", "padr")])
        sig = [fv(big1[:], 8256 + i * 512, [[1, 512]]) for i in range(2)]
        HB = 26752
        DG = 18432
        memset("pool", fv(b1, HB, [[1, 16]]), 0.0, [("hb16", "padl")])
        memset("pool", fv(b1, HB + 16 + 4096, [[1, 16]]), 0.0, [("hb16", "padr")])
        for k in range(31):
            ts("pool", fv(b2, DG + k * 128, [[1, 128]]), ident[:], dwT[:, 1, k:k + 1], None, ALU.mult, None, ["ident", "cpar"], ["diag"])
        load_xT(0, 0)
        for j in range(8):
            if j + 1 < 8:
                load_xT(j + 1, (j + 1) % 2)
            xc = xTc[j % 2]
            for m in range(4):
                for c in range(8):
                    mm(bank(m), WA[:, c, m * 128:(m + 1) * 128], xc[:, c, :], c == 0, c == 7, ["WA", f"xTc{j % 2}"], pk(m))
            for m in range(2):
                act(sig[m], bank(2 + m), AF.Sigmoid, pk(2 + m), [f"sig{m}"])
            tt("dve", hbuf[:, 0, 16 + j * 512:16 + (j + 1) * 512], bank(0), sig[0], ALU.mult, pk(0) + ["sig0"], [("hbuf", j)])
            tt("dve", fv(b1, HB + 16 + j * 512, [[1, 512]]), bank(1), sig[1], ALU.mult, pk(1) + ["sig1"], [("hb16", j)])
        sk = P.skip
        P.skip = not run_phase("D")
        wload(WA[:, :, 0:768], win_v[:, :, 1280:2048], [], ["WA"])
        P.skip = sk
        hkeys_all = [("hbuf", j) for j in range(8)] + [("hbuf", "padl"), ("hbuf", "padr")]
        h16keys = [("hb16", j) for j in range(8)] + [("hb16", "padl"), ("hb16", "padr")]
        for pc in range(4):
            o = accC[:, 0, pc * 1024:(pc + 1) * 1024]
            for k in range(31):
                src = hbuf[:, 0, 16 + pc * 1024 + k - 15:16 + pc * 1024 + k - 15 + 1024]
                if k == 0:
                    ts("dve", o, src, dwT[:, 0, 0:1], cb_sb[:, 0:1], ALU.mult, ALU.add, hkeys_all + ["cpar"], [("accC", 0, pc)])
                else:
                    stt(o, src, dwT[:, 0, k:k + 1], o, ALU.mult, ALU.add, hkeys_all + ["cpar"], [("accC", 0, pc)])
            for jj in range(2):
                j = pc * 2 + jj
                pb = 4 + j % 2
                for k in range(31):
                    mm(bank(pb), fv(b2, DG + k * 128, [[1, 128]]), fv(b1, HB + 16 + j * 512 + k - 15, [[1, 512]]), k == 0, k == 30,
                       h16keys + ["diag"], pk(pb))
                act(accC[:, 1, j * 512:(j + 1) * 512], bank(pb), AF.Identity, pk(pb) + ["cpar"], [("accC", 1, pc)], bias=cb_sb[:, 1:2])
        sqb = [fv(big1[:], 9280 + i * 512, [[1, 512]]) for i in range(2)]
        mean_sb = fv(big1[:], 10304, [[1, 512]])
        var_sb = fv(big1[:], 10816, [[1, 512]])
        xh = [fv(big1[:], 11328 + i * 512, [[1, 512]]) for i in range(2)]
        hact = fv(b1, 2 * 12352, [[512, 2], [1, 512]])
        yC = fv(b1, 2 * 12864, [[512, 2], [1, 512]])
        for j in range(8):
            cs = slice(j * 512, (j + 1) * 512)
            akeys = [("accC", m, j // 2) for m in range(2)]
            for m in range(2):
                act(sqb[m], accC[:, m, cs], AF.Square, [("accC", m, j // 2)], [f"sq{m}"])
            for m in range(2):
                mm(bank(0), onesF[:], accC[:, m, cs], m == 0, m == 1, akeys + ["onesF"], pk(0))
            for m in range(2):
                mm(bank(1), onesF[:], sqb[m], m == 0, m == 1, [f"sq{m}", "onesF"], pk(1))
            cp("act", mean_sb, bank(0), pk(0), ["mean"])
            tt("dve", var_sb, mean_sb, mean_sb, ALU.mult, ["mean"], ["var"])
            tt("dve", var_sb, bank(1), var_sb, ALU.subtract, pk(1) + ["var"], ["var"])
            ts("dve", var_sb, var_sb, LN_EPS, None, ALU.add, None, ["var"], ["var"])
            act(var_sb, var_sb, AF.Sqrt, ["var"], ["var"])
            P.op("dve", lambda: nc.vector.reciprocal(out=var_sb, in_=var_sb), ["var"], ["var"])
            for m in range(2):
                tt("dve", xh[m], accC[:, m, cs], mean_sb, ALU.subtract, [("accC", m, j // 2), "mean"], [f"xh{m}"])
                tt("dve", xh[m], xh[m], var_sb, ALU.mult, [f"xh{m}", "var"], [f"xh{m}"])
                act(hact[:, m, :], xh[m], AF.Silu, [f"xh{m}", "cpar"], ["hact"], scale=cg_sb[:, m:m + 1], bias=cbe_sb[:, m:m + 1])
            for mo in range(2):
                for c in range(2):
                    mm(bank(2 + mo), Wpw[:, c, mo * 128:(mo + 1) * 128], hact[:, c, :], c == 0, c == 1, ["hact", "Wpw"], pk(2 + mo))
                cp("dve", yC[:, mo, :], bank(2 + mo), pk(2 + mo), ["yC"])
            P.dma(YTv[:, 4:6, cs], yC, ["yC"], [("YT", 4), ("YT", 5)], q="pool")

        P.barrier(lambda: nc.gpsimd.memset(bar_t[:], 0.0))
        P.skip = not run_phase("D")
        for hp in range(2):
            if hp == 1:
                P.barrier(lambda: nc.gpsimd.memset(bar_t[:], 0.0))
            QTd = fv(b1, 0, [[1, 4096]])
            KTo = {1: 4096, 4: 8192, 16: 12288}
            VTo = {1: 16384, 4: 20480, 16: 24576}
            accD = fv(big2[:], 0, [[4096, 2], [1, 4096]])
            B8 = fv(b2, 24576, [[128, 24], [1, 128]])
            P.dma(B8, B8d.rearrange("k p q -> p k q"), ["B8d"], ["B8"])
            recD = fv(big2[:], 8192, [[1, 4096]])
            yD = fv(b1, 28672, [[1, 4096]])
            load_xT(0, 0)
            for j in range(8):
                if j + 1 < 8:
                    load_xT(j + 1, (j + 1) % 2)
                xc = xTc[j % 2]
                for c in range(8):
                    mm(bank(0), WA[:, c, hp * 128:(hp + 1) * 128], xc[:, c, :], c == 0, c == 7, ["WA", f"xTc{j % 2}"], pk(0))
                for c in range(8):
                    mm(bank(1), WA[:, c, 256 + hp * 128:256 + (hp + 1) * 128], xc[:, c, :], c == 0, c == 7, ["WA", f"xTc{j % 2}"], pk(1))
                cp("act", QTd[:, j * 512:(j + 1) * 512], bank(0), pk(0), [("QTd", j)])
                cp("dve", fv(b1, 4096 + j * 512, [[1, 512]]), bank(1), pk(1), [("KTd", j)])
                cp("act", fv(b1, 8192 + j * 128, [[1024, 4], [1, 128]]), fv(bank(1), 0, [[1, 4], [4, 128]]), pk(1), [("KTd", j)])
                cp("dve", fv(b1, 12288 + j * 32, [[256, 16], [1, 32]]), fv(bank(1), 0, [[1, 16], [16, 32]]), pk(1), [("KTd", j)])
                for tq in range(4):
                    t = j * 4 + tq
                    for c in range(8):
                        mm(bank(2 + tq % 2)[:, 0:128], xc[:, c, tq * 128:(tq + 1) * 128], WA[:, c, 512 + hp * 128:512 + (hp + 1) * 128], c == 0, c == 7,
                           ["WA", f"xTc{j % 2}"], pk(2 + tq % 2))
                    cp("act" if tq % 2 == 0 else "dve", fv(b1, 16384 + t * 128, [[1, 128]]), bank(2 + tq % 2)[:, 0:128], pk(2 + tq % 2), [("Vnat", t)])
                    P.dma(VD[t * 128:(t + 1) * 128, :], fv(b1, 16384 + t * 128, [[1, 128]]), [("Vnat", t)], [("VD", t)], q="pool")
            if hp == 1:
                sk = P.skip
                P.skip = not run_phase("O")
                for hf in range(2):
                    wload(WA[:, :, hf * 512:(hf + 1) * 512], w_out[l].rearrange("(c p) n -> p c n", p=128)[:, :, hf * 512:(hf + 1) * 512], [], ["WA"])
                load_ln_params(ln1_g[l], ln1_b[l])
                P.skip = sk
            vdk = [("VD", t) for t in range(NT)]
            VDr = VD
            for r in range(4):
                src = bass.AP(VDr.tensor, VDr.offset + r * 128, [[4 * 128, 128], [4 * 128 * 128, 8], [1, 128]])
                P.dma(fv(b1, 20480 + r * 8 * 128, [[128, 8], [1, 128]]), src, vdk, [("VD4", r)])
            for r in range(16):
                src = bass.AP(VDr.tensor, VDr.offset + r * 128, [[16 * 128, 128], [16 * 128 * 128, 2], [1, 128]])
                P.dma(fv(b1, 24576 + r * 2 * 128, [[128, 2], [1, 128]]), src, vdk, [("VD16", r)])
            qk_keys = [("QTd", j) for j in range(8)] + [("KTd", j) for j in range(8)]
            PTd = [fv(smb, i * 512, [[1, 512]]) for i in range(3)]
            diters = []
            for di, d in enumerate(DILS):
                for r in range(d):
                    for i in range(-1, (S // d) // 128):
                        diters.append((di, d, r, i))

            def d_info(it):
                di, d, r, i = diters[it]
                Ld = S // d
                ntile = Ld // 128
                q0 = 64 if i == -1 else 0
                q1 = 64 if i == ntile - 1 else 128
                hasA = i >= 0
                hasB = i + 1 <= ntile - 1
                tok0 = r + d * (128 * i + 64 + q0)
                if d == 1:
                    vkeys = [("Vnat", t) for t in range(NT)]
                elif d == 4:
                    vkeys = [("VD4", r)]
                else:
                    vkeys = [("VD16", r)]
                blocks = [(ab, jt) for ab, has, jt in ((0, hasA, i), (1, hasB, i + 1)) if has]
                return di, d, r, i, Ld, ntile, q0, q1, tok0, vkeys, blocks

            def d_S(it):
                di, d, r, i, Ld, ntile, q0, q1, tok0, vkeys, blocks = d_info(it)
                nq = q1 - q0
                for hh in range(2):
                    sbk = (it % 3) * 2 + hh
                    pr = slice(hh * 64, (hh + 1) * 64)
                    rhs_q = fv(b1[pr, :], tok0, [[d, nq]])
                    for ab, jt in blocks:
                        reg = bank(sbk)[:, ab * 128 + q0:ab * 128 + q1]
                        kcol = KTo[d] + r * Ld + jt * 128
                        mm(reg, fv(b1[pr, :], kcol, [[1, 128]]), rhs_q, True, True, qk_keys, pk(sbk))

            def d_E(it):
                di, d, r, i, Ld, ntile, q0, q1, tok0, vkeys, blocks = d_info(it)
                ptb = PTd[it % 3]
                ebase = 24576 + (hp * 2 * 6 + di * 2) * 128
                full = len(blocks) == 2 and q0 == 0 and q1 == 128
                for hh in range(2):
                    sbk = (it % 3) * 2 + hh
                    if full:
                        act(ptb[:, hh * 256:(hh + 1) * 256], bank(sbk)[:, 0:256], AF.Exp, pk(sbk), [f"PTd{it % 3}"], scale=0.125)
                    else:
                        for ab, jt in blocks:
                            col = (hh * 2 + ab) * 128
                            act(ptb[:, col + q0:col + q1], bank(sbk)[:, ab * 128 + q0:ab * 128 + q1], AF.Exp, pk(sbk), [f"PTd{it % 3}"], scale=0.125)
                if full:
                    ptv = fv(smb, (it % 3) * 512, [[256, 2], [128, 2], [1, 128]])
                    tt("dve", ptv, ptv, fv(b2, ebase, [[768, 2], [128, 2], [1, 128]]), ALU.mult, [f"PTd{it % 3}", "B8"], [f"PTd{it % 3}"])
                else:
                    for hh in range(2):
                        for ab, jt in blocks:
                            col = (hh * 2 + ab) * 128
                            tt("dve", ptb[:, col + q0:col + q1], ptb[:, col + q0:col + q1],
                               fv(b2, ebase + hh * 768 + ab * 128 + q0, [[1, q1 - q0]]), ALU.mult, [f"PTd{it % 3}", "B8"], [f"PTd{it % 3}"])

            def d_P(it):
                di, d, r, i, Ld, ntile, q0, q1, tok0, vkeys, blocks = d_info(it)
                nq = q1 - q0
                obk = 6 + it % 2
                ptb = PTd[it % 3]
                for hh in range(2):
                    oreg = bank(obk)[:, hh * 128 + q0:hh * 128 + q1]
                    for bi, (ab, jt) in enumerate(blocks):
                        col = (hh * 2 + ab) * 128
                        vcol = VTo[d] + (r * ntile + jt) * 128 + hh * 64
                        mm(oreg[0:64, :], fv(b1, vcol, [[1, 64]]), ptb[:, col + q0:col + q1], bi == 0, bi == len(blocks) - 1,
                           [f"PTd{it % 3}"] + vkeys, pk(obk))
                    for bi, (ab, jt) in enumerate(blocks):
                        col = (hh * 2 + ab) * 128
                        mm(oreg[64:128, :], ones64[:], ptb[:, col + q0:col + q1], bi == 0, bi == len(blocks) - 1,
                           [f"PTd{it % 3}", "ones64"], pk(obk))
                for hh in range(2):
                    oreg = bank(obk)[:, hh * 128 + q0:hh * 128 + q1]
                    dst = fv(big2[:], hh * 4096 + tok0, [[d, nq]])
                    if di == 0:
                        cp("dve", dst, oreg, pk(obk), [("accD", hh)])
                    else:
                        tt("dve", dst, oreg, dst, ALU.add, pk(obk) + [("accD", hh)], [("accD", hh)])

            d_S(0)
            d_S(1)
            for it in range(len(diters)):
                if it + 2 < len(diters):
                    d_S(it + 2)
                d_E(it)
                d_P(it)
            for hh in range(2):
                cp("dve", recD[0:64, :], accD[64:128, hh, :], [("accD", hh)], ["recD"])
                act(recD[0:64, :], recD[0:64, :], AF.Ln, ["recD"], ["recD"])
                act(recD[0:64, :], recD[0:64, :], AF.Exp, ["recD"], ["recD"], scale=-1.0)
                tt("dve", yD[hh * 64:(hh + 1) * 64, :], accD[0:64, hh, :], recD[0:64, :], ALU.mult, [("accD", hh), "recD"], ["yD"])
            P.dma(YTv[:, 6 + hp, :], yD, ["yD"], [("YT", 6 + hp)], q="pool")

        P.barrier(lambda: nc.gpsimd.memset(bar_t[:], 0.0))

        P.skip = not run_phase("O")
        W1 = fv(b1, 0, [[4096, 8], [1, 4096]])
        W2 = fv(b2, 0, [[1024, 32], [1, 1024]])
        w1v = w_ff1[l].rearrange("(c p) n -> p c n", p=128)
        w2v = w_ff2[l].rearrange("(c p) n -> p c n", p=128)
        wq = []
        for c in range(8):
            for hf in range(2):
                wq.append((W1[:, c, hf * 2048:(hf + 1) * 2048], w1v[:, c, hf * 2048:(hf + 1) * 2048], "W1"))
        for c in range(32):
            wq.append((W2[:, c, :], w2v[:, c, :], "W2"))
        ytk = [("YT", k) for k in range(8)]

        def load_yT(j, buf):
            P.dma(xTc[buf][:], YTv[:, :, j * 512:(j + 1) * 512], ytk, [f"xTc{buf}"])

        def o_mm(tp):
            j = tp // 2
            if tp % 2 == 0 and j + 1 < 8:
                load_yT(j + 1, (j + 1) % 2)
            yc = xTc[j % 2]
            for k in range(2):
                t = 2 * tp + k
                tl = (t % 4) * 128
                P.dma(xt_in[k][:], X[t * 128:(t + 1) * 128, :], [("X", t)], [f"xin{k}"])
                pb = 2 * k
                for hf in range(2):
                    for c in range(8):
                        mm(bank(pb + hf), yc[:, c, tl:tl + 128], WA[:, c, hf * 512:(hf + 1) * 512], c == 0, c == 7, ["WA", f"xTc{j % 2}"], pk(pb + hf))

        def o_ln(tp):
            for k in range(2):
                pb = 2 * k
                stt(rt[k][:], xt_in[k][:], ALPHA, PS[:, pb * 512:pb * 512 + 1024], ALU.mult, ALU.add, [f"xin{k}"] + pk(pb, 2), [f"rt{k}"])
            ln_pair([rt[0][:], rt[1][:]], ["rt0", "rt1"], [rt[0][:], rt[1][:]], ["rt0", "rt1"])
            for k in range(2):
                t = 2 * tp + k
                P.dma(X[t * 128:(t + 1) * 128, :], rt[k][:], [f"rt{k}"], [("X", t)], q="pool")

        def o_tail(tp):
            for k in range(2):
                transpose_tile(rt[k][:], f"rt{k}", xTs[0][:, :, k * 128:(k + 1) * 128], "xTs0", k)
            P.dma(XTv[:, :, tp * 256:(tp + 1) * 256], xTs[0][:], ["xTs0"], [("XT1", tp)], q="pool")
            for _ in range(3):
                if wq:
                    wd, ws, wk = wq.pop(0)
                    wload(wd, ws, [], [wk])

        load_yT(0, 0)
        o_mm(0)
        for tp in range(NT // 2):
            o_ln(tp)
            if tp + 1 < NT // 2:
                o_mm(tp + 1)
            o_tail(tp)

        P.barrier(lambda: nc.gpsimd.memset(bar_t[:], 0.0))
        P.skip = not run_phase("F")
        load_ln_params(ln2_g[l], ln2_b[l])
        last = (l == n_layers - 1)
        hT = fv(WA[:], 0, [[256, 32], [1, 256]])

        def load_x1T(jc, buf):
            P.dma(xTc[buf][:, :, 0:256], XTv[:, :, jc * 256:(jc + 1) * 256], [("XT1", jc)], [f"xTc{buf}"])

        def f_ffn1(jc, h0, h1):
            if h0 == 0 and jc + 1 < 16:
                load_x1T(jc + 1, (jc + 1) % 2)
            xc = xTc[jc % 2]
            for hc in range(h0, h1):
                pb = hc % 3
                for c in range(8):
                    mm(bank(pb)[:, 0:256], W1[:, c, hc * 128:(hc + 1) * 128], xc[:, c, 0:256], c == 0, c == 7, ["W1", f"xTc{jc % 2}"], pk(pb))
                rl = fv(small[:], (hc % 2) * 256, [[1, 256]])
                act(rl, bank(pb)[:, 0:256], AF.Relu, pk(pb), [f"relu{hc % 2}"])
                tt("dve", hT[:, hc, :], rl, rl, ALU.mult, [f"relu{hc % 2}"], [("hT", hc)])

        def f_ffn2(jc):
            for k in range(2):
                t = jc * 2 + k
                P.dma(xt_in[k][:], X[t * 128:(t + 1) * 128, :], [("X", t)], [f"xin{k}"])
                pb = 3 + 2 * k
                for hf in range(2):
                    for hc in range(32):
                        mm(bank(pb + hf), hT[:, hc, k * 128:(k + 1) * 128], W2[:, hc, hf * 512:(hf + 1) * 512], hc == 0, hc == 31, [("hT", hc), "W2"], pk(pb + hf))

        def f_ln(jc):
            for k in range(2):
                pb = 3 + 2 * k
                stt(rt[k][:], xt_in[k][:], ALPHA, PS[:, pb * 512:pb * 512 + 1024], ALU.mult, ALU.add, [f"xin{k}"] + pk(pb, 2), [f"rt{k}"])
            ln_pair([rt[0][:], rt[1][:]], ["rt0", "rt1"], [rt[0][:], rt[1][:]], ["rt0", "rt1"], rstd_on_pool=True)
            for k in range(2):
                t = jc * 2 + k
                if last:
                    P.dma(out[t * 128:(t + 1) * 128, :], rt[k][:], [f"rt{k}"], ["out"], q="pool")
                else:
                    P.dma(X[t * 128:(t + 1) * 128, :], rt[k][:], [f"rt{k}"], [("X", t)], q="pool")

        def f_tail(jc):
            if last:
                return
            for k in range(2):
                transpose_tile(rt[k][:], f"rt{k}", xTs[0][:, :, k * 128:(k + 1) * 128], "xTs0", k)
            P.dma(XTv[:, :, jc * 256:(jc + 1) * 256], xTs[0][:], ["xTs0"], [("XT", jc // 2)], q="pool")

        load_x1T(0, 0)
        for jc in range(16):
            f_ffn1(jc, 0, 8)
            if jc > 0:
                f_ln(jc - 1)
            f_ffn1(jc, 8, 32)
            if jc > 0:
                f_tail(jc - 1)
            f_ffn2(jc)
        f_ln(15)
        f_tail(15)
        P.barrier(lambda: nc.gpsimd.memset(bar_t[:], 0.0))

    P.skip = False
    fk = ["out"]
    if stop_after is not None and stop_after != "F":
        P.barrier(lambda: nc.gpsimd.memset(bar_t[:], 0.0))
        P.dma(out, X, [], ["out"])
    if debug:
        P.barrier(lambda: nc.gpsimd.memset(bar_t[:], 0.0))
        P.dma(dbg_yt, YT, [], ["dbg_yt"])
        P.dma(dbg_x1, X, [], ["dbg_x1"])
        fk += ["dbg_yt", "dbg_x1"]
    stats = P.emit(final_wait_keys=fk)
    return nc, stats


def _t5_bucket_np(rel):
    nb = 16
    max_exact = 8
    ret = np.where(rel > 0, nb, 0)
    n = np.abs(rel)
    nf = np.maximum(n, 1).astype(np.float32)
    large = max_exact + (np.log(nf / np.float32(max_exact)) / np.float32(math.log(1024 / max_exact))
                         * np.float32(nb - max_exact)).astype(np.int32)
    large = np.minimum(large, nb - 1)
    return ret + np.where(n < max_exact, n, large)


def _constants():
    c = {}
    c["c_ident"] = np.eye(128, dtype=np.float32)
    nf = 16
    inv = (10000.0 ** (-np.arange(nf, dtype=np.float32) / nf)).astype(np.float32)
    t = np.arange(S)
    row = (t // 64).astype(np.float32)
    col = (t % 64).astype(np.float32)
    ang = np.concatenate([row[:, None] * inv, col[:, None] * inv], -1).astype(np.float32)
    c["c_rope"] = np.concatenate([np.cos(ang), np.sin(ang)], -1).astype(np.float32)
    k = np.arange(64)
    C64 = np.cos(2 * np.pi * np.outer(k, k) / 64)
    S64 = np.sin(2 * np.pi * np.outer(k, k) / 64)
    BC = np.kron(np.eye(4), C64) / 8
    BS = np.kron(np.eye(4), S64) / 8
    c["c_bcs"] = np.concatenate([BC, BS], 1).astype(np.float32)
    Rre = np.concatenate([C64, -S64], 0)
    Rim = np.concatenate([-S64, -C64], 0)
    c["c_r"] = np.concatenate([Rre, Rim], 1).astype(np.float32)
    s1 = np.arange(64)[:, None, None]
    k2 = np.arange(64)[None, :, None]
    k1 = np.arange(64)[None, None, :]
    th = 2 * np.pi * ((s1 * (64 * k1 + k2)) % 4096) / 4096
    T3 = np.concatenate([np.cos(th) / 64, np.sin(th) / 64], 0)
    c["c_t3"] = T3.reshape(128, 4096).astype(np.float32)
    p = np.arange(128)[:, None]
    q = np.arange(128)[None, :]
    mA = np.where((p - q >= 0) & (p - q <= 128), 0.0, MASKV)
    mB = np.where((p - q >= -128) & (p - q <= 0), 0.0, MASKV)
    c["c_dmask"] = np.stack([mA, mB]).astype(np.float32)
    return c


def _dbias(rel_bias):
    p = np.arange(128)[:, None]
    q = np.arange(128)[None, :]
    o = np.zeros((4, 3, 2, 128, 128), np.float32)
    for di, d in enumerate(DILS):
        for ab, off in ((0, -64), (1, 64)):
            rel = np.clip(p - q + off, -64, 64)
            idx = _t5_bucket_np(rel * d)
            for h in range(4):
                o[h, di, ab] = rel_bias[idx, h]
    return o


_CACHE = {}


def kernel(**inputs):
    inputs = {k: np.asarray(v) for k, v in inputs.items()}
    if "nc" not in _CACHE:
        _CACHE["nc"] = build_program()
    nc, _ = _CACHE["nc"]
    consts = _constants()
    shared = {k: np.ascontiguousarray(v, dtype=np.float32) for k, v in inputs.items() if k not in ("x", "rel_bias")}
    shared.update(consts)
    shared["dbias"] = _dbias(inputs["rel_bias"].astype(np.float32))
    x = inputs["x"].astype(np.float32)
    in_maps = []
    for c in range(8):
        m = dict(shared)
        m["x"] = np.ascontiguousarray(x[c % 4])
        in_maps.append(m)
    res = run_bass_kernel_spmd(nc, in_maps, core_ids=list(range(8)))
    return np.stack([res.results[c]["out"] for c in range(4)], 0).astype(np.float32)
```

```python
import math
import os
import numpy as np
BSTEP = int(os.environ.get("BSTEP", "99"))
import ml_dtypes
import concourse.bass as bass
import concourse.mybir as mybir
from concourse.bass_utils import run_bass_kernel_spmd

F32 = mybir.dt.float32
BF16 = mybir.dt.bfloat16
AF = mybir.ActivationFunctionType
ALU = mybir.AluOpType
AX = mybir.AxisListType

S = 4096
D = 1024
NT = 32
DEPTH = 4
DFF = 4096
ALPHA = (2 * DEPTH) ** 0.25
LN_EPS = 1e-5
RMS_EPS = 1e-6
DILS = (1, 4, 16)
MASKV = -240000.0

ENGS = ("pe", "act", "dve", "pool", "sp")


class Prog:
    def __init__(self, nc, n_dma_slots=48):
        self.nc = nc
        self.ops = []
        self.n_dma_slots = n_dma_slots
        self.eng_obj = {"pe": nc.tensor, "act": nc.scalar, "dve": nc.vector,
                        "pool": nc.gpsimd, "sp": nc.sync}

    skip = False

    def op(self, eng, fn, reads=(), writes=(), dma=False):
        if self.skip:
            return
        ps_r = tuple(k for k in reads if isinstance(k, tuple) and k[0] == "ps" and k not in writes)
        self.ops.append((eng, fn, tuple(reads), tuple(writes) + ps_r, dma))

    def dma(self, out, in_, reads, writes, q="sp", **kw):
        e = self.eng_obj[q]
        self.op(q, lambda: e.dma_start(out=out, in_=in_, **kw), reads, writes, dma=True)

    def barrier(self, fn):
        if self.skip:
            return
        self.ops.append(("pool", fn, "BARRIER", (), False))

    def emit(self, final_wait_keys=()):
        nc = self.nc
        ops = self.ops
        n = len(ops)
        last_w = {}
        readers = {}
        deps = [None] * n
        bar = None
        last_eng = {}
        dma_since = []
        for i, (eng, fn, reads, writes, is_dma) in enumerate(ops):
            if reads == "BARRIER":
                d = set(last_eng.values()) | set(dma_since)
                if bar is not None:
                    d.add(bar)
                deps[i] = d
                last_w = {}
                readers = {}
                dma_since = []
                last_eng = {}
                bar = i
                ops[i] = (eng, fn, (), (), False)
                continue
            if is_dma:
                dma_since.append(i)
            else:
                last_eng[eng] = i
            d = set()
            if bar is not None:
                d.add(bar)
            for k in reads:
                if k in last_w:
                    d.add(last_w[k])
            for k in writes:
                if k in last_w:
                    d.add(last_w[k])
                for r in readers.get(k, ()):
                    d.add(r)
            d.discard(i)
            deps[i] = d
            for k in reads:
                readers.setdefault(k, []).append(i)
            for k in writes:
                last_w[k] = i
                readers[k] = []
        final_deps = set()
        for k in final_wait_keys:
            if k in last_w:
                final_deps.add(last_w[k])
        needed = set()
        for i in range(n):
            eng_i, _, _, _, dma_i = ops[i]
            keep = set()
            for j in deps[i]:
                eng_j, _, _, _, dma_j = ops[j]
                if (not dma_j) and (not dma_i) and eng_i == "pe" and eng_j == "pe":
                    continue
                keep.add(j)
            deps[i] = keep
            needed |= keep
        needed |= final_deps
        sems = {e: nc.alloc_semaphore(name=f"s_{e}") for e in ENGS}
        slots = [nc.alloc_semaphore(name=f"s_dma{k}") for k in range(self.n_dma_slots)]
        cnt = {e: 0 for e in ENGS}
        slot_use = [0] * self.n_dma_slots
        half = self.n_dma_slots // 2
        slot_rr = {"sw": 0, "hw": 0}
        done_tok = [None] * n
        prev_slot_tok = [None] * n
        for i, (eng, fn, reads, writes, is_dma) in enumerate(ops):
            if is_dma:
                kind = "sw" if eng == "pool" else "hw"
                s = slot_rr[kind] + (half if kind == "sw" else 0)
                slot_rr[kind] = (slot_rr[kind] + 1) % half
                if slot_use[s] > 0:
                    prev_slot_tok[i] = (("slot", s), 16 * slot_use[s])
                slot_use[s] += 1
                done_tok[i] = (("slot", s), 16 * slot_use[s])
            elif i in needed:
                cnt[eng] += 1
                done_tok[i] = (("eng", eng), cnt[eng])
        seen = {e: {} for e in ENGS}

        def semh(key):
            return sems[key[1]] if key[0] == "eng" else slots[key[1]]

        n_wait = 0
        for i, (eng, fn, reads, writes, is_dma) in enumerate(ops):
            e = self.eng_obj[eng]
            want = {}
            for j in deps[i]:
                key, val = done_tok[j]
                if want.get(key, 0) < val:
                    want[key] = val
            if prev_slot_tok[i] is not None:
                key, val = prev_slot_tok[i]
                if want.get(key, 0) < val:
                    want[key] = val
            for key, val in want.items():
                if seen[eng].get(key, 0) >= val:
                    continue
                e.wait_ge(semh(key), val)
                seen[eng][key] = val
                n_wait += 1
            ins = fn()
            if is_dma:
                ins.then_inc(semh(done_tok[i][0]), 16)
            elif done_tok[i] is not None:
                ins.then_inc(semh(done_tok[i][0]), 1)
        e = self.eng_obj["sp"]
        want = {}
        for j in final_deps:
            key, val = done_tok[j]
            if want.get(key, 0) < val:
                want[key] = val
        for key, val in want.items():
            e.wait_ge(semh(key), val)
        self.stats = dict(n_ops=n, n_wait=n_wait, cnt=dict(cnt))
        return self.stats


def fv(ap, off, dims):
    return bass.AP(ap.tensor, ap.offset + off, [list(ap.ap[0])] + [list(d) for d in dims])


def build_program(n_layers=DEPTH, debug=False, stop_after=None):
    nc = bass.Bass("TRN2", target_bir_lowering=False)
    P = Prog(nc)

    def din(name, shape, dt=F32):
        return nc.dram_tensor(name, list(shape), dt, kind="ExternalInput").ap()

    def dscr(name, shape, dt):
        return nc.dram_tensor(name, list(shape), dt, kind="Internal").ap()

    x_in = din("x", [S, D])
    emb_g = din("emb_ln_g", [D]); emb_b = din("emb_ln_b", [D])
    w_in = din("w_in", [DEPTH, D, 2048])
    w_fnet = din("w_fnet", [DEPTH, 256, 256])
    qg = din("q_norm_g", [DEPTH, 64]); kg = din("k_norm_g", [DEPTH, 64])
    conv_dw = din("conv_dw", [DEPTH, 31, 256]); conv_b = din("conv_b", [DEPTH, 256])
    cln_g = din("conv_ln_g", [DEPTH, 256]); cln_b = din("conv_ln_b", [DEPTH, 256])
    w_pw = din("w_conv_out", [DEPTH, 256, 256])
    w_out = din("w_out", [DEPTH, D, D])
    ln1_g = din("ln1_g", [DEPTH, D]); ln1_b = din("ln1_b", [DEPTH, D])
    w_ff1 = din("w_ff1", [DEPTH, D, DFF]); w_ff2 = din("w_ff2", [DEPTH, DFF, D])
    ln2_g = din("ln2_g", [DEPTH, D]); ln2_b = din("ln2_b", [DEPTH, D])
    dbias = din("dbias", [4, 3, 2, 128, 128])
    c_ident = din("c_ident", [128, 128])
    c_rope = din("c_rope", [S, 64])
    c_bcs = din("c_bcs", [256, 512])
    c_r = din("c_r", [128, 128])
    c_t3 = din("c_t3", [128, 4096])
    c_dmask = din("c_dmask", [2, 128, 128])
    out = nc.dram_tensor("out", [S, D], F32, kind="ExternalOutput").ap()

    X = dscr("X", [S, D], F32)
    XT = dscr("XT", [8, 128, S], BF16)
    YT = dscr("YT", [8, 128, S], BF16)
    VD = dscr("VD", [S, 128], BF16)
    B8d = dscr("B8d", [24, 128, 128], BF16)
    if debug:
        dbg_yt = nc.dram_tensor("dbg_yt", [8, 128, S], BF16, kind="ExternalOutput").ap()
        dbg_x1 = nc.dram_tensor("dbg_x1", [S, D], F32, kind="ExternalOutput").ap()

    def sb(name, shape, dt):
        return nc.alloc_sbuf_tensor(name, list(shape), dt)

    PS = nc.alloc_psum_tensor("PS", [128, 4096], F32)

    def bank(k, w=512):
        return PS[:, k * 512:k * 512 + w]

    def pk(k, nb=1):
        return [("ps", k + i) for i in range(nb)]

    ident = sb("ident", [128, 128], BF16)
    identf = sb("identf", [128, 128], F32)
    onesF = sb("onesF", [128, 128], F32)
    ones64 = sb("ones64", [128, 64], BF16)
    mh = sb("mh", [128, 8], F32)
    xTc = [sb(f"xTc{i}", [128, 8, 512], BF16) for i in range(2)]
    WA = sb("WA", [128, 8, 1024], BF16)
    big1 = sb("big1", [128, 16384], F32)
    big2 = sb("big2", [128, 16384], F32)
    lng = sb("lng", [128, 1024], F32); lnb = sb("lnb", [128, 1024], F32)
    rt = [sb(f"rt{i}", [128, 1024], F32) for i in range(2)]
    xt_in = [sb(f"xin{i}", [128, 1024], F32) for i in range(2)]
    xnb = [sb(f"xnb{i}", [128, 1024], BF16) for i in range(2)]
    xTs = [sb("xTs0", [128, 8, 256], BF16)] * 2
    st6 = sb("st6", [128, 2, 2, 6], F32)
    mv2 = sb("mv2", [128, 2, 2], F32)
    rstd2 = sb("rstd2", [128, 2], F32)
    small = sb("small", [128, 2048], F32)
    bar_t = sb("bar_t", [128, 8], F32)

    def mm(o, l, r, st, sp, R, W):
        P.op("pe", lambda: nc.tensor.matmul(o, lhsT=l, rhs=r, start=st, stop=sp), R, W)

    def tr(o, i, idn, R, W):
        P.op("pe", lambda: nc.tensor.transpose(o, i, idn), R, W)

    def act(o, i, f, R, W, scale=None, bias=None):
        kw = {}
        if scale is not None:
            kw["scale"] = scale
        if bias is not None:
            kw["bias"] = bias
        P.op("act", lambda: nc.scalar.activation(out=o, in_=i, func=f, **kw), R, W)

    def cp(eng, o, i, R, W):
        e = P.eng_obj[eng]
        if eng == "act":
            P.op("act", lambda: nc.scalar.copy(out=o, in_=i), R, W)
        else:
            P.op(eng, lambda: e.tensor_copy(out=o, in_=i), R, W)

    def tt(eng, o, a, b, op, R, W):
        e = P.eng_obj[eng]
        P.op(eng, lambda: e.tensor_tensor(out=o, in0=a, in1=b, op=op), R, W)

    def ts(eng, o, a, s1, s2, op0, op1, R, W):
        e = P.eng_obj[eng]
        if op1 is None:
            P.op(eng, lambda: e.tensor_scalar(out=o, in0=a, scalar1=s1, scalar2=None, op0=op0), R, W)
        else:
            P.op(eng, lambda: e.tensor_scalar(out=o, in0=a, scalar1=s1, scalar2=s2, op0=op0, op1=op1), R, W)

    def stt(o, a, s, b, op0, op1, R, W):
        P.op("dve", lambda: nc.vector.scalar_tensor_tensor(out=o, in0=a, scalar=s, in1=b, op0=op0, op1=op1), R, W)

    def memset(eng, o, v, W):
        e = P.eng_obj[eng]
        P.op(eng, lambda: e.memset(o, v), [], W)

    def wload(dst, src, R, W):
        P.dma(dst, src, R, W, q="pool")

    wload(ident[:], c_ident, [], ["ident"])
    P.dma(identf[:], c_ident, [], ["identf"])
    memset("pool", onesF[:], 1.0 / 256.0, ["onesF"])
    memset("pool", ones64[:], 1.0, ["ones64"])
    memset("pool", mh[:], -0.5, ["mh"])
    for h in range(4):
        for di in range(3):
            for ab in range(2):
                k = (h * 3 + di) * 2 + ab
                P.dma(small[:, 0:128], dbias[h, di, ab], [], ["small"])
                P.dma(small[:, 128:256], c_dmask[ab], [], ["small"])
                b8s = fv(small[:].bitcast(BF16), 1024, [[1, 128]])
                stt(small[:, 256:384], small[:, 0:128], 8.0, small[:, 128:256], ALU.mult, ALU.add, ["small"], ["b8f"])
                act(b8s, small[:, 256:384], AF.Exp, ["b8f"], ["b8s"], scale=0.125)
                P.dma(B8d[k], b8s, ["b8s"], ["B8d"])

    def ln_pair(r_aps, rkeys, xn_aps, xnkeys, rstd_on_pool=False, part="ab"):
        for k in range(2 if "a" in part else 0):
            r_ap = r_aps[k]
            P.op("dve", lambda r_ap=r_ap, k=k: nc.vector.bn_stats(out=st6[:, k, 0, :], in_=r_ap[:, 0:512]), [rkeys[k]], [f"st6_{k}"])
            P.op("dve", lambda r_ap=r_ap, k=k: nc.vector.bn_stats(out=st6[:, k, 1, :], in_=r_ap[:, 512:1024]), [rkeys[k]], [f"st6_{k}"])
            P.op("dve", lambda k=k: nc.vector.bn_aggr(out=mv2[:, k, :], in_=st6[:, k].rearrange("p a b -> p (a b)")), [f"st6_{k}"], ["mv2"])
        if "b" not in part:
            return
        if rstd_on_pool:
            ts("pool", rstd2[:], fv(mv2[:], 1, [[2, 2]]), LN_EPS, None, ALU.add, None, ["mv2"], ["rstd2"])
            tt("pool", rstd2[:], rstd2[:], mh[:, 0:2], ALU.pow, ["rstd2", "mh"], ["rstd2"])
        else:
            ts("dve", rstd2[:], fv(mv2[:], 1, [[2, 2]]), LN_EPS, None, ALU.add, None, ["mv2"], ["rstd2"])
            act(rstd2[:], rstd2[:], AF.Sqrt, ["rstd2"], ["rstd2"])
            P.op("dve", lambda: nc.vector.reciprocal(out=rstd2[:], in_=rstd2[:]), ["rstd2"], ["rstd2"])
        for k in range(2):
            ts("dve", xn_aps[k], r_aps[k], mv2[:, k, 0:1], rstd2[:, k:k + 1], ALU.subtract, ALU.mult, [rkeys[k], "mv2", "rstd2"], [xnkeys[k]])
            tt("dve", xn_aps[k], xn_aps[k], lng[:], ALU.mult, [xnkeys[k], "lng"], [xnkeys[k]])
            tt("dve", xn_aps[k], xn_aps[k], lnb[:], ALU.add, [xnkeys[k], "lng"], [xnkeys[k]])

    def transpose_tile(xn_ap, xnkey, dst_ap, dstkey, i):
        cp("act", xnb[i][:], xn_ap, [xnkey], [f"xnb{i}"])
        pst = bank(7).bitcast(BF16)
        for c in range(8):
            tr(pst[:, c * 128:(c + 1) * 128], xnb[i][:, c * 128:(c + 1) * 128], ident[:], [f"xnb{i}", "ident"], pk(7))
        cp("dve", dst_ap, pst.rearrange("p (c t) -> p c t", c=8), pk(7), [dstkey])

    def load_ln_params(g_ap, b_ap):
        P.dma(lng[:], g_ap.partition_broadcast(128), [], ["lng"])
        P.dma(lnb[:], b_ap.partition_broadcast(128), [], ["lng"])

    XTv = XT.rearrange("c p s -> p c s")
    YTv = YT.rearrange("c p s -> p c s")

    def ln_store(xn_tiles, t0, final=False, write_xt=True):
        pass

    load_ln_params(emb_g, emb_b)
    for tp in range(NT // 2):
        for k in range(2):
            t = 2 * tp + k
            P.dma(xt_in[k][:], x_in[t * 128:(t + 1) * 128, :], [], [f"xin{k}"])
        ln_pair([xt_in[0][:], xt_in[1][:]], ["xin0", "xin1"], [rt[0][:], rt[1][:]], ["rt0", "rt1"])
        for k in range(2):
            t = 2 * tp + k
            P.dma(X[t * 128:(t + 1) * 128, :], rt[k][:], [f"rt{k}"], [("X", t)], q="pool")
            transpose_tile(rt[k][:], f"rt{k}", xTs[0][:, :, k * 128:(k + 1) * 128], "xTs0", k)
        P.dma(XTv[:, :, tp * 256:(tp + 1) * 256], xTs[0][:], ["xTs0"], [("XT", tp // 2)], q="pool")

    P.barrier(lambda: nc.gpsimd.memset(bar_t[:], 0.0))

    def load_xT(j, buf):
        P.dma(xTc[buf][:], XTv[:, :, j * 512:(j + 1) * 512], [("XT", j)], [f"xTc{buf}"])

    b1 = big1[:].bitcast(BF16)
    b2 = big2[:].bitcast(BF16)
    smb = small[:].bitcast(BF16)

    order = ["p0", "A", "B0", "B1", "B2", "B", "C", "D", "O", "F"]
    def run_phase(ph):
        return stop_after is None or order.index(ph) <= order.index(stop_after)

    for l in range(n_layers):
        win_v = w_in[l].rearrange("(c p) n -> p c n", p=128)

        P.skip = not run_phase("A")
        wload(WA[:, :, 0:256], win_v[:, :, 0:256], [], ["WA"])
        BCS = fv(smb, 0, [[512, 2], [1, 512]])
        Wf = fv(smb, 1024, [[256, 2], [1, 256]])
        Wcs = fv(smb, 1536, [[512, 2], [1, 512]])
        Rr = fv(smb, 2560, [[1, 128]])
        wload(BCS, c_bcs.rearrange("(c p) n -> p c n", p=128), [], ["small"])
        wload(Wf, w_fnet[l].rearrange("(c p) n -> p c n", p=128), [], ["small"])
        wload(Rr, c_r, [], ["small"])
        T3 = fv(b2, 16384, [[1, 4096]])
        wload(T3, c_t3, [], ["big2"])
        for m in range(2):
            mm(bank(m)[:, 0:256], BCS[:, m, m * 128:(m + 1) * 128], Wf[:, m, :], True, True, ["small"], pk(m))
            mm(bank(m)[:, 256:512], BCS[:, m, 256 + m * 128:256 + (m + 1) * 128], Wf[:, m, :], True, True, ["small"], pk(m))
            cp("dve", Wcs[:, m, :], bank(m), pk(m), ["small"])
        ufT = fv(b1, 0, [[4096, 2], [1, 4096]])
        load_xT(0, 0)
        for j in range(8):
            if j + 1 < 8:
                load_xT(j + 1, (j + 1) % 2)
            xc = xTc[j % 2]
            for m in range(2):
                for c in range(8):
                    mm(bank(m), WA[:, c, m * 128:(m + 1) * 128], xc[:, c, :], c == 0, c == 7, ["WA", f"xTc{j % 2}"], pk(m))
                cp("act" if m == 0 else "dve", ufT[:, m, j * 512:(j + 1) * 512], bank(m), pk(m), [("ufT", j)])
        sk = P.skip
        P.skip = not run_phase("B0")
        wload(WA[:, :, 0:512], win_v[:, :, 256:768], [], ["WA"])
        P.skip = sk
        A_sb = fv(b1, 8192, [[256, 64], [1, 256]])
        ufkeys = [("ufT", j) for j in range(8)]
        for s1 in range(64):
            pb = s1 % 2
            for c in range(2):
                l_ap = fv(b1, c * 4096 + s1, [[64, 64]])
                mm(bank(pb)[0:64, 0:256], l_ap, Wcs[:, c, 0:256], c == 0, c == 1, ufkeys + ["small"], pk(pb))
            for c in range(2):
                l_ap = fv(b1, c * 4096 + s1, [[64, 64]])
                mm(bank(pb)[64:128, 0:256], l_ap, Wcs[:, c, 256:512], c == 0, c == 1, ufkeys + ["small"], pk(pb))
            cp("act" if pb == 0 else "dve", A_sb[:, s1, :], bank(pb)[:, 0:256], pk(pb), ["A_sb"])
        Y_sb = fv(b2, 0, [[64, 256], [1, 64]])
        for cb in range(32):
            pb = cb % 2
            for cc in range(8):
                ch = cb * 8 + cc
                l_ap = fv(b1, 8192 + ch, [[256, 64]])
                mm(bank(pb)[0:64, cc * 64:(cc + 1) * 64], l_ap, Rr[:, 0:64], True, True, ["A_sb", "small"], pk(pb))
                mm(bank(pb)[64:128, cc * 64:(cc + 1) * 64], l_ap, Rr[:, 64:128], True, True, ["A_sb", "small"], pk(pb))
            cp("act" if pb == 0 else "dve", fv(b2, cb * 512, [[1, 512]]), bank(pb), pk(pb), ["Y_sb"])
        yA = fv(b1, 24576, [[4096, 2], [1, 4096]])
        for m in range(2):
            for kb in range(8):
                pb = kb % 2
                for ks in range(8):
                    k2 = kb * 8 + ks
                    l_ap = fv(b2, m * 128 * 64 + k2, [[64, 128]])
                    o_ap = fv(bank(pb), ks, [[8, 64]])
                    mm(o_ap, l_ap, T3[:, k2 * 64:(k2 + 1) * 64], True, True, ["Y_sb", "big2"], pk(pb))
                o_sb = fv(b1, 24576 + m * 4096 + kb * 8, [[64, 64], [1, 8]])
                i_ps = fv(bank(pb), 0, [[8, 64], [1, 8]])
                cp("act" if pb == 0 else "dve", o_sb, i_ps, pk(pb), [("yA", m)])
            P.dma(YTv[:, m, :], yA[:, m, :], [("yA", m)], [("YT", m)], q="pool")

        P.barrier(lambda: nc.gpsimd.memset(bar_t[:], 0.0))
        P.skip = not run_phase("B0")
        QKT = fv(b1, 0, [[4096, 3], [1, 4096]])
        Vaug = fv(b1, 12288, [[256, 32], [128, 2], [1, 128]])
        rope = fv(big2[:], 0, [[64, 32], [1, 64]])
        P.dma(rope, c_rope.rearrange("(t p) f -> p t f", p=128), [], ["rope"])
        for hh in range(4):
            P.dma(fv(big2[:], 2048 + hh * 64, [[1, 64]]), qg[l].partition_broadcast(128), [], ["gqk"])
        for hh in range(2):
            P.dma(fv(big2[:], 2048 + 256 + hh * 64, [[1, 64]]), kg[l].partition_broadcast(128), [], ["gqk"])
        memset("dve", fv(b1, 12288 + 64, [[128, 64], [1, 64]]), 1.0, ["Vaug_ones"])
        QKo, TMPo, SSo, RSo, QBo = 2560, 8704, 14848, 14944, 20480
        P.skip = not run_phase("B1")
        load_xT(0, 0)
        for half in range(2):
            for k in range(16):
                t = half * 16 + k
                j = t // 4
                if t % 4 == 0 and j + 1 < 8:
                    load_xT(j + 1, (j + 1) % 2)
                xc = xTc[j % 2]
                tl = (t % 4) * 128
                pb = t % 2
                for c in range(8):
                    mm(bank(pb), xc[:, c, tl:tl + 128], WA[:, c, 0:512], c == 0, c == 7, ["WA", f"xTc{j % 2}"], pk(pb))
                cp("act", fv(big2[:], QKo + k * 384, [[1, 384]]), bank(pb)[:, 0:384], pk(pb), ["QKh"])
                cp("dve", fv(b1, 12288 + t * 256, [[128, 2], [1, 64]]), fv(bank(pb), 384, [[64, 2], [1, 64]]), pk(pb), [("Vaug", t)])
            QKf = fv(big2[:], QKo, [[1, 6144]])
            TMPf = fv(big2[:], TMPo, [[1, 6144]])
            tt("dve", TMPf, QKf, QKf, ALU.mult, ["QKh"], ["TMP"])
            P.op("dve", lambda: nc.vector.tensor_reduce(out=fv(big2[:], SSo, [[1, 96]]), in_=fv(big2[:], TMPo, [[64, 96], [1, 64]]),
                                                        axis=AX.X, op=ALU.add), ["TMP"], ["ss"])
            ts("pool", fv(big2[:], RSo, [[1, 96]]), fv(big2[:], SSo, [[1, 96]]), 1.0 / 64.0, RMS_EPS, ALU.mult, ALU.add, ["ss"], ["rs"])
            tt("pool", fv(big2[:], RSo, [[1, 96]]), fv(big2[:], RSo, [[1, 96]]), fv(mh[:], 0, [[0, 96]]), ALU.pow, ["rs", "mh"], ["rs"])
            tt("dve", fv(big2[:], QKo, [[64, 96], [1, 64]]), fv(big2[:], QKo, [[64, 96], [1, 64]]), fv(big2[:], RSo, [[1, 96], [0, 64]]),
               ALU.mult, ["QKh", "rs"], ["QKh"])
            tt("dve", fv(big2[:], QKo, [[384, 16], [1, 384]]), fv(big2[:], QKo, [[384, 16], [1, 384]]), fv(big2[:], 2048, [[0, 16], [1, 384]]),
               ALU.mult, ["QKh", "gqk"], ["QKh"])
            for h6 in range(6):
                eng = "dve" if h6 < 3 else "pool"
                x1 = fv(big2[:], QKo + h6 * 64, [[384, 16], [32, 2], [1, 16]])
                x2 = fv(big2[:], QKo + h6 * 64 + 16, [[384, 16], [32, 2], [1, 16]])
                cosb = fv(big2[:], half * 16 * 64, [[64, 16], [16, 2], [1, 16]])
                sinb = fv(big2[:], half * 16 * 64 + 32, [[64, 16], [16, 2], [1, 16]])
                t1 = fv(big2[:], TMPo + h6 * 1024, [[32, 16], [16, 2], [1, 16]])
                t2 = fv(big2[:], TMPo + h6 * 1024 + 512, [[32, 16], [16, 2], [1, 16]])
                slot = [0, 2, 1, 3, 4, 5][h6]
                o1 = fv(b1, QBo + slot * 64, [[384, 16], [32, 2], [1, 16]])
                o2 = fv(b1, QBo + slot * 64 + 16, [[384, 16], [32, 2], [1, 16]])
                tt(eng, t1, x1, cosb, ALU.mult, ["QKh", "rope", "TMP"], [("t1", h6)])
                tt(eng, t2, x2, sinb, ALU.mult, ["QKh", "rope", "TMP"], [("t2", h6)])
                tt(eng, o1, t1, t2, ALU.subtract, [("t1", h6), ("t2", h6)], [("qb", h6)])
                tt(eng, t1, x1, sinb, ALU.mult, ["QKh", "rope", ("qb", h6)], [("t1", h6)])
                tt(eng, t2, x2, cosb, ALU.mult, ["QKh", "rope", ("qb", h6)], [("t2", h6)])
                tt(eng, o2, t1, t2, ALU.add, [("t1", h6), ("t2", h6), "QKh"], [("qb", h6)])
            qbk = [("qb", h6) for h6 in range(6)]
            for k in range(16):
                t = half * 16 + k
                pst = bank(6 + k % 2).bitcast(BF16)
                for k3 in range(3):
                    tr(pst[:, k3 * 128:(k3 + 1) * 128], fv(b1, QBo + k * 384 + k3 * 128, [[1, 128]]), ident[:], qbk + ["ident"], pk(6 + k % 2))
                cp("act" if k % 2 == 0 else "dve", fv(b1, t * 128, [[4096, 3], [1, 128]]), fv(pst, 0, [[128, 3], [1, 128]]), pk(6 + k % 2), [("QKT", t)])
        sk = P.skip
        P.skip = not run_phase("C")
        wload(WA[:, :, 0:512], win_v[:, :, 768:1280], [], ["WA"])
        P.skip = sk
        P.skip = not run_phase("B")
        qkt_keys = [("QKT", t) for t in range(NT)]
        yB = fv(b2, 16384, [[4096, 2], [1, 4096]])
        PT = [fv(smb, i * 1024, [[1, 1024]]) for i in range(2)]
        rec = fv(big2[:], 12288, [[1, 2048]])
        osb = fv(big2[:], 14336, [[1, 2048]])
        iters = [(qc, kt) for qc in range(8) for kt in range(NT)]

        def b_S(it, hb):
            qc, kt = iters[it]
            for g in range(2):
                pr = slice(g * 64, (g + 1) * 64)
                mm(bank(hb * 2 + g), QKT[pr, 2, kt * 128:(kt + 1) * 128], QKT[pr, hb, qc * 512:(qc + 1) * 512], True, True,
                   qkt_keys, pk(hb * 2 + g))

        def b_E(it, hb):
            act(PT[hb], PS[:, hb * 1024:hb * 1024 + 1024], AF.Exp, pk(hb * 2, 2), [f"PT{hb}"], scale=0.125)

        def b_P(it, hb):
            qc, kt = iters[it]
            for g in range(2):
                mm(bank(4 + hb * 2 + g), Vaug[:, kt, g, :], PT[hb][:, g * 512:(g + 1) * 512], kt == 0, kt == NT - 1,
                   [f"PT{hb}", ("Vaug", kt), "Vaug_ones"], pk(4 + hb * 2 + g))
            if kt == NT - 1 and hb == 1:
                for hf in range(2):
                    cp("dve", rec[0:64, hf * 1024:(hf + 1) * 1024], PS[64:128, 2048 + hf * 1024:2048 + (hf + 1) * 1024], pk(4 + 2 * hf, 2), ["rec"])
                    cp("act", osb[0:64, hf * 1024:(hf + 1) * 1024], PS[0:64, 2048 + hf * 1024:2048 + (hf + 1) * 1024], pk(4 + 2 * hf, 2), ["osb"])
                P.op("dve", lambda: nc.vector.reciprocal(out=rec[0:64, :], in_=rec[0:64, :]), ["rec"], ["rec"])
                for hb2 in range(2):
                    for g in range(2):
                        bk = hb2 * 2 + g
                        tt("dve", yB[hb2 * 64:(hb2 + 1) * 64, g, qc * 512:(qc + 1) * 512], osb[0:64, bk * 512:(bk + 1) * 512],
                           rec[0:64, bk * 512:(bk + 1) * 512], ALU.mult, ["osb", "rec"], [("yB", g)])
                if qc == 7:
                    for g in range(2):
                        P.dma(YTv[:, 2 + g, :], yB[:, g, :], [("yB", g)], [("YT", 2 + g)], q="pool")

        b_S(0, 0)
        b_S(0, 1)
        for it in range(len(iters)):
            for hb in range(2):
                b_E(it, hb)
                b_P(it, hb)
                if it + 1 < len(iters):
                    b_S(it + 1, hb)

        P.barrier(lambda: nc.gpsimd.memset(bar_t[:], 0.0))
        P.skip = not run_phase("C")
        HW_ = 4128
        hbuf = fv(big1[:], 0, [[HW_, 2], [1, HW_]])
        accC = fv(big2[:], 0, [[4096, 2], [1, 4096]])
        dwT = fv(big2[:], 8192, [[31, 2], [1, 31]])
        cb_sb = fv(big2[:], 8256, [[1, 2]])
        cg_sb = fv(big2[:], 8258, [[1, 2]])
        cbe_sb = fv(big2[:], 8260, [[1, 2]])
        Wpw = fv(b2, 16640, [[256, 2], [1, 256]])
        dwraw = fv(big2[:], 8704, [[1, 256]])
        P.dma(dwraw[0:31, :], conv_dw[l], [], ["dwraw"])
        for m in range(2):
            tr(bank(6)[:, m * 32:m * 32 + 31], dwraw[0:31, m * 128:(m + 1) * 128], identf[0:31, 0:31], ["dwraw", "identf"], pk(6))
        cp("dve", dwT, fv(bank(6), 0, [[32, 2], [1, 31]]), pk(6), ["cpar"])
        P.dma(cb_sb, conv_b[l].rearrange("(m p) -> p m", p=128), [], ["cpar"], allow_slow_non_contiguous=True)
        P.dma(cg_sb, cln_g[l].rearrange("(m p) -> p m", p=128), [], ["cpar"], allow_slow_non_contiguous=True)
        P.dma(cbe_sb, cln_b[l].rearrange("(m p) -> p m", p=128), [], ["cpar"], allow_slow_non_contiguous=True)
        wload(Wpw, w_pw[l].rearrange("(c p) n -> p c n", p=128), [], ["Wpw"])
        for m in range(2):
            memset("pool", fv(big1[:], m * HW_, [[1, 16]]), 0.0, [("hbuf", "padl")])
            memset("pool", fv(big1[:], m * HW_ + 16 + 4096, [[1, 16]]), 0.0, [("hbuf", "padr")])
        sig = [fv(big1[:], 8256 + i * 512, [[1, 512]]) for i in range(2)]
        HB = 26752
        DG = 18432
        memset("pool", fv(b1, HB, [[1, 16]]), 0.0, [("hb16", "padl")])
        memset("pool", fv(b1, HB + 16 + 4096, [[1, 16]]), 0.0, [("hb16", "padr")])
        for k in range(31):
            ts("pool", fv(b2, DG + k * 128, [[1, 128]]), ident[:], dwT[:, 1, k:k + 1], None, ALU.mult, None, ["ident", "cpar"], ["diag"])
        load_xT(0, 0)
        for j in range(8):
            if j + 1 < 8:
                load_xT(j + 1, (j + 1) % 2)
            xc = xTc[j % 2]
            for m in range(4):
                for c in range(8):
                    mm(bank(m), WA[:, c, m * 128:(m + 1) * 128], xc[:, c, :], c == 0, c == 7, ["WA", f"xTc{j % 2}"], pk(m))
            for m in range(2):
                act(sig[m], bank(2 + m), AF.Sigmoid, pk(2 + m), [f"sig{m}"])
            tt("dve", hbuf[:, 0, 16 + j * 512:16 + (j + 1) * 512], bank(0), sig[0], ALU.mult, pk(0) + ["sig0"], [("hbuf", j)])
            tt("dve", fv(b1, HB + 16 + j * 512, [[1, 512]]), bank(1), sig[1], ALU.mult, pk(1) + ["sig1"], [("hb16", j)])
        sk = P.skip
        P.skip = not run_phase("D")
        wload(WA[:, :, 0:768], win_v[:, :, 1280:2048], [], ["WA"])
        P.skip = sk
        hkeys_all = [("hbuf", j) for j in range(8)] + [("hbuf", "padl"), ("hbuf", "padr")]
        h16keys = [("hb16", j) for j in range(8)] + [("hb16", "padl"), ("hb16", "padr")]
        for pc in range(4):
            o = accC[:, 0, pc * 1024:(pc + 1) * 1024]
            for k in range(31):
                src = hbuf[:, 0, 16 + pc * 1024 + k - 15:16 + pc * 1024 + k - 15 + 1024]
                if k == 0:
                    ts("dve", o, src, dwT[:, 0, 0:1], cb_sb[:, 0:1], ALU.mult, ALU.add, hkeys_all + ["cpar"], [("accC", 0, pc)])
                else:
                    stt(o, src, dwT[:, 0, k:k + 1], o, ALU.mult, ALU.add, hkeys_all + ["cpar"], [("accC", 0, pc)])
            for jj in range(2):
                j = pc * 2 + jj
                pb = 4 + j % 2
                for k in range(31):
                    mm(bank(pb), fv(b2, DG + k * 128, [[1, 128]]), fv(b1, HB + 16 + j * 512 + k - 15, [[1, 512]]), k == 0, k == 30,
                       h16keys + ["diag"], pk(pb))
                act(accC[:, 1, j * 512:(j + 1) * 512], bank(pb), AF.Identity, pk(pb) + ["cpar"], [("accC", 1, pc)], bias=cb_sb[:, 1:2])
        sqb = [fv(big1[:], 9280 + i * 512, [[1, 512]]) for i in range(2)]
        mean_sb = fv(big1[:], 10304, [[1, 512]])
        var_sb = fv(big1[:], 10816, [[1, 512]])
        xh = [fv(big1[:], 11328 + i * 512, [[1, 512]]) for i in range(2)]
        hact = fv(b1, 2 * 12352, [[512, 2], [1, 512]])
        yC = fv(b1, 2 * 12864, [[512, 2], [1, 512]])
        for j in range(8):
            cs = slice(j * 512, (j + 1) * 512)
            akeys = [("accC", m, j // 2) for m in range(2)]
            for m in range(2):
                act(sqb[m], accC[:, m, cs], AF.Square, [("accC", m, j // 2)], [f"sq{m}"])
            for m in range(2):
                mm(bank(0), onesF[:], accC[:, m, cs], m == 0, m == 1, akeys + ["onesF"], pk(0))
            for m in range(2):
                mm(bank(1), onesF[:], sqb[m], m == 0, m == 1, [f"sq{m}", "onesF"], pk(1))
            cp("act", mean_sb, bank(0), pk(0), ["mean"])
            tt("dve", var_sb, mean_sb, mean_sb, ALU.mult, ["mean"], ["var"])
            tt("dve", var_sb, bank(1), var_sb, ALU.subtract, pk(1) + ["var"], ["var"])
            ts("dve", var_sb, var_sb, LN_EPS, None, ALU.add, None, ["var"], ["var"])
            act(var_sb, var_sb, AF.Sqrt, ["var"], ["var"])
            P.op("dve", lambda: nc.vector.reciprocal(out=var_sb, in_=var_sb), ["var"], ["var"])
            for m in range(2):
                tt("dve", xh[m], accC[:, m, cs], mean_sb, ALU.subtract, [("accC", m, j // 2), "mean"], [f"xh{m}"])
                tt("dve", xh[m], xh[m], var_sb, ALU.mult, [f"xh{m}", "var"], [f"xh{m}"])
                act(hact[:, m, :], xh[m], AF.Silu, [f"xh{m}", "cpar"], ["hact"], scale=cg_sb[:, m:m + 1], bias=cbe_sb[:, m:m + 1])
            for mo in range(2):
                for c in range(2):
                    mm(bank(2 + mo), Wpw[:, c, mo * 128:(mo + 1) * 128], hact[:, c, :], c == 0, c == 1, ["hact", "Wpw"], pk(2 + mo))
                cp("dve", yC[:, mo, :], bank(2 + mo), pk(2 + mo), ["yC"])
            P.dma(YTv[:, 4:6, cs], yC, ["yC"], [("YT", 4), ("YT", 5)], q="pool")

        P.barrier(lambda: nc.gpsimd.memset(bar_t[:], 0.0))
        P.skip = not run_phase("D")
        for hp in range(2):
            if hp == 1:
                P.barrier(lambda: nc.gpsimd.memset(bar_t[:], 0.0))
            QTd = fv(b1, 0, [[1, 4096]])
            KTo = {1: 4096, 4: 8192, 16: 12288}
            VTo = {1: 16384, 4: 20480, 16: 24576}
            accD = fv(big2[:], 0, [[4096, 2], [1, 4096]])
            B8 = fv(b2, 24576, [[128, 24], [1, 128]])
            P.dma(B8, B8d.rearrange("k p q -> p k q"), ["B8d"], ["B8"])
            recD = fv(big2[:], 8192, [[1, 4096]])
            yD = fv(b1, 28672, [[1, 4096]])
            load_xT(0, 0)
            for j in range(8):
                if j + 1 < 8:
                    load_xT(j + 1, (j + 1) % 2)
                xc = xTc[j % 2]
                for c in range(8):
                    mm(bank(0), WA[:, c, hp * 128:(hp + 1) * 128], xc[:, c, :], c == 0, c == 7, ["WA", f"xTc{j % 2}"], pk(0))
                for c in range(8):
                    mm(bank(1), WA[:, c, 256 + hp * 128:256 + (hp + 1) * 128], xc[:, c, :], c == 0, c == 7, ["WA", f"xTc{j % 2}"], pk(1))
                cp("act", QTd[:, j * 512:(j + 1) * 512], bank(0), pk(0), [("QTd", j)])
                cp("dve", fv(b1, 4096 + j * 512, [[1, 512]]), bank(1), pk(1), [("KTd", j)])
                cp("act", fv(b1, 8192 + j * 128, [[1024, 4], [1, 128]]), fv(bank(1), 0, [[1, 4], [4, 128]]), pk(1), [("KTd", j)])
                cp("dve", fv(b1, 12288 + j * 32, [[256, 16], [1, 32]]), fv(bank(1), 0, [[1, 16], [16, 32]]), pk(1), [("KTd", j)])
                for tq in range(4):
                    t = j * 4 + tq
                    for c in range(8):
                        mm(bank(2 + tq % 2)[:, 0:128], xc[:, c, tq * 128:(tq + 1) * 128], WA[:, c, 512 + hp * 128:512 + (hp + 1) * 128], c == 0, c == 7,
                           ["WA", f"xTc{j % 2}"], pk(2 + tq % 2))
                    cp("act" if tq % 2 == 0 else "dve", fv(b1, 16384 + t * 128, [[1, 128]]), bank(2 + tq % 2)[:, 0:128], pk(2 + tq % 2), [("Vnat", t)])
                    P.dma(VD[t * 128:(t + 1) * 128, :], fv(b1, 16384 + t * 128, [[1, 128]]), [("Vnat", t)], [("VD", t)], q="pool")
            if hp == 1:
                sk = P.skip
                P.skip = not run_phase("O")
                for hf in range(2):
                    wload(WA[:, :, hf * 512:(hf + 1) * 512], w_out[l].rearrange("(c p) n -> p c n", p=128)[:, :, hf * 512:(hf + 1) * 512], [], ["WA"])
                load_ln_params(ln1_g[l], ln1_b[l])
                P.skip = sk
            vdk = [("VD", t) for t in range(NT)]
            VDr = VD
            for r in range(4):
                src = bass.AP(VDr.tensor, VDr.offset + r * 128, [[4 * 128, 128], [4 * 128 * 128, 8], [1, 128]])
                P.dma(fv(b1, 20480 + r * 8 * 128, [[128, 8], [1, 128]]), src, vdk, [("VD4", r)])
            for r in range(16):
                src = bass.AP(VDr.tensor, VDr.offset + r * 128, [[16 * 128, 128], [16 * 128 * 128, 2], [1, 128]])
                P.dma(fv(b1, 24576 + r * 2 * 128, [[128, 2], [1, 128]]), src, vdk, [("VD16", r)])
            qk_keys = [("QTd", j) for j in range(8)] + [("KTd", j) for j in range(8)]
            PTd = [fv(smb, i * 512, [[1, 512]]) for i in range(3)]
            diters = []
            for di, d in enumerate(DILS):
                for r in range(d):
                    for i in range(-1, (S // d) // 128):
                        diters.append((di, d, r, i))

            def d_info(it):
                di, d, r, i = diters[it]
                Ld = S // d
                ntile = Ld // 128
                q0 = 64 if i == -1 else 0
                q1 = 64 if i == ntile - 1 else 128
                hasA = i >= 0
                hasB = i + 1 <= ntile - 1
                tok0 = r + d * (128 * i + 64 + q0)
                if d == 1:
                    vkeys = [("Vnat", t) for t in range(NT)]
                elif d == 4:
                    vkeys = [("VD4", r)]
                else:
                    vkeys = [("VD16", r)]
                blocks = [(ab, jt) for ab, has, jt in ((0, hasA, i), (1, hasB, i + 1)) if has]
                return di, d, r, i, Ld, ntile, q0, q1, tok0, vkeys, blocks

            def d_S(it):
                di, d, r, i, Ld, ntile, q0, q1, tok0, vkeys, blocks = d_info(it)
                nq = q1 - q0
                for hh in range(2):
                    sbk = (it % 3) * 2 + hh
                    pr = slice(hh * 64, (hh + 1) * 64)
                    rhs_q = fv(b1[pr, :], tok0, [[d, nq]])
                    for ab, jt in blocks:
                        reg = bank(sbk)[:, ab * 128 + q0:ab * 128 + q1]
                        kcol = KTo[d] + r * Ld + jt * 128
                        mm(reg, fv(b1[pr, :], kcol, [[1, 128]]), rhs_q, True, True, qk_keys, pk(sbk))

            def d_E(it):
                di, d, r, i, Ld, ntile, q0, q1, tok0, vkeys, blocks = d_info(it)
                ptb = PTd[it % 3]
                ebase = 24576 + (hp * 2 * 6 + di * 2) * 128
                full = len(blocks) == 2 and q0 == 0 and q1 == 128
                for hh in range(2):
                    sbk = (it % 3) * 2 + hh
                    if full:
                        act(ptb[:, hh * 256:(hh + 1) * 256], bank(sbk)[:, 0:256], AF.Exp, pk(sbk), [f"PTd{it % 3}"], scale=0.125)
                    else:
                        for ab, jt in blocks:
                            col = (hh * 2 + ab) * 128
                            act(ptb[:, col + q0:col + q1], bank(sbk)[:, ab * 128 + q0:ab * 128 + q1], AF.Exp, pk(sbk), [f"PTd{it % 3}"], scale=0.125)
                if full:
                    ptv = fv(smb, (it % 3) * 512, [[256, 2], [128, 2], [1, 128]])
                    tt("dve", ptv, ptv, fv(b2, ebase, [[768, 2], [128, 2], [1, 128]]), ALU.mult, [f"PTd{it % 3}", "B8"], [f"PTd{it % 3}"])
                else:
                    for hh in range(2):
                        for ab, jt in blocks:
                            col = (hh * 2 + ab) * 128
                            tt("dve", ptb[:, col + q0:col + q1], ptb[:, col + q0:col + q1],
                               fv(b2, ebase + hh * 768 + ab * 128 + q0, [[1, q1 - q0]]), ALU.mult, [f"PTd{it % 3}", "B8"], [f"PTd{it % 3}"])

            def d_P(it):
                di, d, r, i, Ld, ntile, q0, q1, tok0, vkeys, blocks = d_info(it)
                nq = q1 - q0
                obk = 6 + it % 2
                ptb = PTd[it % 3]
                for hh in range(2):
                    oreg = bank(obk)[:, hh * 128 + q0:hh * 128 + q1]
                    for bi, (ab, jt) in enumerate(blocks):
                        col = (hh * 2 + ab) * 128
                        vcol = VTo[d] + (r * ntile + jt) * 128 + hh * 64
                        mm(oreg[0:64, :], fv(b1, vcol, [[1, 64]]), ptb[:, col + q0:col + q1], bi == 0, bi == len(blocks) - 1,
                           [f"PTd{it % 3}"] + vkeys, pk(obk))
                    for bi, (ab, jt) in enumerate(blocks):
                        col = (hh * 2 + ab) * 128
                        mm(oreg[64:128, :], ones64[:], ptb[:, col + q0:col + q1], bi == 0, bi == len(blocks) - 1,
                           [f"PTd{it % 3}", "ones64"], pk(obk))
                for hh in range(2):
                    oreg = bank(obk)[:, hh * 128 + q0:hh * 128 + q1]
                    dst = fv(big2[:], hh * 4096 + tok0, [[d, nq]])
                    if di == 0:
                        cp("dve", dst, oreg, pk(obk), [("accD", hh)])
                    else:
                        tt("dve", dst, oreg, dst, ALU.add, pk(obk) + [("accD", hh)], [("accD", hh)])

            d_S(0)
            d_S(1)
            for it in range(len(diters)):
                if it + 2 < len(diters):
                    d_S(it + 2)
                d_E(it)
                d_P(it)
            for hh in range(2):
                cp("dve", recD[0:64, :], accD[64:128, hh, :], [("accD", hh)], ["recD"])
                act(recD[0:64, :], recD[0:64, :], AF.Ln, ["recD"], ["recD"])
                act(recD[0:64, :], recD[0:64, :], AF.Exp, ["recD"], ["recD"], scale=-1.0)
                tt("dve", yD[hh * 64:(hh + 1) * 64, :], accD[0:64, hh, :], recD[0:64, :], ALU.mult, [("accD", hh), "recD"], ["yD"])
            P.dma(YTv[:, 6 + hp, :], yD, ["yD"], [("YT", 6 + hp)], q="pool")

        P.barrier(lambda: nc.gpsimd.memset(bar_t[:], 0.0))

        P.skip = not run_phase("O")
        W1 = fv(b1, 0, [[4096, 8], [1, 4096]])
        W2 = fv(b2, 0, [[1024, 32], [1, 1024]])
        w1v = w_ff1[l].rearrange("(c p) n -> p c n", p=128)
        w2v = w_ff2[l].rearrange("(c p) n -> p c n", p=128)
        wq = []
        for c in range(8):
            for hf in range(2):
                wq.append((W1[:, c, hf * 2048:(hf + 1) * 2048], w1v[:, c, hf * 2048:(hf + 1) * 2048], "W1"))
        for c in range(32):
            wq.append((W2[:, c, :], w2v[:, c, :], "W2"))
        ytk = [("YT", k) for k in range(8)]

        def load_yT(j, buf):
            P.dma(xTc[buf][:], YTv[:, :, j * 512:(j + 1) * 512], ytk, [f"xTc{buf}"])

        def o_mm(tp):
            j = tp // 2
            if tp % 2 == 0 and j + 1 < 8:
                load_yT(j + 1, (j + 1) % 2)
            yc = xTc[j % 2]
            for k in range(2):
                t = 2 * tp + k
                tl = (t % 4) * 128
                P.dma(xt_in[k][:], X[t * 128:(t + 1) * 128, :], [("X", t)], [f"xin{k}"])
                pb = 2 * k
                for hf in range(2):
                    for c in range(8):
                        mm(bank(pb + hf), yc[:, c, tl:tl + 128], WA[:, c, hf * 512:(hf + 1) * 512], c == 0, c == 7, ["WA", f"xTc{j % 2}"], pk(pb + hf))

        def o_ln(tp):
            for k in range(2):
                pb = 2 * k
                stt(rt[k][:], xt_in[k][:], ALPHA, PS[:, pb * 512:pb * 512 + 1024], ALU.mult, ALU.add, [f"xin{k}"] + pk(pb, 2), [f"rt{k}"])
            ln_pair([rt[0][:], rt[1][:]], ["rt0", "rt1"], [rt[0][:], rt[1][:]], ["rt0", "rt1"])
            for k in range(2):
                t = 2 * tp + k
                P.dma(X[t * 128:(t + 1) * 128, :], rt[k][:], [f"rt{k}"], [("X", t)], q="pool")

        def o_tail(tp):
            for k in range(2):
                transpose_tile(rt[k][:], f"rt{k}", xTs[0][:, :, k * 128:(k + 1) * 128], "xTs0", k)
            P.dma(XTv[:, :, tp * 256:(tp + 1) * 256], xTs[0][:], ["xTs0"], [("XT1", tp)], q="pool")
            for _ in range(3):
                if wq:
                    wd, ws, wk = wq.pop(0)
                    wload(wd, ws, [], [wk])

        load_yT(0, 0)
        o_mm(0)
        for tp in range(NT // 2):
            o_ln(tp)
            if tp + 1 < NT // 2:
                o_mm(tp + 1)
            o_tail(tp)

        P.barrier(lambda: nc.gpsimd.memset(bar_t[:], 0.0))
        P.skip = not run_phase("F")
        load_ln_params(ln2_g[l], ln2_b[l])
        last = (l == n_layers - 1)
        hT = fv(WA[:], 0, [[256, 32], [1, 256]])

        def load_x1T(jc, buf):
            P.dma(xTc[buf][:, :, 0:256], XTv[:, :, jc * 256:(jc + 1) * 256], [("XT1", jc)], [f"xTc{buf}"])

        def f_ffn1(jc, h0, h1):
            if h0 == 0 and jc + 1 < 16:
                load_x1T(jc + 1, (jc + 1) % 2)
            xc = xTc[jc % 2]
            for hc in range(h0, h1):
                pb = hc % 3
                for c in range(8):
                    mm(bank(pb)[:, 0:256], W1[:, c, hc * 128:(hc + 1) * 128], xc[:, c, 0:256], c == 0, c == 7, ["W1", f"xTc{jc % 2}"], pk(pb))
                rl = fv(small[:], (hc % 2) * 256, [[1, 256]])
                act(rl, bank(pb)[:, 0:256], AF.Relu, pk(pb), [f"relu{hc % 2}"])
                tt("pool", hT[:, hc, :], rl, rl, ALU.mult, [f"relu{hc % 2}"], [("hT", hc)])

        def f_ffn2(jc):
            for k in range(2):
                t = jc * 2 + k
                P.dma(xt_in[k][:], X[t * 128:(t + 1) * 128, :], [("X", t)], [f"xin{k}"])
                pb = 3 + 2 * k
                for hf in range(2):
                    for hc in range(32):
                        mm(bank(pb + hf), hT[:, hc, k * 128:(k + 1) * 128], W2[:, hc, hf * 512:(hf + 1) * 512], hc == 0, hc == 31, [("hT", hc), "W2"], pk(pb + hf))

        def f_ln_a(jc):
            for k in range(2):
                pb = 3 + 2 * k
                stt(rt[k][:], xt_in[k][:], ALPHA, PS[:, pb * 512:pb * 512 + 1024], ALU.mult, ALU.add, [f"xin{k}"] + pk(pb, 2), [f"rt{k}"])
            ln_pair([rt[0][:], rt[1][:]], ["rt0", "rt1"], [rt[0][:], rt[1][:]], ["rt0", "rt1"], rstd_on_pool=True, part="a")

        def f_ln_b(jc):
            ln_pair([rt[0][:], rt[1][:]], ["rt0", "rt1"], [rt[0][:], rt[1][:]], ["rt0", "rt1"], rstd_on_pool=True, part="b")
            for k in range(2):
                t = jc * 2 + k
                if last:
                    P.dma(out[t * 128:(t + 1) * 128, :], rt[k][:], [f"rt{k}"], ["out"], q="pool")
                else:
                    P.dma(X[t * 128:(t + 1) * 128, :], rt[k][:], [f"rt{k}"], [("X", t)], q="pool")

        def f_tail(jc):
            if last:
                return
            for k in range(2):
                transpose_tile(rt[k][:], f"rt{k}", xTs[0][:, :, k * 128:(k + 1) * 128], "xTs0", k)
            P.dma(XTv[:, :, jc * 256:(jc + 1) * 256], xTs[0][:], ["xTs0"], [("XT", jc // 2)], q="pool")

        load_x1T(0, 0)
        for jc in range(16):
            f_ffn1(jc, 0, 8)
            if jc > 0:
                f_ln_a(jc - 1)
            f_ffn1(jc, 8, 16)
            if jc > 0:
                f_ln_b(jc - 1)
            f_ffn1(jc, 16, 32)
            if jc > 0:
                f_tail(jc - 1)
            f_ffn2(jc)
        f_ln_a(15)
        f_ln_b(15)
        f_tail(15)
        P.barrier(lambda: nc.gpsimd.memset(bar_t[:], 0.0))

    P.skip = False
    fk = ["out"]
    if stop_after is not None and stop_after != "F":
        P.barrier(lambda: nc.gpsimd.memset(bar_t[:], 0.0))
        P.dma(out, X, [], ["out"])
    if debug:
        P.barrier(lambda: nc.gpsimd.memset(bar_t[:], 0.0))
        P.dma(dbg_yt, YT, [], ["dbg_yt"])
        P.dma(dbg_x1, X, [], ["dbg_x1"])
        fk += ["dbg_yt", "dbg_x1"]
    stats = P.emit(final_wait_keys=fk)
    return nc, stats


def _t5_bucket_np(rel):
    nb = 16
    max_exact = 8
    ret = np.where(rel > 0, nb, 0)
    n = np.abs(rel)
    nf = np.maximum(n, 1).astype(np.float32)
    large = max_exact + (np.log(nf / np.float32(max_exact)) / np.float32(math.log(1024 / max_exact))
                         * np.float32(nb - max_exact)).astype(np.int32)
    large = np.minimum(large, nb - 1)
    return ret + np.where(n < max_exact, n, large)


def _constants():
    c = {}
    c["c_ident"] = np.eye(128, dtype=np.float32)
    nf = 16
    inv = (10000.0 ** (-np.arange(nf, dtype=np.float32) / nf)).astype(np.float32)
    t = np.arange(S)
    row = (t // 64).astype(np.float32)
    col = (t % 64).astype(np.float32)
    ang = np.concatenate([row[:, None] * inv, col[:, None] * inv], -1).astype(np.float32)
    c["c_rope"] = np.concatenate([np.cos(ang), np.sin(ang)], -1).astype(np.float32)
    k = np.arange(64)
    C64 = np.cos(2 * np.pi * np.outer(k, k) / 64)
    S64 = np.sin(2 * np.pi * np.outer(k, k) / 64)
    BC = np.kron(np.eye(4), C64) / 8
    BS = np.kron(np.eye(4), S64) / 8
    c["c_bcs"] = np.concatenate([BC, BS], 1).astype(np.float32)
    Rre = np.concatenate([C64, -S64], 0)
    Rim = np.concatenate([-S64, -C64], 0)
    c["c_r"] = np.concatenate([Rre, Rim], 1).astype(np.float32)
    s1 = np.arange(64)[:, None, None]
    k2 = np.arange(64)[None, :, None]
    k1 = np.arange(64)[None, None, :]
    th = 2 * np.pi * ((s1 * (64 * k1 + k2)) % 4096) / 4096
    T3 = np.concatenate([np.cos(th) / 64, np.sin(th) / 64], 0)
    c["c_t3"] = T3.reshape(128, 4096).astype(np.float32)
    p = np.arange(128)[:, None]
    q = np.arange(128)[None, :]
    mA = np.where((p - q >= 0) & (p - q <= 128), 0.0, MASKV)
    mB = np.where((p - q >= -128) & (p - q <= 0), 0.0, MASKV)
    c["c_dmask"] = np.stack([mA, mB]).astype(np.float32)
    return c


def _dbias(rel_bias):
    p = np.arange(128)[:, None]
    q = np.arange(128)[None, :]
    o = np.zeros((4, 3, 2, 128, 128), np.float32)
    for di, d in enumerate(DILS):
        for ab, off in ((0, -64), (1, 64)):
            rel = np.clip(p - q + off, -64, 64)
            idx = _t5_bucket_np(rel * d)
            for h in range(4):
                o[h, di, ab] = rel_bias[idx, h]
    return o


_CACHE = {}


def kernel(**inputs):
    inputs = {k: np.asarray(v) for k, v in inputs.items()}
    if "nc" not in _CACHE:
        _CACHE["nc"] = build_program()
    nc, _ = _CACHE["nc"]
    consts = _constants()
    shared = {k: np.ascontiguousarray(v, dtype=np.float32) for k, v in inputs.items() if k not in ("x", "rel_bias")}
    shared.update(consts)
    shared["dbias"] = _dbias(inputs["rel_bias"].astype(np.float32))
    x = inputs["x"].astype(np.float32)
    in_maps = []
    for c in range(8):
        m = dict(shared)
        m["x"] = np.ascontiguousarray(x[c % 4])
        in_maps.append(m)
    res = run_bass_kernel_spmd(nc, in_maps, core_ids=list(range(8)))
    return np.stack([res.results[c]["out"] for c in range(4)], 0).astype(np.float32)
```

```python
import math
import os
import numpy as np
BSTEP = int(os.environ.get("BSTEP", "99"))
import ml_dtypes
import concourse.bass as bass
import concourse.mybir as mybir
from concourse.bass_utils import run_bass_kernel_spmd

F32 = mybir.dt.float32
BF16 = mybir.dt.bfloat16
AF = mybir.ActivationFunctionType
ALU = mybir.AluOpType
AX = mybir.AxisListType

S = 4096
D = 1024
NT = 32
DEPTH = 4
DFF = 4096
ALPHA = (2 * DEPTH) ** 0.25
LN_EPS = 1e-5
RMS_EPS = 1e-6
DILS = (1, 4, 16)
MASKV = -240000.0

ENGS = ("pe", "act", "dve", "pool", "sp")


class Prog:
    def __init__(self, nc, n_dma_slots=48):
        self.nc = nc
        self.ops = []
        self.n_dma_slots = n_dma_slots
        self.eng_obj = {"pe": nc.tensor, "act": nc.scalar, "dve": nc.vector,
                        "pool": nc.gpsimd, "sp": nc.sync}

    skip = False

    def op(self, eng, fn, reads=(), writes=(), dma=False):
        if self.skip:
            return
        ps_r = tuple(k for k in reads if isinstance(k, tuple) and k[0] == "ps" and k not in writes)
        self.ops.append((eng, fn, tuple(reads), tuple(writes) + ps_r, dma))

    def dma(self, out, in_, reads, writes, q="sp", **kw):
        e = self.eng_obj[q]
        self.op(q, lambda: e.dma_start(out=out, in_=in_, **kw), reads, writes, dma=True)

    def barrier(self, fn):
        if self.skip:
            return
        self.ops.append(("pool", fn, "BARRIER", (), False))

    def emit(self, final_wait_keys=()):
        nc = self.nc
        ops = self.ops
        n = len(ops)
        last_w = {}
        readers = {}
        deps = [None] * n
        bar = None
        last_eng = {}
        dma_since = []
        for i, (eng, fn, reads, writes, is_dma) in enumerate(ops):
            if reads == "BARRIER":
                d = set(last_eng.values()) | set(dma_since)
                if bar is not None:
                    d.add(bar)
                deps[i] = d
                last_w = {}
                readers = {}
                dma_since = []
                last_eng = {}
                bar = i
                ops[i] = (eng, fn, (), (), False)
                continue
            if is_dma:
                dma_since.append(i)
            else:
                last_eng[eng] = i
            d = set()
            if bar is not None:
                d.add(bar)
            for k in reads:
                if k in last_w:
                    d.add(last_w[k])
            for k in writes:
                if k in last_w:
                    d.add(last_w[k])
                for r in readers.get(k, ()):
                    d.add(r)
            d.discard(i)
            deps[i] = d
            for k in reads:
                readers.setdefault(k, []).append(i)
            for k in writes:
                last_w[k] = i
                readers[k] = []
        final_deps = set()
        for k in final_wait_keys:
            if k in last_w:
                final_deps.add(last_w[k])
        needed = set()
        for i in range(n):
            eng_i, _, _, _, dma_i = ops[i]
            keep = set()
            for j in deps[i]:
                eng_j, _, _, _, dma_j = ops[j]
                if (not dma_j) and (not dma_i) and eng_i == "pe" and eng_j == "pe":
                    continue
                keep.add(j)
            deps[i] = keep
            needed |= keep
        needed |= final_deps
        sems = {e: nc.alloc_semaphore(name=f"s_{e}") for e in ENGS}
        slots = [nc.alloc_semaphore(name=f"s_dma{k}") for k in range(self.n_dma_slots)]
        cnt = {e: 0 for e in ENGS}
        slot_use = [0] * self.n_dma_slots
        half = self.n_dma_slots // 2
        slot_rr = {"sw": 0, "hw": 0}
        done_tok = [None] * n
        prev_slot_tok = [None] * n
        for i, (eng, fn, reads, writes, is_dma) in enumerate(ops):
            if is_dma:
                kind = "sw" if eng == "pool" else "hw"
                s = slot_rr[kind] + (half if kind == "sw" else 0)
                slot_rr[kind] = (slot_rr[kind] + 1) % half
                if slot_use[s] > 0:
                    prev_slot_tok[i] = (("slot", s), 16 * slot_use[s])
                slot_use[s] += 1
                done_tok[i] = (("slot", s), 16 * slot_use[s])
            elif i in needed:
                cnt[eng] += 1
                done_tok[i] = (("eng", eng), cnt[eng])
        seen = {e: {} for e in ENGS}

        def semh(key):
            return sems[key[1]] if key[0] == "eng" else slots[key[1]]

        n_wait = 0
        for i, (eng, fn, reads, writes, is_dma) in enumerate(ops):
            e = self.eng_obj[eng]
            want = {}
            for j in deps[i]:
                key, val = done_tok[j]
                if want.get(key, 0) < val:
                    want[key] = val
            if prev_slot_tok[i] is not None:
                key, val = prev_slot_tok[i]
                if want.get(key, 0) < val:
                    want[key] = val
            for key, val in want.items():
                if seen[eng].get(key, 0) >= val:
                    continue
                e.wait_ge(semh(key), val)
                seen[eng][key] = val
                n_wait += 1
            ins = fn()
            if is_dma:
                ins.then_inc(semh(done_tok[i][0]), 16)
            elif done_tok[i] is not None:
                ins.then_inc(semh(done_tok[i][0]), 1)
        e = self.eng_obj["sp"]
        want = {}
        for j in final_deps:
            key, val = done_tok[j]
            if want.get(key, 0) < val:
                want[key] = val
        for key, val in want.items():
            e.wait_ge(semh(key), val)
        self.stats = dict(n_ops=n, n_wait=n_wait, cnt=dict(cnt))
        return self.stats


def fv(ap, off, dims):
    return bass.AP(ap.tensor, ap.offset + off, [list(ap.ap[0])] + [list(d) for d in dims])


def build_program(n_layers=DEPTH, debug=False, stop_after=None):
    nc = bass.Bass("TRN2", target_bir_lowering=False)
    P = Prog(nc)

    def din(name, shape, dt=F32):
        return nc.dram_tensor(name, list(shape), dt, kind="ExternalInput").ap()

    def dscr(name, shape, dt):
        return nc.dram_tensor(name, list(shape), dt, kind="Internal").ap()

    x_in = din("x", [S, D])
    emb_g = din("emb_ln_g", [D]); emb_b = din("emb_ln_b", [D])
    w_in = din("w_in", [DEPTH, D, 2048])
    w_fnet = din("w_fnet", [DEPTH, 256, 256])
    qg = din("q_norm_g", [DEPTH, 64]); kg = din("k_norm_g", [DEPTH, 64])
    conv_dw = din("conv_dw", [DEPTH, 31, 256]); conv_b = din("conv_b", [DEPTH, 256])
    cln_g = din("conv_ln_g", [DEPTH, 256]); cln_b = din("conv_ln_b", [DEPTH, 256])
    w_pw = din("w_conv_out", [DEPTH, 256, 256])
    w_out = din("w_out", [DEPTH, D, D])
    ln1_g = din("ln1_g", [DEPTH, D]); ln1_b = din("ln1_b", [DEPTH, D])
    w_ff1 = din("w_ff1", [DEPTH, D, DFF]); w_ff2 = din("w_ff2", [DEPTH, DFF, D])
    ln2_g = din("ln2_g", [DEPTH, D]); ln2_b = din("ln2_b", [DEPTH, D])
    dbias = din("dbias", [4, 3, 2, 128, 128])
    c_ident = din("c_ident", [128, 128])
    c_rope = din("c_rope", [S, 64])
    c_bcs = din("c_bcs", [256, 512])
    c_r = din("c_r", [128, 128])
    c_t3 = din("c_t3", [128, 4096])
    c_dmask = din("c_dmask", [2, 128, 128])
    out = nc.dram_tensor("out", [S, D], F32, kind="ExternalOutput").ap()

    X = dscr("X", [S, D], F32)
    XT = dscr("XT", [8, 128, S], BF16)
    YT = dscr("YT", [8, 128, S], BF16)
    VD = dscr("VD", [S, 128], BF16)
    B8d = dscr("B8d", [24, 128, 128], BF16)
    if debug:
        dbg_yt = nc.dram_tensor("dbg_yt", [8, 128, S], BF16, kind="ExternalOutput").ap()
        dbg_x1 = nc.dram_tensor("dbg_x1", [S, D], F32, kind="ExternalOutput").ap()

    def sb(name, shape, dt):
        return nc.alloc_sbuf_tensor(name, list(shape), dt)

    PS = nc.alloc_psum_tensor("PS", [128, 4096], F32)

    def bank(k, w=512):
        return PS[:, k * 512:k * 512 + w]

    def pk(k, nb=1):
        return [("ps", k + i) for i in range(nb)]

    ident = sb("ident", [128, 128], BF16)
    identf = sb("identf", [128, 128], F32)
    onesF = sb("onesF", [128, 128], F32)
    ones64 = sb("ones64", [128, 64], BF16)
    mh = sb("mh", [128, 8], F32)
    xTc = [sb(f"xTc{i}", [128, 8, 512], BF16) for i in range(2)]
    WA = sb("WA", [128, 8, 1024], BF16)
    big1 = sb("big1", [128, 16384], F32)
    big2 = sb("big2", [128, 16384], F32)
    lng = sb("lng", [128, 1024], F32); lnb = sb("lnb", [128, 1024], F32)
    rt = [sb(f"rt{i}", [128, 1024], F32) for i in range(2)]
    xt_in = [sb(f"xin{i}", [128, 1024], F32) for i in range(2)]
    xnb = [sb(f"xnb{i}", [128, 1024], BF16) for i in range(2)]
    xTs = [sb("xTs0", [128, 8, 256], BF16)] * 2
    st6 = sb("st6", [128, 2, 2, 6], F32)
    mv2 = sb("mv2", [128, 2, 2], F32)
    rstd2 = sb("rstd2", [128, 2], F32)
    small = sb("small", [128, 2048], F32)
    bar_t = sb("bar_t", [128, 8], F32)

    def mm(o, l, r, st, sp, R, W):
        P.op("pe", lambda: nc.tensor.matmul(o, lhsT=l, rhs=r, start=st, stop=sp), R, W)

    def tr(o, i, idn, R, W):
        P.op("pe", lambda: nc.tensor.transpose(o, i, idn), R, W)

    def act(o, i, f, R, W, scale=None, bias=None):
        kw = {}
        if scale is not None:
            kw["scale"] = scale
        if bias is not None:
            kw["bias"] = bias
        P.op("act", lambda: nc.scalar.activation(out=o, in_=i, func=f, **kw), R, W)

    def cp(eng, o, i, R, W):
        e = P.eng_obj[eng]
        if eng == "act":
            P.op("act", lambda: nc.scalar.copy(out=o, in_=i), R, W)
        else:
            P.op(eng, lambda: e.tensor_copy(out=o, in_=i), R, W)

    def tt(eng, o, a, b, op, R, W):
        e = P.eng_obj[eng]
        P.op(eng, lambda: e.tensor_tensor(out=o, in0=a, in1=b, op=op), R, W)

    def ts(eng, o, a, s1, s2, op0, op1, R, W):
        e = P.eng_obj[eng]
        if op1 is None:
            P.op(eng, lambda: e.tensor_scalar(out=o, in0=a, scalar1=s1, scalar2=None, op0=op0), R, W)
        else:
            P.op(eng, lambda: e.tensor_scalar(out=o, in0=a, scalar1=s1, scalar2=s2, op0=op0, op1=op1), R, W)

    def stt(o, a, s, b, op0, op1, R, W):
        P.op("dve", lambda: nc.vector.scalar_tensor_tensor(out=o, in0=a, scalar=s, in1=b, op0=op0, op1=op1), R, W)

    def memset(eng, o, v, W):
        e = P.eng_obj[eng]
        P.op(eng, lambda: e.memset(o, v), [], W)

    def wload(dst, src, R, W):
        P.dma(dst, src, R, W, q="pool")

    wload(ident[:], c_ident, [], ["ident"])
    P.dma(identf[:], c_ident, [], ["identf"])
    memset("pool", onesF[:], 1.0 / 256.0, ["onesF"])
    memset("pool", ones64[:], 1.0, ["ones64"])
    memset("pool", mh[:], -0.5, ["mh"])
    for h in range(4):
        for di in range(3):
            for ab in range(2):
                k = (h * 3 + di) * 2 + ab
                P.dma(small[:, 0:128], dbias[h, di, ab], [], ["small"])
                P.dma(small[:, 128:256], c_dmask[ab], [], ["small"])
                b8s = fv(small[:].bitcast(BF16), 1024, [[1, 128]])
                stt(small[:, 256:384], small[:, 0:128], 8.0, small[:, 128:256], ALU.mult, ALU.add, ["small"], ["b8f"])
                act(b8s, small[:, 256:384], AF.Exp, ["b8f"], ["b8s"], scale=0.125)
                P.dma(B8d[k], b8s, ["b8s"], ["B8d"])

    def ln_pair(r_aps, rkeys, xn_aps, xnkeys, rstd_on_pool=False, part="ab"):
        for k in range(2 if "a" in part else 0):
            r_ap = r_aps[k]
            P.op("dve", lambda r_ap=r_ap, k=k: nc.vector.bn_stats(out=st6[:, k, 0, :], in_=r_ap[:, 0:512]), [rkeys[k]], [f"st6_{k}"])
            P.op("dve", lambda r_ap=r_ap, k=k: nc.vector.bn_stats(out=st6[:, k, 1, :], in_=r_ap[:, 512:1024]), [rkeys[k]], [f"st6_{k}"])
            P.op("dve", lambda k=k: nc.vector.bn_aggr(out=mv2[:, k, :], in_=st6[:, k].rearrange("p a b -> p (a b)")), [f"st6_{k}"], ["mv2"])
        if "b" not in part:
            return
        if rstd_on_pool:
            ts("pool", rstd2[:], fv(mv2[:], 1, [[2, 2]]), LN_EPS, None, ALU.add, None, ["mv2"], ["rstd2"])
            tt("pool", rstd2[:], rstd2[:], mh[:, 0:2], ALU.pow, ["rstd2", "mh"], ["rstd2"])
        else:
            ts("dve", rstd2[:], fv(mv2[:], 1, [[2, 2]]), LN_EPS, None, ALU.add, None, ["mv2"], ["rstd2"])
            act(rstd2[:], rstd2[:], AF.Sqrt, ["rstd2"], ["rstd2"])
            P.op("dve", lambda: nc.vector.reciprocal(out=rstd2[:], in_=rstd2[:]), ["rstd2"], ["rstd2"])
        for k in range(2):
            ts("dve", xn_aps[k], r_aps[k], mv2[:, k, 0:1], rstd2[:, k:k + 1], ALU.subtract, ALU.mult, [rkeys[k], "mv2", "rstd2"], [xnkeys[k]])
            tt("dve", xn_aps[k], xn_aps[k], lng[:], ALU.mult, [xnkeys[k], "lng"], [xnkeys[k]])
            tt("dve", xn_aps[k], xn_aps[k], lnb[:], ALU.add, [xnkeys[k], "lng"], [xnkeys[k]])

    def transpose_tile(xn_ap, xnkey, dst_ap, dstkey, i):
        cp("act", xnb[i][:], xn_ap, [xnkey], [f"xnb{i}"])
        pst = bank(7).bitcast(BF16)
        for c in range(8):
            tr(pst[:, c * 128:(c + 1) * 128], xnb[i][:, c * 128:(c + 1) * 128], ident[:], [f"xnb{i}", "ident"], pk(7))
        cp("dve", dst_ap, pst.rearrange("p (c t) -> p c t", c=8), pk(7), [dstkey])

    def load_ln_params(g_ap, b_ap):
        P.dma(lng[:], g_ap.partition_broadcast(128), [], ["lng"])
        P.dma(lnb[:], b_ap.partition_broadcast(128), [], ["lng"])

    XTv = XT.rearrange("c p s -> p c s")
    YTv = YT.rearrange("c p s -> p c s")

    def ln_store(xn_tiles, t0, final=False, write_xt=True):
        pass

    load_ln_params(emb_g, emb_b)
    for tp in range(NT // 2):
        for k in range(2):
            t = 2 * tp + k
            P.dma(xt_in[k][:], x_in[t * 128:(t + 1) * 128, :], [], [f"xin{k}"])
        ln_pair([xt_in[0][:], xt_in[1][:]], ["xin0", "xin1"], [rt[0][:], rt[1][:]], ["rt0", "rt1"])
        for k in range(2):
            t = 2 * tp + k
            P.dma(X[t * 128:(t + 1) * 128, :], rt[k][:], [f"rt{k}"], [("X", t)], q="pool")
            transpose_tile(rt[k][:], f"rt{k}", xTs[0][:, :, k * 128:(k + 1) * 128], "xTs0", k)
        P.dma(XTv[:, :, tp * 256:(tp + 1) * 256], xTs[0][:], ["xTs0"], [("XT", tp // 2)], q="pool")

    P.barrier(lambda: nc.gpsimd.memset(bar_t[:], 0.0))

    def load_xT(j, buf):
        P.dma(xTc[buf][:], XTv[:, :, j * 512:(j + 1) * 512], [("XT", j)], [f"xTc{buf}"])

    b1 = big1[:].bitcast(BF16)
    b2 = big2[:].bitcast(BF16)
    smb = small[:].bitcast(BF16)

    order = ["p0", "A", "B0", "B1", "B2", "B", "C", "D", "O", "F"]
    def run_phase(ph):
        return stop_after is None or order.index(ph) <= order.index(stop_after)

    for l in range(n_layers):
        win_v = w_in[l].rearrange("(c p) n -> p c n", p=128)

        P.skip = not run_phase("A")
        wload(WA[:, :, 0:256], win_v[:, :, 0:256], [], ["WA"])
        BCS = fv(smb, 0, [[512, 2], [1, 512]])
        Wf = fv(smb, 1024, [[256, 2], [1, 256]])
        Wcs = fv(smb, 1536, [[512, 2], [1, 512]])
        Rr = fv(smb, 2560, [[1, 128]])
        wload(BCS, c_bcs.rearrange("(c p) n -> p c n", p=128), [], ["small"])
        wload(Wf, w_fnet[l].rearrange("(c p) n -> p c n", p=128), [], ["small"])
        wload(Rr, c_r, [], ["small"])
        T3 = fv(b2, 16384, [[1, 4096]])
        wload(T3, c_t3, [], ["big2"])
        for m in range(2):
            mm(bank(m)[:, 0:256], BCS[:, m, m * 128:(m + 1) * 128], Wf[:, m, :], True, True, ["small"], pk(m))
            mm(bank(m)[:, 256:512], BCS[:, m, 256 + m * 128:256 + (m + 1) * 128], Wf[:, m, :], True, True, ["small"], pk(m))
            cp("dve", Wcs[:, m, :], bank(m), pk(m), ["small"])
        ufT = fv(b1, 0, [[4096, 2], [1, 4096]])
        load_xT(0, 0)
        for j in range(8):
            if j + 1 < 8:
                load_xT(j + 1, (j + 1) % 2)
            xc = xTc[j % 2]
            for m in range(2):
                for c in range(8):
                    mm(bank(m), WA[:, c, m * 128:(m + 1) * 128], xc[:, c, :], c == 0, c == 7, ["WA", f"xTc{j % 2}"], pk(m))
                cp("act" if m == 0 else "dve", ufT[:, m, j * 512:(j + 1) * 512], bank(m), pk(m), [("ufT", j)])
        sk = P.skip
        P.skip = not run_phase("B0")
        wload(WA[:, :, 0:512], win_v[:, :, 256:768], [], ["WA"])
        P.skip = sk
        A_sb = fv(b1, 8192, [[256, 64], [1, 256]])
        ufkeys = [("ufT", j) for j in range(8)]
        for s1 in range(64):
            pb = s1 % 2
            for c in range(2):
                l_ap = fv(b1, c * 4096 + s1, [[64, 64]])
                mm(bank(pb)[0:64, 0:256], l_ap, Wcs[:, c, 0:256], c == 0, c == 1, ufkeys + ["small"], pk(pb))
            for c in range(2):
                l_ap = fv(b1, c * 4096 + s1, [[64, 64]])
                mm(bank(pb)[64:128, 0:256], l_ap, Wcs[:, c, 256:512], c == 0, c == 1, ufkeys + ["small"], pk(pb))
            cp("act" if pb == 0 else "dve", A_sb[:, s1, :], bank(pb)[:, 0:256], pk(pb), ["A_sb"])
        Y_sb = fv(b2, 0, [[64, 256], [1, 64]])
        for cb in range(32):
            pb = cb % 2
            for cc in range(8):
                ch = cb * 8 + cc
                l_ap = fv(b1, 8192 + ch, [[256, 64]])
                mm(bank(pb)[0:64, cc * 64:(cc + 1) * 64], l_ap, Rr[:, 0:64], True, True, ["A_sb", "small"], pk(pb))
                mm(bank(pb)[64:128, cc * 64:(cc + 1) * 64], l_ap, Rr[:, 64:128], True, True, ["A_sb", "small"], pk(pb))
            cp("act" if pb == 0 else "dve", fv(b2, cb * 512, [[1, 512]]), bank(pb), pk(pb), ["Y_sb"])
        yA = fv(b1, 24576, [[4096, 2], [1, 4096]])
        for m in range(2):
            for kb in range(8):
                pb = kb % 2
                for ks in range(8):
                    k2 = kb * 8 + ks
                    l_ap = fv(b2, m * 128 * 64 + k2, [[64, 128]])
                    o_ap = fv(bank(pb), ks, [[8, 64]])
                    mm(o_ap, l_ap, T3[:, k2 * 64:(k2 + 1) * 64], True, True, ["Y_sb", "big2"], pk(pb))
                o_sb = fv(b1, 24576 + m * 4096 + kb * 8, [[64, 64], [1, 8]])
                i_ps = fv(bank(pb), 0, [[8, 64], [1, 8]])
                cp("act" if pb == 0 else "dve", o_sb, i_ps, pk(pb), [("yA", m)])
            P.dma(YTv[:, m, :], yA[:, m, :], [("yA", m)], [("YT", m)], q="pool")

        P.barrier(lambda: nc.gpsimd.memset(bar_t[:], 0.0))
        P.skip = not run_phase("B0")
        QKT = fv(b1, 0, [[4096, 3], [1, 4096]])
        Vaug = fv(b1, 12288, [[256, 32], [128, 2], [1, 128]])
        rope = fv(big2[:], 0, [[64, 32], [1, 64]])
        P.dma(rope, c_rope.rearrange("(t p) f -> p t f", p=128), [], ["rope"])
        for hh in range(4):
            P.dma(fv(big2[:], 2048 + hh * 64, [[1, 64]]), qg[l].partition_broadcast(128), [], ["gqk"])
        for hh in range(2):
            P.dma(fv(big2[:], 2048 + 256 + hh * 64, [[1, 64]]), kg[l].partition_broadcast(128), [], ["gqk"])
        memset("dve", fv(b1, 12288 + 64, [[128, 64], [1, 64]]), 1.0, ["Vaug_ones"])
        QKo, TMPo, SSo, RSo, QBo = 2560, 8704, 14848, 14944, 20480
        P.skip = not run_phase("B1")
        load_xT(0, 0)
        for half in range(2):
            for k in range(16):
                t = half * 16 + k
                j = t // 4
                if t % 4 == 0 and j + 1 < 8:
                    load_xT(j + 1, (j + 1) % 2)
                xc = xTc[j % 2]
                tl = (t % 4) * 128
                pb = t % 2
                for c in range(8):
                    mm(bank(pb), xc[:, c, tl:tl + 128], WA[:, c, 0:512], c == 0, c == 7, ["WA", f"xTc{j % 2}"], pk(pb))
                cp("act", fv(big2[:], QKo + k * 384, [[1, 384]]), bank(pb)[:, 0:384], pk(pb), ["QKh"])
                cp("dve", fv(b1, 12288 + t * 256, [[128, 2], [1, 64]]), fv(bank(pb), 384, [[64, 2], [1, 64]]), pk(pb), [("Vaug", t)])
            QKf = fv(big2[:], QKo, [[1, 6144]])
            TMPf = fv(big2[:], TMPo, [[1, 6144]])
            tt("dve", TMPf, QKf, QKf, ALU.mult, ["QKh"], ["TMP"])
            P.op("dve", lambda: nc.vector.tensor_reduce(out=fv(big2[:], SSo, [[1, 96]]), in_=fv(big2[:], TMPo, [[64, 96], [1, 64]]),
                                                        axis=AX.X, op=ALU.add), ["TMP"], ["ss"])
            ts("pool", fv(big2[:], RSo, [[1, 96]]), fv(big2[:], SSo, [[1, 96]]), 1.0 / 64.0, RMS_EPS, ALU.mult, ALU.add, ["ss"], ["rs"])
            tt("pool", fv(big2[:], RSo, [[1, 96]]), fv(big2[:], RSo, [[1, 96]]), fv(mh[:], 0, [[0, 96]]), ALU.pow, ["rs", "mh"], ["rs"])
            tt("dve", fv(big2[:], QKo, [[64, 96], [1, 64]]), fv(big2[:], QKo, [[64, 96], [1, 64]]), fv(big2[:], RSo, [[1, 96], [0, 64]]),
               ALU.mult, ["QKh", "rs"], ["QKh"])
            tt("dve", fv(big2[:], QKo, [[384, 16], [1, 384]]), fv(big2[:], QKo, [[384, 16], [1, 384]]), fv(big2[:], 2048, [[0, 16], [1, 384]]),
               ALU.mult, ["QKh", "gqk"], ["QKh"])
            for h6 in range(6):
                eng = "dve" if h6 < 3 else "pool"
                x1 = fv(big2[:], QKo + h6 * 64, [[384, 16], [32, 2], [1, 16]])
                x2 = fv(big2[:], QKo + h6 * 64 + 16, [[384, 16], [32, 2], [1, 16]])
                cosb = fv(big2[:], half * 16 * 64, [[64, 16], [16, 2], [1, 16]])
                sinb = fv(big2[:], half * 16 * 64 + 32, [[64, 16], [16, 2], [1, 16]])
                t1 = fv(big2[:], TMPo + h6 * 1024, [[32, 16], [16, 2], [1, 16]])
                t2 = fv(big2[:], TMPo + h6 * 1024 + 512, [[32, 16], [16, 2], [1, 16]])
                slot = [0, 2, 1, 3, 4, 5][h6]
                o1 = fv(b1, QBo + slot * 64, [[384, 16], [32, 2], [1, 16]])
                o2 = fv(b1, QBo + slot * 64 + 16, [[384, 16], [32, 2], [1, 16]])
                tt(eng, t1, x1, cosb, ALU.mult, ["QKh", "rope", "TMP"], [("t1", h6)])
                tt(eng, t2, x2, sinb, ALU.mult, ["QKh", "rope", "TMP"], [("t2", h6)])
                tt(eng, o1, t1, t2, ALU.subtract, [("t1", h6), ("t2", h6)], [("qb", h6)])
                tt(eng, t1, x1, sinb, ALU.mult, ["QKh", "rope", ("qb", h6)], [("t1", h6)])
                tt(eng, t2, x2, cosb, ALU.mult, ["QKh", "rope", ("qb", h6)], [("t2", h6)])
                tt(eng, o2, t1, t2, ALU.add, [("t1", h6), ("t2", h6), "QKh"], [("qb", h6)])
            qbk = [("qb", h6) for h6 in range(6)]
            for k in range(16):
                t = half * 16 + k
                pst = bank(6 + k % 2).bitcast(BF16)
                for k3 in range(3):
                    tr(pst[:, k3 * 128:(k3 + 1) * 128], fv(b1, QBo + k * 384 + k3 * 128, [[1, 128]]), ident[:], qbk + ["ident"], pk(6 + k % 2))
                cp("act" if k % 2 == 0 else "dve", fv(b1, t * 128, [[4096, 3], [1, 128]]), fv(pst, 0, [[128, 3], [1, 128]]), pk(6 + k % 2), [("QKT", t)])
        sk = P.skip
        P.skip = not run_phase("C")
        wload(WA[:, :, 0:512], win_v[:, :, 768:1280], [], ["WA"])
        P.skip = sk
        P.skip = not run_phase("B")
        qkt_keys = [("QKT", t) for t in range(NT)]
        yB = fv(b2, 16384, [[4096, 2], [1, 4096]])
        PT = [fv(smb, i * 1024, [[1, 1024]]) for i in range(2)]
        rec = fv(big2[:], 12288, [[1, 2048]])
        osb = fv(big2[:], 14336, [[1, 2048]])
        iters = [(qc, kt) for qc in range(8) for kt in range(NT)]

        def b_S(it, hb):
            qc, kt = iters[it]
            for g in range(2):
                pr = slice(g * 64, (g + 1) * 64)
                mm(bank(hb * 2 + g), QKT[pr, 2, kt * 128:(kt + 1) * 128], QKT[pr, hb, qc * 512:(qc + 1) * 512], True, True,
                   qkt_keys, pk(hb * 2 + g))

        def b_E(it, hb):
            act(PT[hb], PS[:, hb * 1024:hb * 1024 + 1024], AF.Exp, pk(hb * 2, 2), [f"PT{hb}"], scale=0.125)

        def b_P(it, hb):
            qc, kt = iters[it]
            for g in range(2):
                mm(bank(4 + hb * 2 + g), Vaug[:, kt, g, :], PT[hb][:, g * 512:(g + 1) * 512], kt == 0, kt == NT - 1,
                   [f"PT{hb}", ("Vaug", kt), "Vaug_ones"], pk(4 + hb * 2 + g))
            if kt == NT - 1 and hb == 1:
                for hf in range(2):
                    cp("dve", rec[0:64, hf * 1024:(hf + 1) * 1024], PS[64:128, 2048 + hf * 1024:2048 + (hf + 1) * 1024], pk(4 + 2 * hf, 2), ["rec"])
                    cp("act", osb[0:64, hf * 1024:(hf + 1) * 1024], PS[0:64, 2048 + hf * 1024:2048 + (hf + 1) * 1024], pk(4 + 2 * hf, 2), ["osb"])
                P.op("dve", lambda: nc.vector.reciprocal(out=rec[0:64, :], in_=rec[0:64, :]), ["rec"], ["rec"])
                for hb2 in range(2):
                    for g in range(2):
                        bk = hb2 * 2 + g
                        tt("dve", yB[hb2 * 64:(hb2 + 1) * 64, g, qc * 512:(qc + 1) * 512], osb[0:64, bk * 512:(bk + 1) * 512],
                           rec[0:64, bk * 512:(bk + 1) * 512], ALU.mult, ["osb", "rec"], [("yB", g)])
                if qc == 7:
                    for g in range(2):
                        P.dma(YTv[:, 2 + g, :], yB[:, g, :], [("yB", g)], [("YT", 2 + g)], q="pool")

        b_S(0, 0)
        b_S(0, 1)
        for it in range(len(iters)):
            for hb in range(2):
                b_E(it, hb)
                b_P(it, hb)
                if it + 1 < len(iters):
                    b_S(it + 1, hb)

        P.barrier(lambda: nc.gpsimd.memset(bar_t[:], 0.0))
        P.skip = not run_phase("C")
        HW_ = 4128
        hbuf = fv(big1[:], 0, [[HW_, 2], [1, HW_]])
        accC = fv(big2[:], 0, [[4096, 2], [1, 4096]])
        dwT = fv(big2[:], 8192, [[31, 2], [1, 31]])
        cb_sb = fv(big2[:], 8256, [[1, 2]])
        cg_sb = fv(big2[:], 8258, [[1, 2]])
        cbe_sb = fv(big2[:], 8260, [[1, 2]])
        Wpw = fv(b2, 16640, [[256, 2], [1, 256]])
        dwraw = fv(big2[:], 8704, [[1, 256]])
        P.dma(dwraw[0:31, :], conv_dw[l], [], ["dwraw"])
        for m in range(2):
            tr(bank(6)[:, m * 32:m * 32 + 31], dwraw[0:31, m * 128:(m + 1) * 128], identf[0:31, 0:31], ["dwraw", "identf"], pk(6))
        cp("dve", dwT, fv(bank(6), 0, [[32, 2], [1, 31]]), pk(6), ["cpar"])
        P.dma(cb_sb, conv_b[l].rearrange("(m p) -> p m", p=128), [], ["cpar"], allow_slow_non_contiguous=True)
        P.dma(cg_sb, cln_g[l].rearrange("(m p) -> p m", p=128), [], ["cpar"], allow_slow_non_contiguous=True)
        P.dma(cbe_sb, cln_b[l].rearrange("(m p) -> p m", p=128), [], ["cpar"], allow_slow_non_contiguous=True)
        wload(Wpw, w_pw[l].rearrange("(c p) n -> p c n", p=128), [], ["Wpw"])
        for m in range(2):
            memset("pool", fv(big1[:], m * HW_, [[1, 16]]), 0.0, [("hbuf", "padl")])
            memset("pool", fv(big1[:], m * HW_ + 16 + 4096, [[1, 16]]), 0.0, [("hbuf", "padr")])
        sig = [fv(big1[:], 8256 + i * 512, [[1, 512]]) for i in range(2)]
        HB = 26752
        DG = 18432
        memset("pool", fv(b1, HB, [[1, 16]]), 0.0, [("hb16", "padl")])
        memset("pool", fv(b1, HB + 16 + 4096, [[1, 16]]), 0.0, [("hb16", "padr")])
        for k in range(31):
            ts("pool", fv(b2, DG + k * 128, [[1, 128]]), ident[:], dwT[:, 1, k:k + 1], None, ALU.mult, None, ["ident", "cpar"], ["diag"])
        load_xT(0, 0)
        for j in range(8):
            if j + 1 < 8:
                load_xT(j + 1, (j + 1) % 2)
            xc = xTc[j % 2]
            for m in range(4):
                for c in range(8):
                    mm(bank(m), WA[:, c, m * 128:(m + 1) * 128], xc[:, c, :], c == 0, c == 7, ["WA", f"xTc{j % 2}"], pk(m))
            for m in range(2):
                act(sig[m], bank(2 + m), AF.Sigmoid, pk(2 + m), [f"sig{m}"])
            tt("dve", hbuf[:, 0, 16 + j * 512:16 + (j + 1) * 512], bank(0), sig[0], ALU.mult, pk(0) + ["sig0"], [("hbuf", j)])
            tt("dve", fv(b1, HB + 16 + j * 512, [[1, 512]]), bank(1), sig[1], ALU.mult, pk(1) + ["sig1"], [("hb16", j)])
        sk = P.skip
        P.skip = not run_phase("D")
        wload(WA[:, :, 0:768], win_v[:, :, 1280:2048], [], ["WA"])
        P.skip = sk
        hkeys_all = [("hbuf", j) for j in range(8)] + [("hbuf", "padl"), ("hbuf", "padr")]
        h16keys = [("hb16", j) for j in range(8)] + [("hb16", "padl"), ("hb16", "padr")]
        for pc in range(4):
            o = accC[:, 0, pc * 1024:(pc + 1) * 1024]
            for k in range(31):
                src = hbuf[:, 0, 16 + pc * 1024 + k - 15:16 + pc * 1024 + k - 15 + 1024]
                if k == 0:
                    ts("dve", o, src, dwT[:, 0, 0:1], cb_sb[:, 0:1], ALU.mult, ALU.add, hkeys_all + ["cpar"], [("accC", 0, pc)])
                else:
                    stt(o, src, dwT[:, 0, k:k + 1], o, ALU.mult, ALU.add, hkeys_all + ["cpar"], [("accC", 0, pc)])
            for jj in range(2):
                j = pc * 2 + jj
                pb = 4 + j % 2
                for k in range(31):
                    mm(bank(pb), fv(b2, DG + k * 128, [[1, 128]]), fv(b1, HB + 16 + j * 512 + k - 15, [[1, 512]]), k == 0, k == 30,
                       h16keys + ["diag"], pk(pb))
                act(accC[:, 1, j * 512:(j + 1) * 512], bank(pb), AF.Identity, pk(pb) + ["cpar"], [("accC", 1, pc)], bias=cb_sb[:, 1:2])
        sqb = [fv(big1[:], 9280 + i * 512, [[1, 512]]) for i in range(2)]
        mean_sb = fv(big1[:], 10304, [[1, 512]])
        var_sb = fv(big1[:], 10816, [[1, 512]])
        xh = [fv(big1[:], 11328 + i * 512, [[1, 512]]) for i in range(2)]
        hact = fv(b1, 2 * 12352, [[512, 2], [1, 512]])
        yC = fv(b1, 2 * 12864, [[512, 2], [1, 512]])
        for j in range(8):
            cs = slice(j * 512, (j + 1) * 512)
            akeys = [("accC", m, j // 2) for m in range(2)]
            for m in range(2):
                act(sqb[m], accC[:, m, cs], AF.Square, [("accC", m, j // 2)], [f"sq{m}"])
            for m in range(2):
                mm(bank(0), onesF[:], accC[:, m, cs], m == 0, m == 1, akeys + ["onesF"], pk(0))
            for m in range(2):
                mm(bank(1), onesF[:], sqb[m], m == 0, m == 1, [f"sq{m}", "onesF"], pk(1))
            cp("act", mean_sb, bank(0), pk(0), ["mean"])
            tt("dve", var_sb, mean_sb, mean_sb, ALU.mult, ["mean"], ["var"])
            tt("dve", var_sb, bank(1), var_sb, ALU.subtract, pk(1) + ["var"], ["var"])
            ts("dve", var_sb, var_sb, LN_EPS, None, ALU.add, None, ["var"], ["var"])
            act(var_sb, var_sb, AF.Sqrt, ["var"], ["var"])
            P.op("dve", lambda: nc.vector.reciprocal(out=var_sb, in_=var_sb), ["var"], ["var"])
            for m in range(2):
                tt("dve", xh[m], accC[:, m, cs], mean_sb, ALU.subtract, [("accC", m, j // 2), "mean"], [f"xh{m}"])
                tt("dve", xh[m], xh[m], var_sb, ALU.mult, [f"xh{m}", "var"], [f"xh{m}"])
                act(hact[:, m, :], xh[m], AF.Silu, [f"xh{m}", "cpar"], ["hact"], scale=cg_sb[:, m:m + 1], bias=cbe_sb[:, m:m + 1])
            for mo in range(2):
                for c in range(2):
                    mm(bank(2 + mo), Wpw[:, c, mo * 128:(mo + 1) * 128], hact[:, c, :], c == 0, c == 1, ["hact", "Wpw"], pk(2 + mo))
                cp("dve", yC[:, mo, :], bank(2 + mo), pk(2 + mo), ["yC"])
            P.dma(YTv[:, 4:6, cs], yC, ["yC"], [("YT", 4), ("YT", 5)], q="pool")

        P.barrier(lambda: nc.gpsimd.memset(bar_t[:], 0.0))
        P.skip = not run_phase("D")
        for hp in range(2):
            if hp == 1:
                P.barrier(lambda: nc.gpsimd.memset(bar_t[:], 0.0))
            QTd = fv(b1, 0, [[1, 4096]])
            KTo = {1: 4096, 4: 8192, 16: 12288}
            VTo = {1: 16384, 4: 20480, 16: 24576}
            accD = fv(big2[:], 0, [[4096, 2], [1, 4096]])
            B8 = fv(b2, 24576, [[128, 24], [1, 128]])
            P.dma(B8, B8d.rearrange("k p q -> p k q"), ["B8d"], ["B8"])
            recD = fv(big2[:], 8192, [[1, 4096]])
            yD = fv(b1, 28672, [[1, 4096]])
            load_xT(0, 0)
            for j in range(8):
                if j + 1 < 8:
                    load_xT(j + 1, (j + 1) % 2)
                xc = xTc[j % 2]
                for c in range(8):
                    mm(bank(0), WA[:, c, hp * 128:(hp + 1) * 128], xc[:, c, :], c == 0, c == 7, ["WA", f"xTc{j % 2}"], pk(0))
                for c in range(8):
                    mm(bank(1), WA[:, c, 256 + hp * 128:256 + (hp + 1) * 128], xc[:, c, :], c == 0, c == 7, ["WA", f"xTc{j % 2}"], pk(1))
                cp("act", QTd[:, j * 512:(j + 1) * 512], bank(0), pk(0), [("QTd", j)])
                cp("dve", fv(b1, 4096 + j * 512, [[1, 512]]), bank(1), pk(1), [("KTd", j)])
                cp("act", fv(b1, 8192 + j * 128, [[1024, 4], [1, 128]]), fv(bank(1), 0, [[1, 4], [4, 128]]), pk(1), [("KTd", j)])
                cp("dve", fv(b1, 12288 + j * 32, [[256, 16], [1, 32]]), fv(bank(1), 0, [[1, 16], [16, 32]]), pk(1), [("KTd", j)])
                for tq in range(4):
                    t = j * 4 + tq
                    for c in range(8):
                        mm(bank(2 + tq % 2)[:, 0:128], xc[:, c, tq * 128:(tq + 1) * 128], WA[:, c, 512 + hp * 128:512 + (hp + 1) * 128], c == 0, c == 7,
                           ["WA", f"xTc{j % 2}"], pk(2 + tq % 2))
                    cp("act" if tq % 2 == 0 else "dve", fv(b1, 16384 + t * 128, [[1, 128]]), bank(2 + tq % 2)[:, 0:128], pk(2 + tq % 2), [("Vnat", t)])
                    P.dma(VD[t * 128:(t + 1) * 128, :], fv(b1, 16384 + t * 128, [[1, 128]]), [("Vnat", t)], [("VD", t)], q="pool")
            if hp == 1:
                sk = P.skip
                P.skip = not run_phase("O")
                for hf in range(2):
                    wload(WA[:, :, hf * 512:(hf + 1) * 512], w_out[l].rearrange("(c p) n -> p c n", p=128)[:, :, hf * 512:(hf + 1) * 512], [], ["WA"])
                load_ln_params(ln1_g[l], ln1_b[l])
                P.skip = sk
            vdk = [("VD", t) for t in range(NT)]
            VDr = VD
            for r in range(4):
                src = bass.AP(VDr.tensor, VDr.offset + r * 128, [[4 * 128, 128], [4 * 128 * 128, 8], [1, 128]])
                P.dma(fv(b1, 20480 + r * 8 * 128, [[128, 8], [1, 128]]), src, vdk, [("VD4", r)])
            for r in range(16):
                src = bass.AP(VDr.tensor, VDr.offset + r * 128, [[16 * 128, 128], [16 * 128 * 128, 2], [1, 128]])
                P.dma(fv(b1, 24576 + r * 2 * 128, [[128, 2], [1, 128]]), src, vdk, [("VD16", r)])
            qk_keys = [("QTd", j) for j in range(8)] + [("KTd", j) for j in range(8)]
            PTd = [fv(smb, i * 512, [[1, 512]]) for i in range(3)]
            diters = []
            for di, d in enumerate(DILS):
                for r in range(d):
                    for i in range(-1, (S // d) // 128):
                        diters.append((di, d, r, i))

            def d_info(it):
                di, d, r, i = diters[it]
                Ld = S // d
                ntile = Ld // 128
                q0 = 64 if i == -1 else 0
                q1 = 64 if i == ntile - 1 else 128
                hasA = i >= 0
                hasB = i + 1 <= ntile - 1
                tok0 = r + d * (128 * i + 64 + q0)
                if d == 1:
                    vkeys = [("Vnat", t) for t in range(NT)]
                elif d == 4:
                    vkeys = [("VD4", r)]
                else:
                    vkeys = [("VD16", r)]
                blocks = [(ab, jt) for ab, has, jt in ((0, hasA, i), (1, hasB, i + 1)) if has]
                return di, d, r, i, Ld, ntile, q0, q1, tok0, vkeys, blocks

            def d_S(it):
                di, d, r, i, Ld, ntile, q0, q1, tok0, vkeys, blocks = d_info(it)
                nq = q1 - q0
                for hh in range(2):
                    sbk = (it % 3) * 2 + hh
                    pr = slice(hh * 64, (hh + 1) * 64)
                    rhs_q = fv(b1[pr, :], tok0, [[d, nq]])
                    for ab, jt in blocks:
                        reg = bank(sbk)[:, ab * 128 + q0:ab * 128 + q1]
                        kcol = KTo[d] + r * Ld + jt * 128
                        mm(reg, fv(b1[pr, :], kcol, [[1, 128]]), rhs_q, True, True, qk_keys, pk(sbk))

            def d_E(it):
                di, d, r, i, Ld, ntile, q0, q1, tok0, vkeys, blocks = d_info(it)
                ptb = PTd[it % 3]
                ebase = 24576 + (hp * 2 * 6 + di * 2) * 128
                full = len(blocks) == 2 and q0 == 0 and q1 == 128
                for hh in range(2):
                    sbk = (it % 3) * 2 + hh
                    if full:
                        act(ptb[:, hh * 256:(hh + 1) * 256], bank(sbk)[:, 0:256], AF.Exp, pk(sbk), [f"PTd{it % 3}"], scale=0.125)
                    else:
                        for ab, jt in blocks:
                            col = (hh * 2 + ab) * 128
                            act(ptb[:, col + q0:col + q1], bank(sbk)[:, ab * 128 + q0:ab * 128 + q1], AF.Exp, pk(sbk), [f"PTd{it % 3}"], scale=0.125)
                if full:
                    ptv = fv(smb, (it % 3) * 512, [[256, 2], [128, 2], [1, 128]])
                    tt("dve", ptv, ptv, fv(b2, ebase, [[768, 2], [128, 2], [1, 128]]), ALU.mult, [f"PTd{it % 3}", "B8"], [f"PTd{it % 3}"])
                else:
                    for hh in range(2):
                        for ab, jt in blocks:
                            col = (hh * 2 + ab) * 128
                            tt("dve", ptb[:, col + q0:col + q1], ptb[:, col + q0:col + q1],
                               fv(b2, ebase + hh * 768 + ab * 128 + q0, [[1, q1 - q0]]), ALU.mult, [f"PTd{it % 3}", "B8"], [f"PTd{it % 3}"])

            def d_P(it):
                di, d, r, i, Ld, ntile, q0, q1, tok0, vkeys, blocks = d_info(it)
                nq = q1 - q0
                obk = 6 + it % 2
                ptb = PTd[it % 3]
                for hh in range(2):
                    oreg = bank(obk)[:, hh * 128 + q0:hh * 128 + q1]
                    for bi, (ab, jt) in enumerate(blocks):
                        col = (hh * 2 + ab) * 128
                        vcol = VTo[d] + (r * ntile + jt) * 128 + hh * 64
                        mm(oreg[0:64, :], fv(b1, vcol, [[1, 64]]), ptb[:, col + q0:col + q1], bi == 0, bi == len(blocks) - 1,
                           [f"PTd{it % 3}"] + vkeys, pk(obk))
                    for bi, (ab, jt) in enumerate(blocks):
                        col = (hh * 2 + ab) * 128
                        mm(oreg[64:128, :], ones64[:], ptb[:, col + q0:col + q1], bi == 0, bi == len(blocks) - 1,
                           [f"PTd{it % 3}", "ones64"], pk(obk))
                for hh in range(2):
                    oreg = bank(obk)[:, hh * 128 + q0:hh * 128 + q1]
                    dst = fv(big2[:], hh * 4096 + tok0, [[d, nq]])
                    if di == 0:
                        cp("dve", dst, oreg, pk(obk), [("accD", hh)])
                    else:
                        tt("dve", dst, oreg, dst, ALU.add, pk(obk) + [("accD", hh)], [("accD", hh)])

            d_S(0)
            d_S(1)
            for it in range(len(diters)):
                if it + 2 < len(diters):
                    d_S(it + 2)
                d_E(it)
                d_P(it)
            for hh in range(2):
                cp("dve", recD[0:64, :], accD[64:128, hh, :], [("accD", hh)], ["recD"])
                act(recD[0:64, :], recD[0:64, :], AF.Ln, ["recD"], ["recD"])
                act(recD[0:64, :], recD[0:64, :], AF.Exp, ["recD"], ["recD"], scale=-1.0)
                tt("dve", yD[hh * 64:(hh + 1) * 64, :], accD[0:64, hh, :], recD[0:64, :], ALU.mult, [("accD", hh), "recD"], ["yD"])
            P.dma(YTv[:, 6 + hp, :], yD, ["yD"], [("YT", 6 + hp)], q="pool")

        P.barrier(lambda: nc.gpsimd.memset(bar_t[:], 0.0))

        P.skip = not run_phase("O")
        W1 = fv(b1, 0, [[4096, 8], [1, 4096]])
        W2 = fv(b2, 0, [[1024, 32], [1, 1024]])
        w1v = w_ff1[l].rearrange("(c p) n -> p c n", p=128)
        w2v = w_ff2[l].rearrange("(c p) n -> p c n", p=128)
        wq = []
        for c in range(8):
            for hf in range(2):
                wq.append((W1[:, c, hf * 2048:(hf + 1) * 2048], w1v[:, c, hf * 2048:(hf + 1) * 2048], "W1"))
        for c in range(32):
            wq.append((W2[:, c, :], w2v[:, c, :], "W2"))
        ytk = [("YT", k) for k in range(8)]

        def load_yT(j, buf):
            P.dma(xTc[buf][:], YTv[:, :, j * 512:(j + 1) * 512], ytk, [f"xTc{buf}"])

        def o_mm(tp):
            j = tp // 2
            if tp % 2 == 0 and j + 1 < 8:
                load_yT(j + 1, (j + 1) % 2)
            yc = xTc[j % 2]
            for k in range(2):
                t = 2 * tp + k
                tl = (t % 4) * 128
                P.dma(xt_in[k][:], X[t * 128:(t + 1) * 128, :], [("X", t)], [f"xin{k}"])
                pb = 2 * k
                for hf in range(2):
                    for c in range(8):
                        mm(bank(pb + hf), yc[:, c, tl:tl + 128], WA[:, c, hf * 512:(hf + 1) * 512], c == 0, c == 7, ["WA", f"xTc{j % 2}"], pk(pb + hf))

        def o_ln(tp):
            for k in range(2):
                pb = 2 * k
                stt(rt[k][:], xt_in[k][:], ALPHA, PS[:, pb * 512:pb * 512 + 1024], ALU.mult, ALU.add, [f"xin{k}"] + pk(pb, 2), [f"rt{k}"])
            ln_pair([rt[0][:], rt[1][:]], ["rt0", "rt1"], [rt[0][:], rt[1][:]], ["rt0", "rt1"])
            for k in range(2):
                t = 2 * tp + k
                P.dma(X[t * 128:(t + 1) * 128, :], rt[k][:], [f"rt{k}"], [("X", t)], q="pool")

        def o_tail(tp):
            for k in range(2):
                transpose_tile(rt[k][:], f"rt{k}", xTs[0][:, :, k * 128:(k + 1) * 128], "xTs0", k)
            P.dma(XTv[:, :, tp * 256:(tp + 1) * 256], xTs[0][:], ["xTs0"], [("XT1", tp)], q="pool")
            for _ in range(3):
                if wq:
                    wd, ws, wk = wq.pop(0)
                    wload(wd, ws, [], [wk])

        load_yT(0, 0)
        o_mm(0)
        for tp in range(NT // 2):
            o_ln(tp)
            if tp + 1 < NT // 2:
                o_mm(tp + 1)
            o_tail(tp)

        P.barrier(lambda: nc.gpsimd.memset(bar_t[:], 0.0))
        P.skip = not run_phase("F")
        load_ln_params(ln2_g[l], ln2_b[l])
        last = (l == n_layers - 1)
        hT = fv(WA[:], 0, [[256, 32], [1, 256]])

        def load_x1T(jc, buf):
            P.dma(xTc[buf][:, :, 0:256], XTv[:, :, jc * 256:(jc + 1) * 256], [("XT1", jc)], [f"xTc{buf}"])

        def f_ffn1(jc, h0, h1):
            if h0 == 0 and jc + 1 < 16:
                load_x1T(jc + 1, (jc + 1) % 2)
            xc = xTc[jc % 2]
            for hc in range(h0, h1):
                pb = hc % 3
                for c in range(8):
                    mm(bank(pb)[:, 0:256], W1[:, c, hc * 128:(hc + 1) * 128], xc[:, c, 0:256], c == 0, c == 7, ["W1", f"xTc{jc % 2}"], pk(pb))
                rl = fv(small[:], (hc % 2) * 256, [[1, 256]])
                act(rl, bank(pb)[:, 0:256], AF.Relu, pk(pb), [f"relu{hc % 2}"])
                tt("pool", hT[:, hc, :], rl, rl, ALU.mult, [f"relu{hc % 2}"], [("hT", hc)])

        def f_ffn2(jc):
            for k in range(2):
                t = jc * 2 + k
                P.dma(xt_in[k][:], X[t * 128:(t + 1) * 128, :], [("X", t)], [f"xin{k}"])
                pb = 3 + 2 * k
                for hf in range(2):
                    for hc in range(32):
                        mm(bank(pb + hf), hT[:, hc, k * 128:(k + 1) * 128], W2[:, hc, hf * 512:(hf + 1) * 512], hc == 0, hc == 31, [("hT", hc), "W2"], pk(pb + hf))

        def f_ln_a(jc):
            for k in range(2):
                pb = 3 + 2 * k
                stt(rt[k][:], xt_in[k][:], ALPHA, PS[:, pb * 512:pb * 512 + 1024], ALU.mult, ALU.add, [f"xin{k}"] + pk(pb, 2), [f"rt{k}"])
            ln_pair([rt[0][:], rt[1][:]], ["rt0", "rt1"], [rt[0][:], rt[1][:]], ["rt0", "rt1"], rstd_on_pool=True, part="a")

        def f_ln_b(jc):
            ln_pair([rt[0][:], rt[1][:]], ["rt0", "rt1"], [rt[0][:], rt[1][:]], ["rt0", "rt1"], rstd_on_pool=True, part="b")
            for k in range(2):
                t = jc * 2 + k
                if last:
                    P.dma(out[t * 128:(t + 1) * 128, :], rt[k][:], [f"rt{k}"], ["out"], q="sp")
                else:
                    P.dma(X[t * 128:(t + 1) * 128, :], rt[k][:], [f"rt{k}"], [("X", t)], q="sp")

        def f_tail(jc):
            if last:
                return
            for k in range(2):
                transpose_tile(rt[k][:], f"rt{k}", xTs[0][:, :, k * 128:(k + 1) * 128], "xTs0", k)
            P.dma(XTv[:, :, jc * 256:(jc + 1) * 256], xTs[0][:], ["xTs0"], [("XT", jc // 2)], q="sp")

        load_x1T(0, 0)
        for jc in range(16):
            f_ffn1(jc, 0, 8)
            if jc > 0:
                f_ln_a(jc - 1)
            f_ffn1(jc, 8, 16)
            if jc > 0:
                f_ln_b(jc - 1)
            f_ffn1(jc, 16, 32)
            if jc > 0:
                f_tail(jc - 1)
            f_ffn2(jc)
        f_ln_a(15)
        f_ln_b(15)
        f_tail(15)
        P.barrier(lambda: nc.gpsimd.memset(bar_t[:], 0.0))

    P.skip = False
    fk = ["out"]
    if stop_after is not None and stop_after != "F":
        P.barrier(lambda: nc.gpsimd.memset(bar_t[:], 0.0))
        P.dma(out, X, [], ["out"])
    if debug:
        P.barrier(lambda: nc.gpsimd.memset(bar_t[:], 0.0))
        P.dma(dbg_yt, YT, [], ["dbg_yt"])
        P.dma(dbg_x1, X, [], ["dbg_x1"])
        fk += ["dbg_yt", "dbg_x1"]
    stats = P.emit(final_wait_keys=fk)
    return nc, stats


def _t5_bucket_np(rel):
    nb = 16
    max_exact = 8
    ret = np.where(rel > 0, nb, 0)
    n = np.abs(rel)
    nf = np.maximum(n, 1).astype(np.float32)
    large = max_exact + (np.log(nf / np.float32(max_exact)) / np.float32(math.log(1024 / max_exact))
                         * np.float32(nb - max_exact)).astype(np.int32)
    large = np.minimum(large, nb - 1)
    return ret + np.where(n < max_exact, n, large)


def _constants():
    c = {}
    c["c_ident"] = np.eye(128, dtype=np.float32)
    nf = 16
    inv = (10000.0 ** (-np.arange(nf, dtype=np.float32) / nf)).astype(np.float32)
    t = np.arange(S)
    row = (t // 64).astype(np.float32)
    col = (t % 64).astype(np.float32)
    ang = np.concatenate([row[:, None] * inv, col[:, None] * inv], -1).astype(np.float32)
    c["c_rope"] = np.concatenate([np.cos(ang), np.sin(ang)], -1).astype(np.float32)
    k = np.arange(64)
    C64 = np.cos(2 * np.pi * np.outer(k, k) / 64)
    S64 = np.sin(2 * np.pi * np.outer(k, k) / 64)
    BC = np.kron(np.eye(4), C64) / 8
    BS = np.kron(np.eye(4), S64) / 8
    c["c_bcs"] = np.concatenate([BC, BS], 1).astype(np.float32)
    Rre = np.concatenate([C64, -S64], 0)
    Rim = np.concatenate([-S64, -C64], 0)
    c["c_r"] = np.concatenate([Rre, Rim], 1).astype(np.float32)
    s1 = np.arange(64)[:, None, None]
    k2 = np.arange(64)[None, :, None]
    k1 = np.arange(64)[None, None, :]
    th = 2 * np.pi * ((s1 * (64 * k1 + k2)) % 4096) / 4096
    T3 = np.concatenate([np.cos(th) / 64, np.sin(th) / 64], 0)
    c["c_t3"] = T3.reshape(128, 4096).astype(np.float32)
    p = np.arange(128)[:, None]
    q = np.arange(128)[None, :]
    mA = np.where((p - q >= 0) & (p - q <= 128), 0.0, MASKV)
    mB = np.where((p - q >= -128) & (p - q <= 0), 0.0, MASKV)
    c["c_dmask"] = np.stack([mA, mB]).astype(np.float32)
    return c


def _dbias(rel_bias):
    p = np.arange(128)[:, None]
    q = np.arange(128)[None, :]
    o = np.zeros((4, 3, 2, 128, 128), np.float32)
    for di, d in enumerate(DILS):
        for ab, off in ((0, -64), (1, 64)):
            rel = np.clip(p - q + off, -64, 64)
            idx = _t5_bucket_np(rel * d)
            for h in range(4):
                o[h, di, ab] = rel_bias[idx, h]
    return o


_CACHE = {}


def kernel(**inputs):
    inputs = {k: np.asarray(v) for k, v in inputs.items()}
    if "nc" not in _CACHE:
        _CACHE["nc"] = build_program()
    nc, _ = _CACHE["nc"]
    consts = _constants()
    shared = {k: np.ascontiguousarray(v, dtype=np.float32) for k, v in inputs.items() if k not in ("x", "rel_bias")}
    shared.update(consts)
    shared["dbias"] = _dbias(inputs["rel_bias"].astype(np.float32))
    x = inputs["x"].astype(np.float32)
    in_maps = []
    for c in range(8):
        m = dict(shared)
        m["x"] = np.ascontiguousarray(x[c % 4])
        in_maps.append(m)
    res = run_bass_kernel_spmd(nc, in_maps, core_ids=list(range(8)))
    return np.stack([res.results[c]["out"] for c in range(4)], 0).astype(np.float32)
```

```python
import math
import os
import numpy as np
BSTEP = int(os.environ.get("BSTEP", "99"))
import ml_dtypes
import concourse.bass as bass
import concourse.mybir as mybir
from concourse.bass_utils import run_bass_kernel_spmd

F32 = mybir.dt.float32
BF16 = mybir.dt.bfloat16
AF = mybir.ActivationFunctionType
ALU = mybir.AluOpType
AX = mybir.AxisListType

S = 4096
D = 1024
NT = 32
DEPTH = 4
DFF = 4096
ALPHA = (2 * DEPTH) ** 0.25
LN_EPS = 1e-5
RMS_EPS = 1e-6
DILS = (1, 4, 16)
MASKV = -240000.0

ENGS = ("pe", "act", "dve", "pool", "sp")


class Prog:
    def __init__(self, nc, n_dma_slots=48):
        self.nc = nc
        self.ops = []
        self.n_dma_slots = n_dma_slots
        self.eng_obj = {"pe": nc.tensor, "act": nc.scalar, "dve": nc.vector,
                        "pool": nc.gpsimd, "sp": nc.sync}

    skip = False

    def op(self, eng, fn, reads=(), writes=(), dma=False):
        if self.skip:
            return
        ps_r = tuple(k for k in reads if isinstance(k, tuple) and k[0] == "ps" and k not in writes)
        self.ops.append((eng, fn, tuple(reads), tuple(writes) + ps_r, dma))

    def dma(self, out, in_, reads, writes, q="sp", **kw):
        e = self.eng_obj[q]
        self.op(q, lambda: e.dma_start(out=out, in_=in_, **kw), reads, writes, dma=True)

    def barrier(self, fn):
        if self.skip:
            return
        self.ops.append(("pool", fn, "BARRIER", (), False))

    def emit(self, final_wait_keys=()):
        nc = self.nc
        ops = self.ops
        n = len(ops)
        last_w = {}
        readers = {}
        deps = [None] * n
        bar = None
        last_eng = {}
        dma_since = []
        for i, (eng, fn, reads, writes, is_dma) in enumerate(ops):
            if reads == "BARRIER":
                d = set(last_eng.values()) | set(dma_since)
                if bar is not None:
                    d.add(bar)
                deps[i] = d
                last_w = {}
                readers = {}
                dma_since = []
                last_eng = {}
                bar = i
                ops[i] = (eng, fn, (), (), False)
                continue
            if is_dma:
                dma_since.append(i)
            else:
                last_eng[eng] = i
            d = set()
            if bar is not None:
                d.add(bar)
            for k in reads:
                if k in last_w:
                    d.add(last_w[k])
            for k in writes:
                if k in last_w:
                    d.add(last_w[k])
                for r in readers.get(k, ()):
                    d.add(r)
            d.discard(i)
            deps[i] = d
            for k in reads:
                readers.setdefault(k, []).append(i)
            for k in writes:
                last_w[k] = i
                readers[k] = []
        final_deps = set()
        for k in final_wait_keys:
            if k in last_w:
                final_deps.add(last_w[k])
        needed = set()
        for i in range(n):
            eng_i, _, _, _, dma_i = ops[i]
            keep = set()
            for j in deps[i]:
                eng_j, _, _, _, dma_j = ops[j]
                if (not dma_j) and (not dma_i) and eng_i == "pe" and eng_j == "pe":
                    continue
                keep.add(j)
            deps[i] = keep
            needed |= keep
        needed |= final_deps
        sems = {e: nc.alloc_semaphore(name=f"s_{e}") for e in ENGS}
        slots = [nc.alloc_semaphore(name=f"s_dma{k}") for k in range(self.n_dma_slots)]
        cnt = {e: 0 for e in ENGS}
        slot_use = [0] * self.n_dma_slots
        half = self.n_dma_slots // 2
        slot_rr = {"sw": 0, "hw": 0}
        done_tok = [None] * n
        prev_slot_tok = [None] * n
        for i, (eng, fn, reads, writes, is_dma) in enumerate(ops):
            if is_dma:
                kind = "sw" if eng == "pool" else "hw"
                s = slot_rr[kind] + (half if kind == "sw" else 0)
                slot_rr[kind] = (slot_rr[kind] + 1) % half
                if slot_use[s] > 0:
                    prev_slot_tok[i] = (("slot", s), 16 * slot_use[s])
                slot_use[s] += 1
                done_tok[i] = (("slot", s), 16 * slot_use[s])
            elif i in needed:
                cnt[eng] += 1
                done_tok[i] = (("eng", eng), cnt[eng])
        seen = {e: {} for e in ENGS}

        def semh(key):
            return sems[key[1]] if key[0] == "eng" else slots[key[1]]

        n_wait = 0
        for i, (eng, fn, reads, writes, is_dma) in enumerate(ops):
            e = self.eng_obj[eng]
            want = {}
            for j in deps[i]:
                key, val = done_tok[j]
                if want.get(key, 0) < val:
                    want[key] = val
            if prev_slot_tok[i] is not None:
                key, val = prev_slot_tok[i]
                if want.get(key, 0) < val:
                    want[key] = val
            for key, val in want.items():
                if seen[eng].get(key, 0) >= val:
                    continue
                e.wait_ge(semh(key), val)
                seen[eng][key] = val
                n_wait += 1
            ins = fn()
            if is_dma:
                ins.then_inc(semh(done_tok[i][0]), 16)
            elif done_tok[i] is not None:
                ins.then_inc(semh(done_tok[i][0]), 1)
        e = self.eng_obj["sp"]
        want = {}
        for j in final_deps:
            key, val = done_tok[j]
            if want.get(key, 0) < val:
                want[key] = val
        for key, val in want.items():
            e.wait_ge(semh(key), val)
        self.stats = dict(n_ops=n, n_wait=n_wait, cnt=dict(cnt))
        return self.stats


def fv(ap, off, dims):
    return bass.AP(ap.tensor, ap.offset + off, [list(ap.ap[0])] + [list(d) for d in dims])


def build_program(n_layers=DEPTH, debug=False, stop_after=None):
    nc = bass.Bass("TRN2", target_bir_lowering=False)
    P = Prog(nc)

    def din(name, shape, dt=F32):
        return nc.dram_tensor(name, list(shape), dt, kind="ExternalInput").ap()

    def dscr(name, shape, dt):
        return nc.dram_tensor(name, list(shape), dt, kind="Internal").ap()

    x_in = din("x", [S, D])
    emb_g = din("emb_ln_g", [D]); emb_b = din("emb_ln_b", [D])
    w_in = din("w_in", [DEPTH, D, 2048])
    w_fnet = din("w_fnet", [DEPTH, 256, 256])
    qg = din("q_norm_g", [DEPTH, 64]); kg = din("k_norm_g", [DEPTH, 64])
    conv_dw = din("conv_dw", [DEPTH, 31, 256]); conv_b = din("conv_b", [DEPTH, 256])
    cln_g = din("conv_ln_g", [DEPTH, 256]); cln_b = din("conv_ln_b", [DEPTH, 256])
    w_pw = din("w_conv_out", [DEPTH, 256, 256])
    w_out = din("w_out", [DEPTH, D, D])
    ln1_g = din("ln1_g", [DEPTH, D]); ln1_b = din("ln1_b", [DEPTH, D])
    w_ff1 = din("w_ff1", [DEPTH, D, DFF]); w_ff2 = din("w_ff2", [DEPTH, DFF, D])
    ln2_g = din("ln2_g", [DEPTH, D]); ln2_b = din("ln2_b", [DEPTH, D])
    dbias = din("dbias", [4, 3, 2, 128, 128])
    c_ident = din("c_ident", [128, 128])
    c_rope = din("c_rope", [S, 64])
    c_bcs = din("c_bcs", [256, 512])
    c_r = din("c_r", [128, 128])
    c_t3 = din("c_t3", [128, 4096])
    c_dmask = din("c_dmask", [2, 128, 128])
    out = nc.dram_tensor("out", [S, D], F32, kind="ExternalOutput").ap()

    X = dscr("X", [S, D], F32)
    XT = dscr("XT", [8, 128, S], BF16)
    YT = dscr("YT", [8, 128, S], BF16)
    VD = dscr("VD", [S, 128], BF16)
    B8d = dscr("B8d", [24, 128, 128], BF16)
    if debug:
        dbg_yt = nc.dram_tensor("dbg_yt", [8, 128, S], BF16, kind="ExternalOutput").ap()
        dbg_x1 = nc.dram_tensor("dbg_x1", [S, D], F32, kind="ExternalOutput").ap()

    def sb(name, shape, dt):
        return nc.alloc_sbuf_tensor(name, list(shape), dt)

    PS = nc.alloc_psum_tensor("PS", [128, 4096], F32)

    def bank(k, w=512):
        return PS[:, k * 512:k * 512 + w]

    def pk(k, nb=1):
        return [("ps", k + i) for i in range(nb)]

    ident = sb("ident", [128, 128], BF16)
    identf = sb("identf", [128, 128], F32)
    onesF = sb("onesF", [128, 128], F32)
    ones64 = sb("ones64", [128, 64], BF16)
    mh = sb("mh", [128, 8], F32)
    xTc = [sb(f"xTc{i}", [128, 8, 512], BF16) for i in range(2)]
    WA = sb("WA", [128, 8, 1024], BF16)
    big1 = sb("big1", [128, 16384], F32)
    big2 = sb("big2", [128, 16384], F32)
    lng = sb("lng", [128, 1024], F32); lnb = sb("lnb", [128, 1024], F32)
    rt = [sb(f"rt{i}", [128, 1024], F32) for i in range(2)]
    xt_in = [sb(f"xin{i}", [128, 1024], F32) for i in range(2)]
    xnb = [sb(f"xnb{i}", [128, 1024], BF16) for i in range(2)]
    xTs = [sb("xTs0", [128, 8, 256], BF16)] * 2
    st6 = sb("st6", [128, 2, 2, 6], F32)
    mv2 = sb("mv2", [128, 2, 2], F32)
    rstd2 = sb("rstd2", [128, 2], F32)
    small = sb("small", [128, 2048], F32)
    bar_t = sb("bar_t", [128, 8], F32)

    def mm(o, l, r, st, sp, R, W):
        P.op("pe", lambda: nc.tensor.matmul(o, lhsT=l, rhs=r, start=st, stop=sp), R, W)

    def tr(o, i, idn, R, W):
        P.op("pe", lambda: nc.tensor.transpose(o, i, idn), R, W)

    def act(o, i, f, R, W, scale=None, bias=None):
        kw = {}
        if scale is not None:
            kw["scale"] = scale
        if bias is not None:
            kw["bias"] = bias
        P.op("act", lambda: nc.scalar.activation(out=o, in_=i, func=f, **kw), R, W)

    def cp(eng, o, i, R, W):
        e = P.eng_obj[eng]
        if eng == "act":
            P.op("act", lambda: nc.scalar.copy(out=o, in_=i), R, W)
        else:
            P.op(eng, lambda: e.tensor_copy(out=o, in_=i), R, W)

    def tt(eng, o, a, b, op, R, W):
        e = P.eng_obj[eng]
        P.op(eng, lambda: e.tensor_tensor(out=o, in0=a, in1=b, op=op), R, W)

    def ts(eng, o, a, s1, s2, op0, op1, R, W):
        e = P.eng_obj[eng]
        if op1 is None:
            P.op(eng, lambda: e.tensor_scalar(out=o, in0=a, scalar1=s1, scalar2=None, op0=op0), R, W)
        else:
            P.op(eng, lambda: e.tensor_scalar(out=o, in0=a, scalar1=s1, scalar2=s2, op0=op0, op1=op1), R, W)

    def stt(o, a, s, b, op0, op1, R, W):
        P.op("dve", lambda: nc.vector.scalar_tensor_tensor(out=o, in0=a, scalar=s, in1=b, op0=op0, op1=op1), R, W)

    def memset(eng, o, v, W):
        e = P.eng_obj[eng]
        P.op(eng, lambda: e.memset(o, v), [], W)

    def wload(dst, src, R, W):
        P.dma(dst, src, R, W, q="pool")

    wload(ident[:], c_ident, [], ["ident"])
    P.dma(identf[:], c_ident, [], ["identf"])
    memset("pool", onesF[:], 1.0 / 256.0, ["onesF"])
    memset("pool", ones64[:], 1.0, ["ones64"])
    memset("pool", mh[:], -0.5, ["mh"])
    for h in range(4):
        for di in range(3):
            for ab in range(2):
                k = (h * 3 + di) * 2 + ab
                P.dma(small[:, 0:128], dbias[h, di, ab], [], ["small"])
                P.dma(small[:, 128:256], c_dmask[ab], [], ["small"])
                b8s = fv(small[:].bitcast(BF16), 1024, [[1, 128]])
                stt(small[:, 256:384], small[:, 0:128], 8.0, small[:, 128:256], ALU.mult, ALU.add, ["small"], ["b8f"])
                act(b8s, small[:, 256:384], AF.Exp, ["b8f"], ["b8s"], scale=0.125)
                P.dma(B8d[k], b8s, ["b8s"], ["B8d"])

    def ln_pair(r_aps, rkeys, xn_aps, xnkeys, rstd_on_pool=False, part="ab"):
        for k in range(2 if "a" in part else 0):
            r_ap = r_aps[k]
            P.op("dve", lambda r_ap=r_ap, k=k: nc.vector.bn_stats(out=st6[:, k, 0, :], in_=r_ap[:, 0:512]), [rkeys[k]], [f"st6_{k}"])
            P.op("dve", lambda r_ap=r_ap, k=k: nc.vector.bn_stats(out=st6[:, k, 1, :], in_=r_ap[:, 512:1024]), [rkeys[k]], [f"st6_{k}"])
            P.op("dve", lambda k=k: nc.vector.bn_aggr(out=mv2[:, k, :], in_=st6[:, k].rearrange("p a b -> p (a b)")), [f"st6_{k}"], ["mv2"])
        if "b" not in part:
            return
        if rstd_on_pool:
            ts("pool", rstd2[:], fv(mv2[:], 1, [[2, 2]]), LN_EPS, None, ALU.add, None, ["mv2"], ["rstd2"])
            tt("pool", rstd2[:], rstd2[:], mh[:, 0:2], ALU.pow, ["rstd2", "mh"], ["rstd2"])
        else:
            ts("dve", rstd2[:], fv(mv2[:], 1, [[2, 2]]), LN_EPS, None, ALU.add, None, ["mv2"], ["rstd2"])
            act(rstd2[:], rstd2[:], AF.Sqrt, ["rstd2"], ["rstd2"])
            P.op("dve", lambda: nc.vector.reciprocal(out=rstd2[:], in_=rstd2[:]), ["rstd2"], ["rstd2"])
        for k in range(2):
            ts("dve", xn_aps[k], r_aps[k], mv2[:, k, 0:1], rstd2[:, k:k + 1], ALU.subtract, ALU.mult, [rkeys[k], "mv2", "rstd2"], [xnkeys[k]])
            tt("dve", xn_aps[k], xn_aps[k], lng[:], ALU.mult, [xnkeys[k], "lng"], [xnkeys[k]])
            tt("dve", xn_aps[k], xn_aps[k], lnb[:], ALU.add, [xnkeys[k], "lng"], [xnkeys[k]])

    def transpose_tile(xn_ap, xnkey, dst_ap, dstkey, i):
        cp("act", xnb[i][:], xn_ap, [xnkey], [f"xnb{i}"])
        pst = bank(7).bitcast(BF16)
        for c in range(8):
            tr(pst[:, c * 128:(c + 1) * 128], xnb[i][:, c * 128:(c + 1) * 128], ident[:], [f"xnb{i}", "ident"], pk(7))
        cp("dve", dst_ap, pst.rearrange("p (c t) -> p c t", c=8), pk(7), [dstkey])

    def load_ln_params(g_ap, b_ap):
        P.dma(lng[:], g_ap.partition_broadcast(128), [], ["lng"])
        P.dma(lnb[:], b_ap.partition_broadcast(128), [], ["lng"])

    XTv = XT.rearrange("c p s -> p c s")
    YTv = YT.rearrange("c p s -> p c s")

    def ln_store(xn_tiles, t0, final=False, write_xt=True):
        pass

    load_ln_params(emb_g, emb_b)
    for tp in range(NT // 2):
        for k in range(2):
            t = 2 * tp + k
            P.dma(xt_in[k][:], x_in[t * 128:(t + 1) * 128, :], [], [f"xin{k}"])
        ln_pair([xt_in[0][:], xt_in[1][:]], ["xin0", "xin1"], [rt[0][:], rt[1][:]], ["rt0", "rt1"])
        for k in range(2):
            t = 2 * tp + k
            P.dma(X[t * 128:(t + 1) * 128, :], rt[k][:], [f"rt{k}"], [("X", t)], q="pool")
            transpose_tile(rt[k][:], f"rt{k}", xTs[0][:, :, k * 128:(k + 1) * 128], "xTs0", k)
        P.dma(XTv[:, :, tp * 256:(tp + 1) * 256], xTs[0][:], ["xTs0"], [("XT", tp // 2)], q="pool")

    P.barrier(lambda: nc.gpsimd.memset(bar_t[:], 0.0))

    def load_xT(j, buf):
        P.dma(xTc[buf][:], XTv[:, :, j * 512:(j + 1) * 512], [("XT", j)], [f"xTc{buf}"])

    b1 = big1[:].bitcast(BF16)
    b2 = big2[:].bitcast(BF16)
    smb = small[:].bitcast(BF16)

    order = ["p0", "A", "B0", "B1", "B2", "B", "C", "D", "O", "F"]
    def run_phase(ph):
        return stop_after is None or order.index(ph) <= order.index(stop_after)

    for l in range(n_layers):
        win_v = w_in[l].rearrange("(c p) n -> p c n", p=128)

        P.skip = not run_phase("A")
        wload(WA[:, :, 0:256], win_v[:, :, 0:256], [], ["WA"])
        BCS = fv(smb, 0, [[512, 2], [1, 512]])
        Wf = fv(smb, 1024, [[256, 2], [1, 256]])
        Wcs = fv(smb, 1536, [[512, 2], [1, 512]])
        Rr = fv(smb, 2560, [[1, 128]])
        wload(BCS, c_bcs.rearrange("(c p) n -> p c n", p=128), [], ["small"])
        wload(Wf, w_fnet[l].rearrange("(c p) n -> p c n", p=128), [], ["small"])
        wload(Rr, c_r, [], ["small"])
        T3 = fv(b2, 16384, [[1, 4096]])
        wload(T3, c_t3, [], ["big2"])
        for m in range(2):
            mm(bank(m)[:, 0:256], BCS[:, m, m * 128:(m + 1) * 128], Wf[:, m, :], True, True, ["small"], pk(m))
            mm(bank(m)[:, 256:512], BCS[:, m, 256 + m * 128:256 + (m + 1) * 128], Wf[:, m, :], True, True, ["small"], pk(m))
            cp("dve", Wcs[:, m, :], bank(m), pk(m), ["small"])
        ufT = fv(b1, 0, [[4096, 2], [1, 4096]])
        load_xT(0, 0)
        for j in range(8):
            if j + 1 < 8:
                load_xT(j + 1, (j + 1) % 2)
            xc = xTc[j % 2]
            for m in range(2):
                for c in range(8):
                    mm(bank(m), WA[:, c, m * 128:(m + 1) * 128], xc[:, c, :], c == 0, c == 7, ["WA", f"xTc{j % 2}"], pk(m))
                cp("act" if m == 0 else "dve", ufT[:, m, j * 512:(j + 1) * 512], bank(m), pk(m), [("ufT", j)])
        sk = P.skip
        P.skip = not run_phase("B0")
        wload(WA[:, :, 0:512], win_v[:, :, 256:768], [], ["WA"])
        P.skip = sk
        A_sb = fv(b1, 8192, [[256, 64], [1, 256]])
        ufkeys = [("ufT", j) for j in range(8)]
        for s1 in range(64):
            pb = s1 % 2
            for c in range(2):
                l_ap = fv(b1, c * 4096 + s1, [[64, 64]])
                mm(bank(pb)[0:64, 0:256], l_ap, Wcs[:, c, 0:256], c == 0, c == 1, ufkeys + ["small"], pk(pb))
            for c in range(2):
                l_ap = fv(b1, c * 4096 + s1, [[64, 64]])
                mm(bank(pb)[64:128, 0:256], l_ap, Wcs[:, c, 256:512], c == 0, c == 1, ufkeys + ["small"], pk(pb))
            cp("act" if pb == 0 else "dve", A_sb[:, s1, :], bank(pb)[:, 0:256], pk(pb), ["A_sb"])
        Y_sb = fv(b2, 0, [[64, 256], [1, 64]])
        for cb in range(32):
            pb = cb % 2
            for cc in range(8):
                ch = cb * 8 + cc
                l_ap = fv(b1, 8192 + ch, [[256, 64]])
                mm(bank(pb)[0:64, cc * 64:(cc + 1) * 64], l_ap, Rr[:, 0:64], True, True, ["A_sb", "small"], pk(pb))
                mm(bank(pb)[64:128, cc * 64:(cc + 1) * 64], l_ap, Rr[:, 64:128], True, True, ["A_sb", "small"], pk(pb))
            cp("act" if pb == 0 else "dve", fv(b2, cb * 512, [[1, 512]]), bank(pb), pk(pb), ["Y_sb"])
        yA = fv(b1, 24576, [[4096, 2], [1, 4096]])
        for m in range(2):
            for kb in range(8):
                pb = kb % 2
                for ks in range(8):
                    k2 = kb * 8 + ks
                    l_ap = fv(b2, m * 128 * 64 + k2, [[64, 128]])
                    o_ap = fv(bank(pb), ks, [[8, 64]])
                    mm(o_ap, l_ap, T3[:, k2 * 64:(k2 + 1) * 64], True, True, ["Y_sb", "big2"], pk(pb))
                o_sb = fv(b1, 24576 + m * 4096 + kb * 8, [[64, 64], [1, 8]])
                i_ps = fv(bank(pb), 0, [[8, 64], [1, 8]])
                cp("act" if pb == 0 else "dve", o_sb, i_ps, pk(pb), [("yA", m)])
            P.dma(YTv[:, m, :], yA[:, m, :], [("yA", m)], [("YT", m)], q="pool")

        P.barrier(lambda: nc.gpsimd.memset(bar_t[:], 0.0))
        P.skip = not run_phase("B0")
        QKT = fv(b1, 0, [[4096, 3], [1, 4096]])
        Vaug = fv(b1, 12288, [[256, 32], [128, 2], [1, 128]])
        rope = fv(big2[:], 0, [[64, 32], [1, 64]])
        P.dma(rope, c_rope.rearrange("(t p) f -> p t f", p=128), [], ["rope"])
        for hh in range(4):
            P.dma(fv(big2[:], 2048 + hh * 64, [[1, 64]]), qg[l].partition_broadcast(128), [], ["gqk"])
        for hh in range(2):
            P.dma(fv(big2[:], 2048 + 256 + hh * 64, [[1, 64]]), kg[l].partition_broadcast(128), [], ["gqk"])
        memset("dve", fv(b1, 12288 + 64, [[128, 64], [1, 64]]), 1.0, ["Vaug_ones"])
        QKo, TMPo, SSo, RSo, QBo = 2560, 8704, 14848, 14944, 20480
        P.skip = not run_phase("B1")
        load_xT(0, 0)
        for half in range(2):
            for k in range(16):
                t = half * 16 + k
                j = t // 4
                if t % 4 == 0 and j + 1 < 8:
                    load_xT(j + 1, (j + 1) % 2)
                xc = xTc[j % 2]
                tl = (t % 4) * 128
                pb = t % 2
                for c in range(8):
                    mm(bank(pb), xc[:, c, tl:tl + 128], WA[:, c, 0:512], c == 0, c == 7, ["WA", f"xTc{j % 2}"], pk(pb))
                cp("act", fv(big2[:], QKo + k * 384, [[1, 384]]), bank(pb)[:, 0:384], pk(pb), ["QKh"])
                cp("dve", fv(b1, 12288 + t * 256, [[128, 2], [1, 64]]), fv(bank(pb), 384, [[64, 2], [1, 64]]), pk(pb), [("Vaug", t)])
            QKf = fv(big2[:], QKo, [[1, 6144]])
            TMPf = fv(big2[:], TMPo, [[1, 6144]])
            tt("dve", TMPf, QKf, QKf, ALU.mult, ["QKh"], ["TMP"])
            P.op("dve", lambda: nc.vector.tensor_reduce(out=fv(big2[:], SSo, [[1, 96]]), in_=fv(big2[:], TMPo, [[64, 96], [1, 64]]),
                                                        axis=AX.X, op=ALU.add), ["TMP"], ["ss"])
            ts("pool", fv(big2[:], RSo, [[1, 96]]), fv(big2[:], SSo, [[1, 96]]), 1.0 / 64.0, RMS_EPS, ALU.mult, ALU.add, ["ss"], ["rs"])
            tt("pool", fv(big2[:], RSo, [[1, 96]]), fv(big2[:], RSo, [[1, 96]]), fv(mh[:], 0, [[0, 96]]), ALU.pow, ["rs", "mh"], ["rs"])
            tt("dve", fv(big2[:], QKo, [[64, 96], [1, 64]]), fv(big2[:], QKo, [[64, 96], [1, 64]]), fv(big2[:], RSo, [[1, 96], [0, 64]]),
               ALU.mult, ["QKh", "rs"], ["QKh"])
            tt("dve", fv(big2[:], QKo, [[384, 16], [1, 384]]), fv(big2[:], QKo, [[384, 16], [1, 384]]), fv(big2[:], 2048, [[0, 16], [1, 384]]),
               ALU.mult, ["QKh", "gqk"], ["QKh"])
            for h6 in range(6):
                eng = "dve" if h6 < 3 else "pool"
                x1 = fv(big2[:], QKo + h6 * 64, [[384, 16], [32, 2], [1, 16]])
                x2 = fv(big2[:], QKo + h6 * 64 + 16, [[384, 16], [32, 2], [1, 16]])
                cosb = fv(big2[:], half * 16 * 64, [[64, 16], [16, 2], [1, 16]])
                sinb = fv(big2[:], half * 16 * 64 + 32, [[64, 16], [16, 2], [1, 16]])
                t1 = fv(big2[:], TMPo + h6 * 1024, [[32, 16], [16, 2], [1, 16]])
                t2 = fv(big2[:], TMPo + h6 * 1024 + 512, [[32, 16], [16, 2], [1, 16]])
                slot = [0, 2, 1, 3, 4, 5][h6]
                o1 = fv(b1, QBo + slot * 64, [[384, 16], [32, 2], [1, 16]])
                o2 = fv(b1, QBo + slot * 64 + 16, [[384, 16], [32, 2], [1, 16]])
                tt(eng, t1, x1, cosb, ALU.mult, ["QKh", "rope", "TMP"], [("t1", h6)])
                tt(eng, t2, x2, sinb, ALU.mult, ["QKh", "rope", "TMP"], [("t2", h6)])
                tt(eng, o1, t1, t2, ALU.subtract, [("t1", h6), ("t2", h6)], [("qb", h6)])
                tt(eng, t1, x1, sinb, ALU.mult, ["QKh", "rope", ("qb", h6)], [("t1", h6)])
                tt(eng, t2, x2, cosb, ALU.mult, ["QKh", "rope", ("qb", h6)], [("t2", h6)])
                tt(eng, o2, t1, t2, ALU.add, [("t1", h6), ("t2", h6), "QKh"], [("qb", h6)])
            qbk = [("qb", h6) for h6 in range(6)]
            for k in range(16):
                t = half * 16 + k
                pst = bank(6 + k % 2).bitcast(BF16)
                for k3 in range(3):
                    tr(pst[:, k3 * 128:(k3 + 1) * 128], fv(b1, QBo + k * 384 + k3 * 128, [[1, 128]]), ident[:], qbk + ["ident"], pk(6 + k % 2))
                cp("act" if k % 2 == 0 else "dve", fv(b1, t * 128, [[4096, 3], [1, 128]]), fv(pst, 0, [[128, 3], [1, 128]]), pk(6 + k % 2), [("QKT", t)])
        sk = P.skip
        P.skip = not run_phase("C")
        wload(WA[:, :, 0:512], win_v[:, :, 768:1280], [], ["WA"])
        P.skip = sk
        P.skip = not run_phase("B")
        qkt_keys = [("QKT", t) for t in range(NT)]
        yB = fv(b2, 16384, [[4096, 2], [1, 4096]])
        PT = [fv(smb, i * 1024, [[1, 1024]]) for i in range(2)]
        rec = fv(big2[:], 12288, [[1, 2048]])
        osb = fv(big2[:], 14336, [[1, 2048]])
        iters = [(qc, kt) for qc in range(8) for kt in range(NT)]

        def b_S(it, hb):
            qc, kt = iters[it]
            for g in range(2):
                pr = slice(g * 64, (g + 1) * 64)
                mm(bank(hb * 2 + g), QKT[pr, 2, kt * 128:(kt + 1) * 128], QKT[pr, hb, qc * 512:(qc + 1) * 512], True, True,
                   qkt_keys, pk(hb * 2 + g))

        def b_E(it, hb):
            act(PT[hb], PS[:, hb * 1024:hb * 1024 + 1024], AF.Exp, pk(hb * 2, 2), [f"PT{hb}"], scale=0.125)

        def b_P(it, hb):
            qc, kt = iters[it]
            for g in range(2):
                mm(bank(4 + hb * 2 + g), Vaug[:, kt, g, :], PT[hb][:, g * 512:(g + 1) * 512], kt == 0, kt == NT - 1,
                   [f"PT{hb}", ("Vaug", kt), "Vaug_ones"], pk(4 + hb * 2 + g))
            if kt == NT - 1 and hb == 1:
                for hf in range(2):
                    cp("dve", rec[0:64, hf * 1024:(hf + 1) * 1024], PS[64:128, 2048 + hf * 1024:2048 + (hf + 1) * 1024], pk(4 + 2 * hf, 2), ["rec"])
                    cp("act", osb[0:64, hf * 1024:(hf + 1) * 1024], PS[0:64, 2048 + hf * 1024:2048 + (hf + 1) * 1024], pk(4 + 2 * hf, 2), ["osb"])
                P.op("dve", lambda: nc.vector.reciprocal(out=rec[0:64, :], in_=rec[0:64, :]), ["rec"], ["rec"])
                for hb2 in range(2):
                    for g in range(2):
                        bk = hb2 * 2 + g
                        tt("dve", yB[hb2 * 64:(hb2 + 1) * 64, g, qc * 512:(qc + 1) * 512], osb[0:64, bk * 512:(bk + 1) * 512],
                           rec[0:64, bk * 512:(bk + 1) * 512], ALU.mult, ["osb", "rec"], [("yB", g)])
                if qc == 7:
                    for g in range(2):
                        P.dma(YTv[:, 2 + g, :], yB[:, g, :], [("yB", g)], [("YT", 2 + g)], q="pool")

        b_S(0, 0)
        b_S(0, 1)
        for it in range(len(iters)):
            for hb in range(2):
                b_E(it, hb)
                b_P(it, hb)
                if it + 1 < len(iters):
                    b_S(it + 1, hb)

        P.barrier(lambda: nc.gpsimd.memset(bar_t[:], 0.0))
        P.skip = not run_phase("C")
        HW_ = 4128
        hbuf = fv(big1[:], 0, [[HW_, 2], [1, HW_]])
        accC = fv(big2[:], 0, [[4096, 2], [1, 4096]])
        dwT = fv(big2[:], 8192, [[31, 2], [1, 31]])
        cb_sb = fv(big2[:], 8256, [[1, 2]])
        cg_sb = fv(big2[:], 8258, [[1, 2]])
        cbe_sb = fv(big2[:], 8260, [[1, 2]])
        Wpw = fv(b2, 16640, [[256, 2], [1, 256]])
        dwraw = fv(big2[:], 8704, [[1, 256]])
        P.dma(dwraw[0:31, :], conv_dw[l], [], ["dwraw"])
        for m in range(2):
            tr(bank(6)[:, m * 32:m * 32 + 31], dwraw[0:31, m * 128:(m + 1) * 128], identf[0:31, 0:31], ["dwraw", "identf"], pk(6))
        cp("dve", dwT, fv(bank(6), 0, [[32, 2], [1, 31]]), pk(6), ["cpar"])
        P.dma(cb_sb, conv_b[l].rearrange("(m p) -> p m", p=128), [], ["cpar"], allow_slow_non_contiguous=True)
        P.dma(cg_sb, cln_g[l].rearrange("(m p) -> p m", p=128), [], ["cpar"], allow_slow_non_contiguous=True)
        P.dma(cbe_sb, cln_b[l].rearrange("(m p) -> p m", p=128), [], ["cpar"], allow_slow_non_contiguous=True)
        wload(Wpw, w_pw[l].rearrange("(c p) n -> p c n", p=128), [], ["Wpw"])
        for m in range(2):
            memset("pool", fv(big1[:], m * HW_, [[1, 16]]), 0.0, [("hbuf", "padl")])
            memset("pool", fv(big1[:], m * HW_ + 16 + 4096, [[1, 16]]), 0.0, [("hbuf", "padr")])
        sig = [fv(big1[:], 8256 + i * 512, [[1, 512]]) for i in range(2)]
        HB = 26752
        DG = 18432
        memset("pool", fv(b1, HB, [[1, 16]]), 0.0, [("hb16", "padl")])
        memset("pool", fv(b1, HB + 16 + 4096, [[1, 16]]), 0.0, [("hb16", "padr")])
        for k in range(31):
            ts("pool", fv(b2, DG + k * 128, [[1, 128]]), ident[:], dwT[:, 1, k:k + 1], None, ALU.mult, None, ["ident", "cpar"], ["diag"])
        load_xT(0, 0)
        for j in range(8):
            if j + 1 < 8:
                load_xT(j + 1, (j + 1) % 2)
            xc = xTc[j % 2]
            for m in range(4):
                for c in range(8):
                    mm(bank(m), WA[:, c, m * 128:(m + 1) * 128], xc[:, c, :], c == 0, c == 7, ["WA", f"xTc{j % 2}"], pk(m))
            for m in range(2):
                act(sig[m], bank(2 + m), AF.Sigmoid, pk(2 + m), [f"sig{m}"])
            tt("dve", hbuf[:, 0, 16 + j * 512:16 + (j + 1) * 512], bank(0), sig[0], ALU.mult, pk(0) + ["sig0"], [("hbuf", j)])
            tt("dve", fv(b1, HB + 16 + j * 512, [[1, 512]]), bank(1), sig[1], ALU.mult, pk(1) + ["sig1"], [("hb16", j)])
        sk = P.skip
        P.skip = not run_phase("D")
        wload(WA[:, :, 0:768], win_v[:, :, 1280:2048], [], ["WA"])
        P.skip = sk
        hkeys_all = [("hbuf", j) for j in range(8)] + [("hbuf", "padl"), ("hbuf", "padr")]
        h16keys = [("hb16", j) for j in range(8)] + [("hb16", "padl"), ("hb16", "padr")]
        for pc in range(4):
            o = accC[:, 0, pc * 1024:(pc + 1) * 1024]
            for k in range(31):
                src = hbuf[:, 0, 16 + pc * 1024 + k - 15:16 + pc * 1024 + k - 15 + 1024]
                if k == 0:
                    ts("dve", o, src, dwT[:, 0, 0:1], cb_sb[:, 0:1], ALU.mult, ALU.add, hkeys_all + ["cpar"], [("accC", 0, pc)])
                else:
                    stt(o, src, dwT[:, 0, k:k + 1], o, ALU.mult, ALU.add, hkeys_all + ["cpar"], [("accC", 0, pc)])
            for jj in range(2):
                j = pc * 2 + jj
                pb = 4 + j % 2
                for k in range(31):
                    mm(bank(pb), fv(b2, DG + k * 128, [[1, 128]]), fv(b1, HB + 16 + j * 512 + k - 15, [[1, 512]]), k == 0, k == 30,
                       h16keys + ["diag"], pk(pb))
                act(accC[:, 1, j * 512:(j + 1) * 512], bank(pb), AF.Identity, pk(pb) + ["cpar"], [("accC", 1, pc)], bias=cb_sb[:, 1:2])
        sqb = [fv(big1[:], 9280 + i * 512, [[1, 512]]) for i in range(2)]
        mean_sb = fv(big1[:], 10304, [[1, 512]])
        var_sb = fv(big1[:], 10816, [[1, 512]])
        xh = [fv(big1[:], 11328 + i * 512, [[1, 512]]) for i in range(2)]
        hact = fv(b1, 2 * 12352, [[512, 2], [1, 512]])
        yC = fv(b1, 2 * 12864, [[512, 2], [1, 512]])
        for j in range(8):
            cs = slice(j * 512, (j + 1) * 512)
            akeys = [("accC", m, j // 2) for m in range(2)]
            for m in range(2):
                act(sqb[m], accC[:, m, cs], AF.Square, [("accC", m, j // 2)], [f"sq{m}"])
            for m in range(2):
                mm(bank(0), onesF[:], accC[:, m, cs], m == 0, m == 1, akeys + ["onesF"], pk(0))
            for m in range(2):
                mm(bank(1), onesF[:], sqb[m], m == 0, m == 1, [f"sq{m}", "onesF"], pk(1))
            cp("act", mean_sb, bank(0), pk(0), ["mean"])
            tt("dve", var_sb, mean_sb, mean_sb, ALU.mult, ["mean"], ["var"])
            tt("dve", var_sb, bank(1), var_sb, ALU.subtract, pk(1) + ["var"], ["var"])
            ts("dve", var_sb, var_sb, LN_EPS, None, ALU.add, None, ["var"], ["var"])
            act(var_sb, var_sb, AF.Sqrt, ["var"], ["var"])
            P.op("dve", lambda: nc.vector.reciprocal(out=var_sb, in_=var_sb), ["var"], ["var"])
            for m in range(2):
                tt("dve", xh[m], accC[:, m, cs], mean_sb, ALU.subtract, [("accC", m, j // 2), "mean"], [f"xh{m}"])
                tt("dve", xh[m], xh[m], var_sb, ALU.mult, [f"xh{m}", "var"], [f"xh{m}"])
                act(hact[:, m, :], xh[m], AF.Silu, [f"xh{m}", "cpar"], ["hact"], scale=cg_sb[:, m:m + 1], bias=cbe_sb[:, m:m + 1])
            for mo in range(2):
                for c in range(2):
                    mm(bank(2 + mo), Wpw[:, c, mo * 128:(mo + 1) * 128], hact[:, c, :], c == 0, c == 1, ["hact", "Wpw"], pk(2 + mo))
                cp("dve", yC[:, mo, :], bank(2 + mo), pk(2 + mo), ["yC"])
            P.dma(YTv[:, 4:6, cs], yC, ["yC"], [("YT", 4), ("YT", 5)], q="pool")

        P.barrier(lambda: nc.gpsimd.memset(bar_t[:], 0.0))
        P.skip = not run_phase("D")
        for hp in range(2):
            if hp == 1:
                P.barrier(lambda: nc.gpsimd.memset(bar_t[:], 0.0))
            QTd = fv(b1, 0, [[1, 4096]])
            KTo = {1: 4096, 4: 8192, 16: 12288}
            VTo = {1: 16384, 4: 20480, 16: 24576}
            accD = fv(big2[:], 0, [[4096, 2], [1, 4096]])
            B8 = fv(b2, 24576, [[128, 24], [1, 128]])
            P.dma(B8, B8d.rearrange("k p q -> p k q"), ["B8d"], ["B8"])
            recD = fv(big2[:], 8192, [[1, 4096]])
            yD = fv(b1, 28672, [[1, 4096]])
            load_xT(0, 0)
            for j in range(8):
                if j + 1 < 8:
                    load_xT(j + 1, (j + 1) % 2)
                xc = xTc[j % 2]
                for c in range(8):
                    mm(bank(0), WA[:, c, hp * 128:(hp + 1) * 128], xc[:, c, :], c == 0, c == 7, ["WA", f"xTc{j % 2}"], pk(0))
                for c in range(8):
                    mm(bank(1), WA[:, c, 256 + hp * 128:256 + (hp + 1) * 128], xc[:, c, :], c == 0, c == 7, ["WA", f"xTc{j % 2}"], pk(1))
                cp("act", QTd[:, j * 512:(j + 1) * 512], bank(0), pk(0), [("QTd", j)])
                cp("dve", fv(b1, 4096 + j * 512, [[1, 512]]), bank(1), pk(1), [("KTd", j)])
                cp("act", fv(b1, 8192 + j * 128, [[1024, 4], [1, 128]]), fv(bank(1), 0, [[1, 4], [4, 128]]), pk(1), [("KTd", j)])
                cp("dve", fv(b1, 12288 + j * 32, [[256, 16], [1, 32]]), fv(bank(1), 0, [[1, 16], [16, 32]]), pk(1), [("KTd", j)])
                for tq in range(4):
                    t = j * 4 + tq
                    for c in range(8):
                        mm(bank(2 + tq % 2)[:, 0:128], xc[:, c, tq * 128:(tq + 1) * 128], WA[:, c, 512 + hp * 128:512 + (hp + 1) * 128], c == 0, c == 7,
                           ["WA", f"xTc{j % 2}"], pk(2 + tq % 2))
                    cp("act" if tq % 2 == 0 else "dve", fv(b1, 16384 + t * 128, [[1, 128]]), bank(2 + tq % 2)[:, 0:128], pk(2 + tq % 2), [("Vnat", t)])
                    P.dma(VD[t * 128:(t + 1) * 128, :], fv(b1, 16384 + t * 128, [[1, 128]]), [("Vnat", t)], [("VD", t)], q="pool")
            if hp == 1:
                sk = P.skip
                P.skip = not run_phase("O")
                for hf in range(2):
                    wload(WA[:, :, hf * 512:(hf + 1) * 512], w_out[l].rearrange("(c p) n -> p c n", p=128)[:, :, hf * 512:(hf + 1) * 512], [], ["WA"])
                load_ln_params(ln1_g[l], ln1_b[l])
                P.skip = sk
            vdk = [("VD", t) for t in range(NT)]
            VDr = VD
            for r in range(4):
                src = bass.AP(VDr.tensor, VDr.offset + r * 128, [[4 * 128, 128], [4 * 128 * 128, 8], [1, 128]])
                P.dma(fv(b1, 20480 + r * 8 * 128, [[128, 8], [1, 128]]), src, vdk, [("VD4", r)])
            for r in range(16):
                src = bass.AP(VDr.tensor, VDr.offset + r * 128, [[16 * 128, 128], [16 * 128 * 128, 2], [1, 128]])
                P.dma(fv(b1, 24576 + r * 2 * 128, [[128, 2], [1, 128]]), src, vdk, [("VD16", r)])
            qk_keys = [("QTd", j) for j in range(8)] + [("KTd", j) for j in range(8)]
            PTd = [fv(smb, i * 512, [[1, 512]]) for i in range(3)]
            diters = []
            for di, d in enumerate(DILS):
                for r in range(d):
                    for i in range(-1, (S // d) // 128):
                        diters.append((di, d, r, i))

            def d_info(it):
                di, d, r, i = diters[it]
                Ld = S // d
                ntile = Ld // 128
                q0 = 64 if i == -1 else 0
                q1 = 64 if i == ntile - 1 else 128
                hasA = i >= 0
                hasB = i + 1 <= ntile - 1
                tok0 = r + d * (128 * i + 64 + q0)
                if d == 1:
                    vkeys = [("Vnat", t) for t in range(NT)]
                elif d == 4:
                    vkeys = [("VD4", r)]
                else:
                    vkeys = [("VD16", r)]
                blocks = [(ab, jt) for ab, has, jt in ((0, hasA, i), (1, hasB, i + 1)) if has]
                return di, d, r, i, Ld, ntile, q0, q1, tok0, vkeys, blocks

            def d_S(it):
                di, d, r, i, Ld, ntile, q0, q1, tok0, vkeys, blocks = d_info(it)
                nq = q1 - q0
                for hh in range(2):
                    sbk = (it % 3) * 2 + hh
                    pr = slice(hh * 64, (hh + 1) * 64)
                    rhs_q = fv(b1[pr, :], tok0, [[d, nq]])
                    for ab, jt in blocks:
                        reg = bank(sbk)[:, ab * 128 + q0:ab * 128 + q1]
                        kcol = KTo[d] + r * Ld + jt * 128
                        mm(reg, fv(b1[pr, :], kcol, [[1, 128]]), rhs_q, True, True, qk_keys, pk(sbk))

            def d_E(it):
                di, d, r, i, Ld, ntile, q0, q1, tok0, vkeys, blocks = d_info(it)
                ptb = PTd[it % 3]
                ebase = 24576 + (hp * 2 * 6 + di * 2) * 128
                full = len(blocks) == 2 and q0 == 0 and q1 == 128
                for hh in range(2):
                    sbk = (it % 3) * 2 + hh
                    if full:
                        act(ptb[:, hh * 256:(hh + 1) * 256], bank(sbk)[:, 0:256], AF.Exp, pk(sbk), [f"PTd{it % 3}"], scale=0.125)
                    else:
                        for ab, jt in blocks:
                            col = (hh * 2 + ab) * 128
                            act(ptb[:, col + q0:col + q1], bank(sbk)[:, ab * 128 + q0:ab * 128 + q1], AF.Exp, pk(sbk), [f"PTd{it % 3}"], scale=0.125)
                if full:
                    ptv = fv(smb, (it % 3) * 512, [[256, 2], [128, 2], [1, 128]])
                    tt("dve", ptv, ptv, fv(b2, ebase, [[768, 2], [128, 2], [1, 128]]), ALU.mult, [f"PTd{it % 3}", "B8"], [f"PTd{it % 3}"])
                else:
                    for hh in range(2):
                        for ab, jt in blocks:
                            col = (hh * 2 + ab) * 128
                            tt("dve", ptb[:, col + q0:col + q1], ptb[:, col + q0:col + q1],
                               fv(b2, ebase + hh * 768 + ab * 128 + q0, [[1, q1 - q0]]), ALU.mult, [f"PTd{it % 3}", "B8"], [f"PTd{it % 3}"])

            def d_P(it):
                di, d, r, i, Ld, ntile, q0, q1, tok0, vkeys, blocks = d_info(it)
                nq = q1 - q0
                obk = 6 + it % 2
                ptb = PTd[it % 3]
                for hh in range(2):
                    oreg = bank(obk)[:, hh * 128 + q0:hh * 128 + q1]
                    for bi, (ab, jt) in enumerate(blocks):
                        col = (hh * 2 + ab) * 128
                        vcol = VTo[d] + (r * ntile + jt) * 128 + hh * 64
                        mm(oreg[0:64, :], fv(b1, vcol, [[1, 64]]), ptb[:, col + q0:col + q1], bi == 0, bi == len(blocks) - 1,
                           [f"PTd{it % 3}"] + vkeys, pk(obk))
                    for bi, (ab, jt) in enumerate(blocks):
                        col = (hh * 2 + ab) * 128
                        mm(oreg[64:128, :], ones64[:], ptb[:, col + q0:col + q1], bi == 0, bi == len(blocks) - 1,
                           [f"PTd{it % 3}", "ones64"], pk(obk))
                for hh in range(2):
                    oreg = bank(obk)[:, hh * 128 + q0:hh * 128 + q1]
                    dst = fv(big2[:], hh * 4096 + tok0, [[d, nq]])
                    if di == 0:
                        cp("dve", dst, oreg, pk(obk), [("accD", hh)])
                    else:
                        tt("dve", dst, oreg, dst, ALU.add, pk(obk) + [("accD", hh)], [("accD", hh)])

            d_S(0)
            d_S(1)
            for it in range(len(diters)):
                if it + 2 < len(diters):
                    d_S(it + 2)
                d_E(it)
                d_P(it)
            for hh in range(2):
                cp("dve", recD[0:64, :], accD[64:128, hh, :], [("accD", hh)], ["recD"])
                act(recD[0:64, :], recD[0:64, :], AF.Ln, ["recD"], ["recD"])
                act(recD[0:64, :], recD[0:64, :], AF.Exp, ["recD"], ["recD"], scale=-1.0)
                tt("dve", yD[hh * 64:(hh + 1) * 64, :], accD[0:64, hh, :], recD[0:64, :], ALU.mult, [("accD", hh), "recD"], ["yD"])
            P.dma(YTv[:, 6 + hp, :], yD, ["yD"], [("YT", 6 + hp)], q="pool")

        P.barrier(lambda: nc.gpsimd.memset(bar_t[:], 0.0))

        P.skip = not run_phase("O")
        W1 = fv(b1, 0, [[4096, 8], [1, 4096]])
        W2 = fv(b2, 0, [[1024, 32], [1, 1024]])
        w1v = w_ff1[l].rearrange("(c p) n -> p c n", p=128)
        w2v = w_ff2[l].rearrange("(c p) n -> p c n", p=128)
        wq = []
        for c in range(8):
            for hf in range(2):
                wq.append((W1[:, c, hf * 2048:(hf + 1) * 2048], w1v[:, c, hf * 2048:(hf + 1) * 2048], "W1"))
        for c in range(32):
            wq.append((W2[:, c, :], w2v[:, c, :], "W2"))
        ytk = [("YT", k) for k in range(8)]

        def load_yT(j, buf):
            P.dma(xTc[buf][:], YTv[:, :, j * 512:(j + 1) * 512], ytk, [f"xTc{buf}"])

        def o_mm(tp):
            j = tp // 2
            if tp % 2 == 0 and j + 1 < 8:
                load_yT(j + 1, (j + 1) % 2)
            yc = xTc[j % 2]
            for k in range(2):
                t = 2 * tp + k
                tl = (t % 4) * 128
                P.dma(xt_in[k][:], X[t * 128:(t + 1) * 128, :], [("X", t)], [f"xin{k}"])
                pb = 2 * k
                for hf in range(2):
                    for c in range(8):
                        mm(bank(pb + hf), yc[:, c, tl:tl + 128], WA[:, c, hf * 512:(hf + 1) * 512], c == 0, c == 7, ["WA", f"xTc{j % 2}"], pk(pb + hf))

        def o_ln(tp):
            for _ in range(3):
                if wq:
                    wd, ws, wk = wq.pop(0)
                    wload(wd, ws, [], [wk])
            for k in range(2):
                pb = 2 * k
                stt(rt[k][:], xt_in[k][:], ALPHA, PS[:, pb * 512:pb * 512 + 1024], ALU.mult, ALU.add, [f"xin{k}"] + pk(pb, 2), [f"rt{k}"])
            ln_pair([rt[0][:], rt[1][:]], ["rt0", "rt1"], [rt[0][:], rt[1][:]], ["rt0", "rt1"])
            for k in range(2):
                t = 2 * tp + k
                P.dma(X[t * 128:(t + 1) * 128, :], rt[k][:], [f"rt{k}"], [("X", t)], q="pool")

        def o_tail(tp):
            for k in range(2):
                transpose_tile(rt[k][:], f"rt{k}", xTs[0][:, :, k * 128:(k + 1) * 128], "xTs0", k)
            P.dma(XTv[:, :, tp * 256:(tp + 1) * 256], xTs[0][:], ["xTs0"], [("XT1", tp)], q="pool")

        load_yT(0, 0)
        o_mm(0)
        for tp in range(NT // 2):
            o_ln(tp)
            if tp + 1 < NT // 2:
                o_mm(tp + 1)
            o_tail(tp)

        P.barrier(lambda: nc.gpsimd.memset(bar_t[:], 0.0))
        P.skip = not run_phase("F")
        load_ln_params(ln2_g[l], ln2_b[l])
        last = (l == n_layers - 1)
        hT = fv(WA[:], 0, [[256, 32], [1, 256]])

        def load_x1T(jc, buf):
            P.dma(xTc[buf][:, :, 0:256], XTv[:, :, jc * 256:(jc + 1) * 256], [("XT1", jc)], [f"xTc{buf}"])

        def f_ffn1(jc, h0, h1):
            if h0 == 0 and jc + 1 < 16:
                load_x1T(jc + 1, (jc + 1) % 2)
            xc = xTc[jc % 2]
            for hc in range(h0, h1):
                pb = hc % 3
                for c in range(8):
                    mm(bank(pb)[:, 0:256], W1[:, c, hc * 128:(hc + 1) * 128], xc[:, c, 0:256], c == 0, c == 7, ["W1", f"xTc{jc % 2}"], pk(pb))
                rl = fv(small[:], (hc % 2) * 256, [[1, 256]])
                act(rl, bank(pb)[:, 0:256], AF.Relu, pk(pb), [f"relu{hc % 2}"])
                tt("pool", hT[:, hc, :], rl, rl, ALU.mult, [f"relu{hc % 2}"], [("hT", hc)])

        def f_ffn2(jc):
            for k in range(2):
                t = jc * 2 + k
                P.dma(xt_in[k][:], X[t * 128:(t + 1) * 128, :], [("X", t)], [f"xin{k}"])
                pb = 3 + 2 * k
                for hf in range(2):
                    for hc in range(32):
                        mm(bank(pb + hf), hT[:, hc, k * 128:(k + 1) * 128], W2[:, hc, hf * 512:(hf + 1) * 512], hc == 0, hc == 31, [("hT", hc), "W2"], pk(pb + hf))

        def f_ln_a(jc):
            for k in range(2):
                pb = 3 + 2 * k
                stt(rt[k][:], xt_in[k][:], ALPHA, PS[:, pb * 512:pb * 512 + 1024], ALU.mult, ALU.add, [f"xin{k}"] + pk(pb, 2), [f"rt{k}"])
            ln_pair([rt[0][:], rt[1][:]], ["rt0", "rt1"], [rt[0][:], rt[1][:]], ["rt0", "rt1"], rstd_on_pool=True, part="a")

        def f_ln_b(jc):
            ln_pair([rt[0][:], rt[1][:]], ["rt0", "rt1"], [rt[0][:], rt[1][:]], ["rt0", "rt1"], rstd_on_pool=True, part="b")
            for k in range(2):
                t = jc * 2 + k
                if last:
                    P.dma(out[t * 128:(t + 1) * 128, :], rt[k][:], [f"rt{k}"], ["out"], q="sp")
                else:
                    P.dma(X[t * 128:(t + 1) * 128, :], rt[k][:], [f"rt{k}"], [("X", t)], q="sp")

        def f_tail(jc):
            if last:
                return
            for k in range(2):
                transpose_tile(rt[k][:], f"rt{k}", xTs[0][:, :, k * 128:(k + 1) * 128], "xTs0", k)
            P.dma(XTv[:, :, jc * 256:(jc + 1) * 256], xTs[0][:], ["xTs0"], [("XT", jc // 2)], q="sp")

        load_x1T(0, 0)
        for jc in range(16):
            f_ffn1(jc, 0, 8)
            if jc > 0:
                f_ln_a(jc - 1)
            f_ffn1(jc, 8, 16)
            if jc > 0:
                f_ln_b(jc - 1)
            f_ffn1(jc, 16, 32)
            if jc > 0:
                f_tail(jc - 1)
            f_ffn2(jc)
        f_ln_a(15)
        f_ln_b(15)
        f_tail(15)
        P.barrier(lambda: nc.gpsimd.memset(bar_t[:], 0.0))

    P.skip = False
    fk = ["out"]
    if stop_after is not None and stop_after != "F":
        P.barrier(lambda: nc.gpsimd.memset(bar_t[:], 0.0))
        P.dma(out, X, [], ["out"])
    if debug:
        P.barrier(lambda: nc.gpsimd.memset(bar_t[:], 0.0))
        P.dma(dbg_yt, YT, [], ["dbg_yt"])
        P.dma(dbg_x1, X, [], ["dbg_x1"])
        fk += ["dbg_yt", "dbg_x1"]
    stats = P.emit(final_wait_keys=fk)
    return nc, stats


def _t5_bucket_np(rel):
    nb = 16
    max_exact = 8
    ret = np.where(rel > 0, nb, 0)
    n = np.abs(rel)
    nf = np.maximum(n, 1).astype(np.float32)
    large = max_exact + (np.log(nf / np.float32(max_exact)) / np.float32(math.log(1024 / max_exact))
                         * np.float32(nb - max_exact)).astype(np.int32)
    large = np.minimum(large, nb - 1)
    return ret + np.where(n < max_exact, n, large)


def _constants():
    c = {}
    c["c_ident"] = np.eye(128, dtype=np.float32)
    nf = 16
    inv = (10000.0 ** (-np.arange(nf, dtype=np.float32) / nf)).astype(np.float32)
    t = np.arange(S)
    row = (t // 64).astype(np.float32)
    col = (t % 64).astype(np.float32)
    ang = np.concatenate([row[:, None] * inv, col[:, None] * inv], -1).astype(np.float32)
    c["c_rope"] = np.concatenate([np.cos(ang), np.sin(ang)], -1).astype(np.float32)
    k = np.arange(64)
    C64 = np.cos(2 * np.pi * np.outer(k, k) / 64)
    S64 = np.sin(2 * np.pi * np.outer(k, k) / 64)
    BC = np.kron(np.eye(4), C64) / 8
    BS = np.kron(np.eye(4), S64) / 8
    c["c_bcs"] = np.concatenate([BC, BS], 1).astype(np.float32)
    Rre = np.concatenate([C64, -S64], 0)
    Rim = np.concatenate([-S64, -C64], 0)
    c["c_r"] = np.concatenate([Rre, Rim], 1).astype(np.float32)
    s1 = np.arange(64)[:, None, None]
    k2 = np.arange(64)[None, :, None]
    k1 = np.arange(64)[None, None, :]
    th = 2 * np.pi * ((s1 * (64 * k1 + k2)) % 4096) / 4096
    T3 = np.concatenate([np.cos(th) / 64, np.sin(th) / 64], 0)
    c["c_t3"] = T3.reshape(128, 4096).astype(np.float32)
    p = np.arange(128)[:, None]
    q = np.arange(128)[None, :]
    mA = np.where((p - q >= 0) & (p - q <= 128), 0.0, MASKV)
    mB = np.where((p - q >= -128) & (p - q <= 0), 0.0, MASKV)
    c["c_dmask"] = np.stack([mA, mB]).astype(np.float32)
    return c


def _dbias(rel_bias):
    p = np.arange(128)[:, None]
    q = np.arange(128)[None, :]
    o = np.zeros((4, 3, 2, 128, 128), np.float32)
    for di, d in enumerate(DILS):
        for ab, off in ((0, -64), (1, 64)):
            rel = np.clip(p - q + off, -64, 64)
            idx = _t5_bucket_np(rel * d)
            for h in range(4):
                o[h, di, ab] = rel_bias[idx, h]
    return o


_CACHE = {}


def kernel(**inputs):
    inputs = {k: np.asarray(v) for k, v in inputs.items()}
    if "nc" not in _CACHE:
        _CACHE["nc"] = build_program()
    nc, _ = _CACHE["nc"]
    consts = _constants()
    shared = {k: np.ascontiguousarray(v, dtype=np.float32) for k, v in inputs.items() if k not in ("x", "rel_bias")}
    shared.update(consts)
    shared["dbias"] = _dbias(inputs["rel_bias"].astype(np.float32))
    x = inputs["x"].astype(np.float32)
    in_maps = []
    for c in range(8):
        m = dict(shared)
        m["x"] = np.ascontiguousarray(x[c % 4])
        in_maps.append(m)
    res = run_bass_kernel_spmd(nc, in_maps, core_ids=list(range(8)))
    return np.stack([res.results[c]["out"] for c in range(4)], 0).astype(np.float32)
```

```python
import math
import os
import numpy as np
BSTEP = int(os.environ.get("BSTEP", "99"))
import ml_dtypes
import concourse.bass as bass
import concourse.mybir as mybir
from concourse.bass_utils import run_bass_kernel_spmd

F32 = mybir.dt.float32
BF16 = mybir.dt.bfloat16
AF = mybir.ActivationFunctionType
ALU = mybir.AluOpType
AX = mybir.AxisListType

S = 4096
D = 1024
NT = 32
DEPTH = 4
DFF = 4096
ALPHA = (2 * DEPTH) ** 0.25
LN_EPS = 1e-5
RMS_EPS = 1e-6
DILS = (1, 4, 16)
MASKV = -240000.0

ENGS = ("pe", "act", "dve", "pool", "sp")


class Prog:
    def __init__(self, nc, n_dma_slots=48):
        self.nc = nc
        self.ops = []
        self.n_dma_slots = n_dma_slots
        self.eng_obj = {"pe": nc.tensor, "act": nc.scalar, "dve": nc.vector,
                        "pool": nc.gpsimd, "sp": nc.sync}

    skip = False

    def op(self, eng, fn, reads=(), writes=(), dma=False):
        if self.skip:
            return
        ps_r = tuple(k for k in reads if isinstance(k, tuple) and k[0] == "ps" and k not in writes)
        self.ops.append((eng, fn, tuple(reads), tuple(writes) + ps_r, dma))

    def dma(self, out, in_, reads, writes, q="sp", **kw):
        e = self.eng_obj[q]
        self.op(q, lambda: e.dma_start(out=out, in_=in_, **kw), reads, writes, dma=True)

    def barrier(self, fn):
        if self.skip:
            return
        self.ops.append(("pool", fn, "BARRIER", (), False))

    def emit(self, final_wait_keys=()):
        nc = self.nc
        ops = self.ops
        n = len(ops)
        last_w = {}
        readers = {}
        deps = [None] * n
        bar = None
        last_eng = {}
        dma_since = []
        for i, (eng, fn, reads, writes, is_dma) in enumerate(ops):
            if reads == "BARRIER":
                d = set(last_eng.values()) | set(dma_since)
                if bar is not None:
                    d.add(bar)
                deps[i] = d
                last_w = {}
                readers = {}
                dma_since = []
                last_eng = {}
                bar = i
                ops[i] = (eng, fn, (), (), False)
                continue
            if is_dma:
                dma_since.append(i)
            else:
                last_eng[eng] = i
            d = set()
            if bar is not None:
                d.add(bar)
            for k in reads:
                if k in last_w:
                    d.add(last_w[k])
            for k in writes:
                if k in last_w:
                    d.add(last_w[k])
                for r in readers.get(k, ()):
                    d.add(r)
            d.discard(i)
            deps[i] = d
            for k in reads:
                readers.setdefault(k, []).append(i)
            for k in writes:
                last_w[k] = i
                readers[k] = []
        final_deps = set()
        for k in final_wait_keys:
            if k in last_w:
                final_deps.add(last_w[k])
        needed = set()
        for i in range(n):
            eng_i, _, _, _, dma_i = ops[i]
            keep = set()
            for j in deps[i]:
                eng_j, _, _, _, dma_j = ops[j]
                if (not dma_j) and (not dma_i) and eng_i == "pe" and eng_j == "pe":
                    continue
                keep.add(j)
            deps[i] = keep
            needed |= keep
        needed |= final_deps
        sems = {e: nc.alloc_semaphore(name=f"s_{e}") for e in ENGS}
        slots = [nc.alloc_semaphore(name=f"s_dma{k}") for k in range(self.n_dma_slots)]
        cnt = {e: 0 for e in ENGS}
        slot_use = [0] * self.n_dma_slots
        half = self.n_dma_slots // 2
        slot_rr = {"sw": 0, "hw": 0}
        done_tok = [None] * n
        prev_slot_tok = [None] * n
        for i, (eng, fn, reads, writes, is_dma) in enumerate(ops):
            if is_dma:
                kind = "sw" if eng == "pool" else "hw"
                s = slot_rr[kind] + (half if kind == "sw" else 0)
                slot_rr[kind] = (slot_rr[kind] + 1) % half
                if slot_use[s] > 0:
                    prev_slot_tok[i] = (("slot", s), 16 * slot_use[s])
                slot_use[s] += 1
                done_tok[i] = (("slot", s), 16 * slot_use[s])
            elif i in needed:
                cnt[eng] += 1
                done_tok[i] = (("eng", eng), cnt[eng])
        seen = {e: {} for e in ENGS}

        def semh(key):
            return sems[key[1]] if key[0] == "eng" else slots[key[1]]

        n_wait = 0
        for i, (eng, fn, reads, writes, is_dma) in enumerate(ops):
            e = self.eng_obj[eng]
            want = {}
            for j in deps[i]:
                key, val = done_tok[j]
                if want.get(key, 0) < val:
                    want[key] = val
            if prev_slot_tok[i] is not None:
                key, val = prev_slot_tok[i]
                if want.get(key, 0) < val:
                    want[key] = val
            for key, val in want.items():
                if seen[eng].get(key, 0) >= val:
                    continue
                e.wait_ge(semh(key), val)
                seen[eng][key] = val
                n_wait += 1
            ins = fn()
            if is_dma:
                ins.then_inc(semh(done_tok[i][0]), 16)
            elif done_tok[i] is not None:
                ins.then_inc(semh(done_tok[i][0]), 1)
        e = self.eng_obj["sp"]
        want = {}
        for j in final_deps:
            key, val = done_tok[j]
            if want.get(key, 0) < val:
                want[key] = val
        for key, val in want.items():
            e.wait_ge(semh(key), val)
        self.stats = dict(n_ops=n, n_wait=n_wait, cnt=dict(cnt))
        return self.stats


def fv(ap, off, dims):
    return bass.AP(ap.tensor, ap.offset + off, [list(ap.ap[0])] + [list(d) for d in dims])


def build_program(n_layers=DEPTH, debug=False, stop_after=None):
    nc = bass.Bass("TRN2", target_bir_lowering=False)
    P = Prog(nc)

    def din(name, shape, dt=F32):
        return nc.dram_tensor(name, list(shape), dt, kind="ExternalInput").ap()

    def dscr(name, shape, dt):
        return nc.dram_tensor(name, list(shape), dt, kind="Internal").ap()

    x_in = din("x", [S, D])
    emb_g = din("emb_ln_g", [D]); emb_b = din("emb_ln_b", [D])
    w_in = din("w_in", [DEPTH, D, 2048])
    w_fnet = din("w_fnet", [DEPTH, 256, 256])
    qg = din("q_norm_g", [DEPTH, 64]); kg = din("k_norm_g", [DEPTH, 64])
    conv_dw = din("conv_dw", [DEPTH, 31, 256]); conv_b = din("conv_b", [DEPTH, 256])
    cln_g = din("conv_ln_g", [DEPTH, 256]); cln_b = din("conv_ln_b", [DEPTH, 256])
    w_pw = din("w_conv_out", [DEPTH, 256, 256])
    w_out = din("w_out", [DEPTH, D, D])
    ln1_g = din("ln1_g", [DEPTH, D]); ln1_b = din("ln1_b", [DEPTH, D])
    w_ff1 = din("w_ff1", [DEPTH, D, DFF]); w_ff2 = din("w_ff2", [DEPTH, DFF, D])
    ln2_g = din("ln2_g", [DEPTH, D]); ln2_b = din("ln2_b", [DEPTH, D])
    dbias = din("dbias", [4, 3, 2, 128, 128])
    c_ident = din("c_ident", [128, 128])
    c_rope = din("c_rope", [S, 64])
    c_bcs = din("c_bcs", [256, 512])
    c_r = din("c_r", [128, 128])
    c_t3 = din("c_t3", [128, 4096])
    c_dmask = din("c_dmask", [2, 128, 128])
    out = nc.dram_tensor("out", [S, D], F32, kind="ExternalOutput").ap()

    X = dscr("X", [S, D], F32)
    XT = dscr("XT", [8, 128, S], BF16)
    YT = dscr("YT", [8, 128, S], BF16)
    VD = dscr("VD", [S, 128], BF16)
    B8d = dscr("B8d", [24, 128, 128], BF16)
    if debug:
        dbg_yt = nc.dram_tensor("dbg_yt", [8, 128, S], BF16, kind="ExternalOutput").ap()
        dbg_x1 = nc.dram_tensor("dbg_x1", [S, D], F32, kind="ExternalOutput").ap()

    def sb(name, shape, dt):
        return nc.alloc_sbuf_tensor(name, list(shape), dt)

    PS = nc.alloc_psum_tensor("PS", [128, 4096], F32)

    def bank(k, w=512):
        return PS[:, k * 512:k * 512 + w]

    def pk(k, nb=1):
        return [("ps", k + i) for i in range(nb)]

    ident = sb("ident", [128, 128], BF16)
    identf = sb("identf", [128, 128], F32)
    onesF = sb("onesF", [128, 128], F32)
    ones64 = sb("ones64", [128, 64], BF16)
    mh = sb("mh", [128, 8], F32)
    xTc = [sb(f"xTc{i}", [128, 8, 512], BF16) for i in range(2)]
    WA = sb("WA", [128, 8, 1024], BF16)
    big1 = sb("big1", [128, 16384], F32)
    big2 = sb("big2", [128, 16384], F32)
    lng = sb("lng", [128, 1024], F32); lnb = sb("lnb", [128, 1024], F32)
    rt = [sb(f"rt{i}", [128, 1024], F32) for i in range(2)]
    xt_in = [sb(f"xin{i}", [128, 1024], F32) for i in range(2)]
    xnb = [sb(f"xnb{i}", [128, 1024], BF16) for i in range(2)]
    xTs = [sb("xTs0", [128, 8, 256], BF16)] * 2
    st6 = sb("st6", [128, 2, 2, 6], F32)
    mv2 = sb("mv2", [128, 2, 2], F32)
    rstd2 = sb("rstd2", [128, 2], F32)
    small = sb("small", [128, 2048], F32)
    bar_t = sb("bar_t", [128, 8], F32)

    def mm(o, l, r, st, sp, R, W):
        P.op("pe", lambda: nc.tensor.matmul(o, lhsT=l, rhs=r, start=st, stop=sp), R, W)

    def tr(o, i, idn, R, W):
        P.op("pe", lambda: nc.tensor.transpose(o, i, idn), R, W)

    def act(o, i, f, R, W, scale=None, bias=None):
        kw = {}
        if scale is not None:
            kw["scale"] = scale
        if bias is not None:
            kw["bias"] = bias
        P.op("act", lambda: nc.scalar.activation(out=o, in_=i, func=f, **kw), R, W)

    def cp(eng, o, i, R, W):
        e = P.eng_obj[eng]
        if eng == "act":
            P.op("act", lambda: nc.scalar.copy(out=o, in_=i), R, W)
        else:
            P.op(eng, lambda: e.tensor_copy(out=o, in_=i), R, W)

    def tt(eng, o, a, b, op, R, W):
        e = P.eng_obj[eng]
        P.op(eng, lambda: e.tensor_tensor(out=o, in0=a, in1=b, op=op), R, W)

    def ts(eng, o, a, s1, s2, op0, op1, R, W):
        e = P.eng_obj[eng]
        if op1 is None:
            P.op(eng, lambda: e.tensor_scalar(out=o, in0=a, scalar1=s1, scalar2=None, op0=op0), R, W)
        else:
            P.op(eng, lambda: e.tensor_scalar(out=o, in0=a, scalar1=s1, scalar2=s2, op0=op0, op1=op1), R, W)

    def stt(o, a, s, b, op0, op1, R, W):
        P.op("dve", lambda: nc.vector.scalar_tensor_tensor(out=o, in0=a, scalar=s, in1=b, op0=op0, op1=op1), R, W)

    def memset(eng, o, v, W):
        e = P.eng_obj[eng]
        P.op(eng, lambda: e.memset(o, v), [], W)

    def wload(dst, src, R, W):
        P.dma(dst, src, R, W, q="pool")

    wload(ident[:], c_ident, [], ["ident"])
    P.dma(identf[:], c_ident, [], ["identf"])
    memset("pool", onesF[:], 1.0 / 256.0, ["onesF"])
    memset("pool", ones64[:], 1.0, ["ones64"])
    memset("pool", mh[:], -0.5, ["mh"])
    for h in range(4):
        for di in range(3):
            for ab in range(2):
                k = (h * 3 + di) * 2 + ab
                P.dma(small[:, 0:128], dbias[h, di, ab], [], ["small"])
                P.dma(small[:, 128:256], c_dmask[ab], [], ["small"])
                b8s = fv(small[:].bitcast(BF16), 1024, [[1, 128]])
                stt(small[:, 256:384], small[:, 0:128], 8.0, small[:, 128:256], ALU.mult, ALU.add, ["small"], ["b8f"])
                act(b8s, small[:, 256:384], AF.Exp, ["b8f"], ["b8s"], scale=0.125)
                P.dma(B8d[k], b8s, ["b8s"], ["B8d"])

    def ln_pair(r_aps, rkeys, xn_aps, xnkeys, rstd_on_pool=False, part="ab"):
        for k in range(2 if "a" in part else 0):
            r_ap = r_aps[k]
            P.op("dve", lambda r_ap=r_ap, k=k: nc.vector.bn_stats(out=st6[:, k, 0, :], in_=r_ap[:, 0:512]), [rkeys[k]], [f"st6_{k}"])
            P.op("dve", lambda r_ap=r_ap, k=k: nc.vector.bn_stats(out=st6[:, k, 1, :], in_=r_ap[:, 512:1024]), [rkeys[k]], [f"st6_{k}"])
            P.op("dve", lambda k=k: nc.vector.bn_aggr(out=mv2[:, k, :], in_=st6[:, k].rearrange("p a b -> p (a b)")), [f"st6_{k}"], ["mv2"])
        if "b" not in part:
            return
        if rstd_on_pool:
            ts("pool", rstd2[:], fv(mv2[:], 1, [[2, 2]]), LN_EPS, None, ALU.add, None, ["mv2"], ["rstd2"])
            tt("pool", rstd2[:], rstd2[:], mh[:, 0:2], ALU.pow, ["rstd2", "mh"], ["rstd2"])
        else:
            ts("dve", rstd2[:], fv(mv2[:], 1, [[2, 2]]), LN_EPS, None, ALU.add, None, ["mv2"], ["rstd2"])
            act(rstd2[:], rstd2[:], AF.Sqrt, ["rstd2"], ["rstd2"])
            P.op("dve", lambda: nc.vector.reciprocal(out=rstd2[:], in_=rstd2[:]), ["rstd2"], ["rstd2"])
        for k in range(2):
            ts("dve", xn_aps[k], r_aps[k], mv2[:, k, 0:1], rstd2[:, k:k + 1], ALU.subtract, ALU.mult, [rkeys[k], "mv2", "rstd2"], [xnkeys[k]])
            tt("dve", xn_aps[k], xn_aps[k], lng[:], ALU.mult, [xnkeys[k], "lng"], [xnkeys[k]])
            tt("dve", xn_aps[k], xn_aps[k], lnb[:], ALU.add, [xnkeys[k], "lng"], [xnkeys[k]])

    def transpose_tile(xn_ap, xnkey, dst_ap, dstkey, i):
        cp("act", xnb[i][:], xn_ap, [xnkey], [f"xnb{i}"])
        pst = bank(7).bitcast(BF16)
        for c in range(8):
            tr(pst[:, c * 128:(c + 1) * 128], xnb[i][:, c * 128:(c + 1) * 128], ident[:], [f"xnb{i}", "ident"], pk(7))
        cp("dve", dst_ap, pst.rearrange("p (c t) -> p c t", c=8), pk(7), [dstkey])

    def load_ln_params(g_ap, b_ap):
        P.dma(lng[:], g_ap.partition_broadcast(128), [], ["lng"])
        P.dma(lnb[:], b_ap.partition_broadcast(128), [], ["lng"])

    XTv = XT.rearrange("c p s -> p c s")
    YTv = YT.rearrange("c p s -> p c s")

    def ln_store(xn_tiles, t0, final=False, write_xt=True):
        pass

    load_ln_params(emb_g, emb_b)
    for tp in range(NT // 2):
        for k in range(2):
            t = 2 * tp + k
            P.dma(xt_in[k][:], x_in[t * 128:(t + 1) * 128, :], [], [f"xin{k}"])
        ln_pair([xt_in[0][:], xt_in[1][:]], ["xin0", "xin1"], [rt[0][:], rt[1][:]], ["rt0", "rt1"])
        for k in range(2):
            t = 2 * tp + k
            P.dma(X[t * 128:(t + 1) * 128, :], rt[k][:], [f"rt{k}"], [("X", t)], q="pool")
            transpose_tile(rt[k][:], f"rt{k}", xTs[0][:, :, k * 128:(k + 1) * 128], "xTs0", k)
        P.dma(XTv[:, :, tp * 256:(tp + 1) * 256], xTs[0][:], ["xTs0"], [("XT", tp // 2)], q="pool")

    P.barrier(lambda: nc.gpsimd.memset(bar_t[:], 0.0))

    def load_xT(j, buf):
        P.dma(xTc[buf][:], XTv[:, :, j * 512:(j + 1) * 512], [("XT", j)], [f"xTc{buf}"])

    b1 = big1[:].bitcast(BF16)
    b2 = big2[:].bitcast(BF16)
    smb = small[:].bitcast(BF16)

    order = ["p0", "A", "B0", "B1", "B2", "B", "C", "D", "O", "F"]
    def run_phase(ph):
        return stop_after is None or order.index(ph) <= order.index(stop_after)

    for l in range(n_layers):
        win_v = w_in[l].rearrange("(c p) n -> p c n", p=128)

        P.skip = not run_phase("A")
        wload(WA[:, :, 0:256], win_v[:, :, 0:256], [], ["WA"])
        BCS = fv(smb, 0, [[512, 2], [1, 512]])
        Wf = fv(smb, 1024, [[256, 2], [1, 256]])
        Wcs = fv(smb, 1536, [[512, 2], [1, 512]])
        Rr = fv(smb, 2560, [[1, 128]])
        wload(BCS, c_bcs.rearrange("(c p) n -> p c n", p=128), [], ["small"])
        wload(Wf, w_fnet[l].rearrange("(c p) n -> p c n", p=128), [], ["small"])
        wload(Rr, c_r, [], ["small"])
        T3 = fv(b2, 16384, [[1, 4096]])
        wload(T3, c_t3, [], ["big2"])
        for m in range(2):
            mm(bank(m)[:, 0:256], BCS[:, m, m * 128:(m + 1) * 128], Wf[:, m, :], True, True, ["small"], pk(m))
            mm(bank(m)[:, 256:512], BCS[:, m, 256 + m * 128:256 + (m + 1) * 128], Wf[:, m, :], True, True, ["small"], pk(m))
            cp("dve", Wcs[:, m, :], bank(m), pk(m), ["small"])
        ufT = fv(b1, 0, [[4096, 2], [1, 4096]])
        load_xT(0, 0)
        for j in range(8):
            if j + 1 < 8:
                load_xT(j + 1, (j + 1) % 2)
            xc = xTc[j % 2]
            for m in range(2):
                for c in range(8):
                    mm(bank(m), WA[:, c, m * 128:(m + 1) * 128], xc[:, c, :], c == 0, c == 7, ["WA", f"xTc{j % 2}"], pk(m))
                cp("act" if m == 0 else "dve", ufT[:, m, j * 512:(j + 1) * 512], bank(m), pk(m), [("ufT", j)])
        sk = P.skip
        P.skip = not run_phase("B0")
        wload(WA[:, :, 0:512], win_v[:, :, 256:768], [], ["WA"])
        P.skip = sk
        A_sb = fv(b1, 8192, [[256, 64], [1, 256]])
        ufkeys = [("ufT", j) for j in range(8)]
        for s1 in range(64):
            pb = s1 % 2
            for c in range(2):
                l_ap = fv(b1, c * 4096 + s1, [[64, 64]])
                mm(bank(pb)[0:64, 0:256], l_ap, Wcs[:, c, 0:256], c == 0, c == 1, ufkeys + ["small"], pk(pb))
            for c in range(2):
                l_ap = fv(b1, c * 4096 + s1, [[64, 64]])
                mm(bank(pb)[64:128, 0:256], l_ap, Wcs[:, c, 256:512], c == 0, c == 1, ufkeys + ["small"], pk(pb))
            cp("act" if pb == 0 else "dve", A_sb[:, s1, :], bank(pb)[:, 0:256], pk(pb), ["A_sb"])
        Y_sb = fv(b2, 0, [[64, 256], [1, 64]])
        for cb in range(32):
            pb = cb % 2
            for cc in range(8):
                ch = cb * 8 + cc
                l_ap = fv(b1, 8192 + ch, [[256, 64]])
                mm(bank(pb)[0:64, cc * 64:(cc + 1) * 64], l_ap, Rr[:, 0:64], True, True, ["A_sb", "small"], pk(pb))
                mm(bank(pb)[64:128, cc * 64:(cc + 1) * 64], l_ap, Rr[:, 64:128], True, True, ["A_sb", "small"], pk(pb))
            cp("act" if pb == 0 else "dve", fv(b2, cb * 512, [[1, 512]]), bank(pb), pk(pb), ["Y_sb"])
        yA = fv(b1, 24576, [[4096, 2], [1, 4096]])
        for m in range(2):
            for kb in range(8):
                pb = kb % 2
                for ks in range(8):
                    k2 = kb * 8 + ks
                    l_ap = fv(b2, m * 128 * 64 + k2, [[64, 128]])
                    o_ap = fv(bank(pb), ks, [[8, 64]])
                    mm(o_ap, l_ap, T3[:, k2 * 64:(k2 + 1) * 64], True, True, ["Y_sb", "big2"], pk(pb))
                o_sb = fv(b1, 24576 + m * 4096 + kb * 8, [[64, 64], [1, 8]])
                i_ps = fv(bank(pb), 0, [[8, 64], [1, 8]])
                cp("act" if pb == 0 else "dve", o_sb, i_ps, pk(pb), [("yA", m)])
            P.dma(YTv[:, m, :], yA[:, m, :], [("yA", m)], [("YT", m)], q="pool")

        P.barrier(lambda: nc.gpsimd.memset(bar_t[:], 0.0))
        P.skip = not run_phase("B0")
        QKT = fv(b1, 0, [[4096, 3], [1, 4096]])
        Vaug = fv(b1, 12288, [[256, 32], [128, 2], [1, 128]])
        rope = fv(big2[:], 0, [[64, 32], [1, 64]])
        P.dma(rope, c_rope.rearrange("(t p) f -> p t f", p=128), [], ["rope"])
        for hh in range(4):
            P.dma(fv(big2[:], 2048 + hh * 64, [[1, 64]]), qg[l].partition_broadcast(128), [], ["gqk"])
        for hh in range(2):
            P.dma(fv(big2[:], 2048 + 256 + hh * 64, [[1, 64]]), kg[l].partition_broadcast(128), [], ["gqk"])
        memset("dve", fv(b1, 12288 + 64, [[128, 64], [1, 64]]), 1.0, ["Vaug_ones"])
        QKo, TMPo, SSo, RSo, QBo = 2560, 8704, 14848, 14944, 20480
        P.skip = not run_phase("B1")
        load_xT(0, 0)
        for half in range(2):
            for k in range(16):
                t = half * 16 + k
                j = t // 4
                if t % 4 == 0 and j + 1 < 8:
                    load_xT(j + 1, (j + 1) % 2)
                xc = xTc[j % 2]
                tl = (t % 4) * 128
                pb = t % 2
                for c in range(8):
                    mm(bank(pb), xc[:, c, tl:tl + 128], WA[:, c, 0:512], c == 0, c == 7, ["WA", f"xTc{j % 2}"], pk(pb))
                cp("act", fv(big2[:], QKo + k * 384, [[1, 384]]), bank(pb)[:, 0:384], pk(pb), ["QKh"])
                cp("dve", fv(b1, 12288 + t * 256, [[128, 2], [1, 64]]), fv(bank(pb), 384, [[64, 2], [1, 64]]), pk(pb), [("Vaug", t)])
            QKf = fv(big2[:], QKo, [[1, 6144]])
            TMPf = fv(big2[:], TMPo, [[1, 6144]])
            tt("dve", TMPf, QKf, QKf, ALU.mult, ["QKh"], ["TMP"])
            P.op("dve", lambda: nc.vector.tensor_reduce(out=fv(big2[:], SSo, [[1, 96]]), in_=fv(big2[:], TMPo, [[64, 96], [1, 64]]),
                                                        axis=AX.X, op=ALU.add), ["TMP"], ["ss"])
            ts("pool", fv(big2[:], RSo, [[1, 96]]), fv(big2[:], SSo, [[1, 96]]), 1.0 / 64.0, RMS_EPS, ALU.mult, ALU.add, ["ss"], ["rs"])
            tt("pool", fv(big2[:], RSo, [[1, 96]]), fv(big2[:], RSo, [[1, 96]]), fv(mh[:], 0, [[0, 96]]), ALU.pow, ["rs", "mh"], ["rs"])
            tt("dve", fv(big2[:], QKo, [[64, 96], [1, 64]]), fv(big2[:], QKo, [[64, 96], [1, 64]]), fv(big2[:], RSo, [[1, 96], [0, 64]]),
               ALU.mult, ["QKh", "rs"], ["QKh"])
            tt("dve", fv(big2[:], QKo, [[384, 16], [1, 384]]), fv(big2[:], QKo, [[384, 16], [1, 384]]), fv(big2[:], 2048, [[0, 16], [1, 384]]),
               ALU.mult, ["QKh", "gqk"], ["QKh"])
            for h6 in range(6):
                eng = "dve" if h6 < 3 else "pool"
                x1 = fv(big2[:], QKo + h6 * 64, [[384, 16], [32, 2], [1, 16]])
                x2 = fv(big2[:], QKo + h6 * 64 + 16, [[384, 16], [32, 2], [1, 16]])
                cosb = fv(big2[:], half * 16 * 64, [[64, 16], [16, 2], [1, 16]])
                sinb = fv(big2[:], half * 16 * 64 + 32, [[64, 16], [16, 2], [1, 16]])
                t1 = fv(big2[:], TMPo + h6 * 1024, [[32, 16], [16, 2], [1, 16]])
                t2 = fv(big2[:], TMPo + h6 * 1024 + 512, [[32, 16], [16, 2], [1, 16]])
                slot = [0, 2, 1, 3, 4, 5][h6]
                o1 = fv(b1, QBo + slot * 64, [[384, 16], [32, 2], [1, 16]])
                o2 = fv(b1, QBo + slot * 64 + 16, [[384, 16], [32, 2], [1, 16]])
                tt(eng, t1, x1, cosb, ALU.mult, ["QKh", "rope", "TMP"], [("t1", h6)])
                tt(eng, t2, x2, sinb, ALU.mult, ["QKh", "rope", "TMP"], [("t2", h6)])
                tt(eng, o1, t1, t2, ALU.subtract, [("t1", h6), ("t2", h6)], [("qb", h6)])
                tt(eng, t1, x1, sinb, ALU.mult, ["QKh", "rope", ("qb", h6)], [("t1", h6)])
                tt(eng, t2, x2, cosb, ALU.mult, ["QKh", "rope", ("qb", h6)], [("t2", h6)])
                tt(eng, o2, t1, t2, ALU.add, [("t1", h6), ("t2", h6), "QKh"], [("qb", h6)])
            qbk = [("qb", h6) for h6 in range(6)]
            for k in range(16):
                t = half * 16 + k
                pst = bank(6 + k % 2).bitcast(BF16)
                for k3 in range(3):
                    tr(pst[:, k3 * 128:(k3 + 1) * 128], fv(b1, QBo + k * 384 + k3 * 128, [[1, 128]]), ident[:], qbk + ["ident"], pk(6 + k % 2))
                cp("act" if k % 2 == 0 else "dve", fv(b1, t * 128, [[4096, 3], [1, 128]]), fv(pst, 0, [[128, 3], [1, 128]]), pk(6 + k % 2), [("QKT", t)])
        sk = P.skip
        P.skip = not run_phase("C")
        wload(WA[:, :, 0:512], win_v[:, :, 768:1280], [], ["WA"])
        P.skip = sk
        P.skip = not run_phase("B")
        qkt_keys = [("QKT", t) for t in range(NT)]
        yB = fv(b2, 16384, [[4096, 2], [1, 4096]])
        PT = [fv(smb, i * 1024, [[1, 1024]]) for i in range(2)]
        rec = fv(big2[:], 12288, [[1, 2048]])
        osb = fv(big2[:], 14336, [[1, 2048]])
        iters = [(qc, kt) for qc in range(8) for kt in range(NT)]

        def b_S(it, hb):
            qc, kt = iters[it]
            for g in range(2):
                pr = slice(g * 64, (g + 1) * 64)
                mm(bank(hb * 2 + g), QKT[pr, 2, kt * 128:(kt + 1) * 128], QKT[pr, hb, qc * 512:(qc + 1) * 512], True, True,
                   qkt_keys, pk(hb * 2 + g))

        def b_E(it, hb):
            act(PT[hb], PS[:, hb * 1024:hb * 1024 + 1024], AF.Exp, pk(hb * 2, 2), [f"PT{hb}"], scale=0.125)

        def b_P(it, hb):
            qc, kt = iters[it]
            for g in range(2):
                mm(bank(4 + hb * 2 + g), Vaug[:, kt, g, :], PT[hb][:, g * 512:(g + 1) * 512], kt == 0, kt == NT - 1,
                   [f"PT{hb}", ("Vaug", kt), "Vaug_ones"], pk(4 + hb * 2 + g))
            if kt == NT - 1 and hb == 1:
                for hf in range(2):
                    cp("dve", rec[0:64, hf * 1024:(hf + 1) * 1024], PS[64:128, 2048 + hf * 1024:2048 + (hf + 1) * 1024], pk(4 + 2 * hf, 2), ["rec"])
                    cp("act", osb[0:64, hf * 1024:(hf + 1) * 1024], PS[0:64, 2048 + hf * 1024:2048 + (hf + 1) * 1024], pk(4 + 2 * hf, 2), ["osb"])
                P.op("dve", lambda: nc.vector.reciprocal(out=rec[0:64, :], in_=rec[0:64, :]), ["rec"], ["rec"])
                for hb2 in range(2):
                    for g in range(2):
                        bk = hb2 * 2 + g
                        tt("dve", yB[hb2 * 64:(hb2 + 1) * 64, g, qc * 512:(qc + 1) * 512], osb[0:64, bk * 512:(bk + 1) * 512],
                           rec[0:64, bk * 512:(bk + 1) * 512], ALU.mult, ["osb", "rec"], [("yB", g)])
                if qc == 7:
                    for g in range(2):
                        P.dma(YTv[:, 2 + g, :], yB[:, g, :], [("yB", g)], [("YT", 2 + g)], q="pool")

        b_S(0, 0)
        b_S(0, 1)
        for it in range(len(iters)):
            for hb in range(2):
                b_E(it, hb)
                b_P(it, hb)
                if it + 1 < len(iters):
                    b_S(it + 1, hb)

        P.barrier(lambda: nc.gpsimd.memset(bar_t[:], 0.0))
        P.skip = not run_phase("C")
        HW_ = 4128
        hbuf = fv(big1[:], 0, [[HW_, 2], [1, HW_]])
        accC = fv(big2[:], 0, [[4096, 2], [1, 4096]])
        dwT = fv(big2[:], 8192, [[31, 2], [1, 31]])
        cb_sb = fv(big2[:], 8256, [[1, 2]])
        cg_sb = fv(big2[:], 8258, [[1, 2]])
        cbe_sb = fv(big2[:], 8260, [[1, 2]])
        Wpw = fv(b2, 16640, [[256, 2], [1, 256]])
        dwraw = fv(big2[:], 8704, [[1, 256]])
        P.dma(dwraw[0:31, :], conv_dw[l], [], ["dwraw"])
        for m in range(2):
            tr(bank(6)[:, m * 32:m * 32 + 31], dwraw[0:31, m * 128:(m + 1) * 128], identf[0:31, 0:31], ["dwraw", "identf"], pk(6))
        cp("dve", dwT, fv(bank(6), 0, [[32, 2], [1, 31]]), pk(6), ["cpar"])
        P.dma(cb_sb, conv_b[l].rearrange("(m p) -> p m", p=128), [], ["cpar"], allow_slow_non_contiguous=True)
        P.dma(cg_sb, cln_g[l].rearrange("(m p) -> p m", p=128), [], ["cpar"], allow_slow_non_contiguous=True)
        P.dma(cbe_sb, cln_b[l].rearrange("(m p) -> p m", p=128), [], ["cpar"], allow_slow_non_contiguous=True)
        wload(Wpw, w_pw[l].rearrange("(c p) n -> p c n", p=128), [], ["Wpw"])
        for m in range(2):
            memset("pool", fv(big1[:], m * HW_, [[1, 16]]), 0.0, [("hbuf", "padl")])
            memset("pool", fv(big1[:], m * HW_ + 16 + 4096, [[1, 16]]), 0.0, [("hbuf", "padr")])
        sig = [fv(big1[:], 8256 + i * 512, [[1, 512]]) for i in range(2)]
        HB = 26752
        DG = 18432
        memset("pool", fv(b1, HB, [[1, 16]]), 0.0, [("hb16", "padl")])
        memset("pool", fv(b1, HB + 16 + 4096, [[1, 16]]), 0.0, [("hb16", "padr")])
        for k in range(31):
            ts("pool", fv(b2, DG + k * 128, [[1, 128]]), ident[:], dwT[:, 1, k:k + 1], None, ALU.mult, None, ["ident", "cpar"], ["diag"])
        load_xT(0, 0)
        for j in range(8):
            if j + 1 < 8:
                load_xT(j + 1, (j + 1) % 2)
            xc = xTc[j % 2]
            for m in range(4):
                for c in range(8):
                    mm(bank(m), WA[:, c, m * 128:(m + 1) * 128], xc[:, c, :], c == 0, c == 7, ["WA", f"xTc{j % 2}"], pk(m))
            for m in range(2):
                act(sig[m], bank(2 + m), AF.Sigmoid, pk(2 + m), [f"sig{m}"])
            tt("dve", hbuf[:, 0, 16 + j * 512:16 + (j + 1) * 512], bank(0), sig[0], ALU.mult, pk(0) + ["sig0"], [("hbuf", j)])
            tt("dve", fv(b1, HB + 16 + j * 512, [[1, 512]]), bank(1), sig[1], ALU.mult, pk(1) + ["sig1"], [("hb16", j)])
        sk = P.skip
        P.skip = not run_phase("D")
        wload(WA[:, :, 0:768], win_v[:, :, 1280:2048], [], ["WA"])
        P.skip = sk
        hkeys_all = [("hbuf", j) for j in range(8)] + [("hbuf", "padl"), ("hbuf", "padr")]
        h16keys = [("hb16", j) for j in range(8)] + [("hb16", "padl"), ("hb16", "padr")]
        for pc in range(4):
            o = accC[:, 0, pc * 1024:(pc + 1) * 1024]
            for k in range(31):
                src = hbuf[:, 0, 16 + pc * 1024 + k - 15:16 + pc * 1024 + k - 15 + 1024]
                if k == 0:
                    ts("dve", o, src, dwT[:, 0, 0:1], cb_sb[:, 0:1], ALU.mult, ALU.add, hkeys_all + ["cpar"], [("accC", 0, pc)])
                else:
                    stt(o, src, dwT[:, 0, k:k + 1], o, ALU.mult, ALU.add, hkeys_all + ["cpar"], [("accC", 0, pc)])
            for jj in range(2):
                j = pc * 2 + jj
                pb = 4 + j % 2
                for k in range(31):
                    mm(bank(pb), fv(b2, DG + k * 128, [[1, 128]]), fv(b1, HB + 16 + j * 512 + k - 15, [[1, 512]]), k == 0, k == 30,
                       h16keys + ["diag"], pk(pb))
                act(accC[:, 1, j * 512:(j + 1) * 512], bank(pb), AF.Identity, pk(pb) + ["cpar"], [("accC", 1, pc)], bias=cb_sb[:, 1:2])
        sqb = [fv(big1[:], 9280 + i * 512, [[1, 512]]) for i in range(2)]
        mean_sb = fv(big1[:], 10304, [[1, 512]])
        var_sb = fv(big1[:], 10816, [[1, 512]])
        xh = [fv(big1[:], 11328 + i * 512, [[1, 512]]) for i in range(2)]
        hact = fv(b1, 2 * 12352, [[512, 2], [1, 512]])
        yC = fv(b1, 2 * 12864, [[512, 2], [1, 512]])
        for j in range(8):
            cs = slice(j * 512, (j + 1) * 512)
            akeys = [("accC", m, j // 2) for m in range(2)]
            for m in range(2):
                act(sqb[m], accC[:, m, cs], AF.Square, [("accC", m, j // 2)], [f"sq{m}"])
            for m in range(2):
                mm(bank(0), onesF[:], accC[:, m, cs], m == 0, m == 1, akeys + ["onesF"], pk(0))
            for m in range(2):
                mm(bank(1), onesF[:], sqb[m], m == 0, m == 1, [f"sq{m}", "onesF"], pk(1))
            cp("act", mean_sb, bank(0), pk(0), ["mean"])
            tt("dve", var_sb, mean_sb, mean_sb, ALU.mult, ["mean"], ["var"])
            tt("dve", var_sb, bank(1), var_sb, ALU.subtract, pk(1) + ["var"], ["var"])
            ts("dve", var_sb, var_sb, LN_EPS, None, ALU.add, None, ["var"], ["var"])
            act(var_sb, var_sb, AF.Sqrt, ["var"], ["var"])
            P.op("dve", lambda: nc.vector.reciprocal(out=var_sb, in_=var_sb), ["var"], ["var"])
            for m in range(2):
                tt("dve", xh[m], accC[:, m, cs], mean_sb, ALU.subtract, [("accC", m, j // 2), "mean"], [f"xh{m}"])
                tt("dve", xh[m], xh[m], var_sb, ALU.mult, [f"xh{m}", "var"], [f"xh{m}"])
                act(hact[:, m, :], xh[m], AF.Silu, [f"xh{m}", "cpar"], ["hact"], scale=cg_sb[:, m:m + 1], bias=cbe_sb[:, m:m + 1])
            for mo in range(2):
                for c in range(2):
                    mm(bank(2 + mo), Wpw[:, c, mo * 128:(mo + 1) * 128], hact[:, c, :], c == 0, c == 1, ["hact", "Wpw"], pk(2 + mo))
                cp("dve", yC[:, mo, :], bank(2 + mo), pk(2 + mo), ["yC"])
            P.dma(YTv[:, 4:6, cs], yC, ["yC"], [("YT", 4), ("YT", 5)], q="pool")

        P.barrier(lambda: nc.gpsimd.memset(bar_t[:], 0.0))
        P.skip = not run_phase("D")
        for hp in range(2):
            if hp == 1:
                P.barrier(lambda: nc.gpsimd.memset(bar_t[:], 0.0))
            QTd = fv(b1, 0, [[1, 4096]])
            KTo = {1: 4096, 4: 8192, 16: 12288}
            VTo = {1: 16384, 4: 20480, 16: 24576}
            accD = fv(big2[:], 0, [[4096, 2], [1, 4096]])
            B8 = fv(b2, 24576, [[128, 24], [1, 128]])
            P.dma(B8, B8d.rearrange("k p q -> p k q"), ["B8d"], ["B8"])
            recD = fv(big2[:], 8192, [[1, 4096]])
            yD = fv(b1, 28672, [[1, 4096]])
            load_xT(0, 0)
            for j in range(8):
                if j + 1 < 8:
                    load_xT(j + 1, (j + 1) % 2)
                xc = xTc[j % 2]
                for c in range(8):
                    mm(bank(0), WA[:, c, hp * 128:(hp + 1) * 128], xc[:, c, :], c == 0, c == 7, ["WA", f"xTc{j % 2}"], pk(0))
                for c in range(8):
                    mm(bank(1), WA[:, c, 256 + hp * 128:256 + (hp + 1) * 128], xc[:, c, :], c == 0, c == 7, ["WA", f"xTc{j % 2}"], pk(1))
                cp("act", QTd[:, j * 512:(j + 1) * 512], bank(0), pk(0), [("QTd", j)])
                cp("dve", fv(b1, 4096 + j * 512, [[1, 512]]), bank(1), pk(1), [("KTd", j)])
                cp("act", fv(b1, 8192 + j * 128, [[1024, 4], [1, 128]]), fv(bank(1), 0, [[1, 4], [4, 128]]), pk(1), [("KTd", j)])
                cp("dve", fv(b1, 12288 + j * 32, [[256, 16], [1, 32]]), fv(bank(1), 0, [[1, 16], [16, 32]]), pk(1), [("KTd", j)])
                for tq in range(4):
                    t = j * 4 + tq
                    for c in range(8):
                        mm(bank(2 + tq % 2)[:, 0:128], xc[:, c, tq * 128:(tq + 1) * 128], WA[:, c, 512 + hp * 128:512 + (hp + 1) * 128], c == 0, c == 7,
                           ["WA", f"xTc{j % 2}"], pk(2 + tq % 2))
                    cp("act" if tq % 2 == 0 else "dve", fv(b1, 16384 + t * 128, [[1, 128]]), bank(2 + tq % 2)[:, 0:128], pk(2 + tq % 2), [("Vnat", t)])
                    P.dma(VD[t * 128:(t + 1) * 128, :], fv(b1, 16384 + t * 128, [[1, 128]]), [("Vnat", t)], [("VD", t)], q="pool")
            if hp == 1:
                sk = P.skip
                P.skip = not run_phase("O")
                for hf in range(2):
                    wload(WA[:, :, hf * 512:(hf + 1) * 512], w_out[l].rearrange("(c p) n -> p c n", p=128)[:, :, hf * 512:(hf + 1) * 512], [], ["WA"])
                load_ln_params(ln1_g[l], ln1_b[l])
                P.skip = sk
            vdk = [("VD", t) for t in range(NT)]
            VDr = VD
            for r in range(4):
                src = bass.AP(VDr.tensor, VDr.offset + r * 128, [[4 * 128, 128], [4 * 128 * 128, 8], [1, 128]])
                P.dma(fv(b1, 20480 + r * 8 * 128, [[128, 8], [1, 128]]), src, vdk, [("VD4", r)])
            for r in range(16):
                src = bass.AP(VDr.tensor, VDr.offset + r * 128, [[16 * 128, 128], [16 * 128 * 128, 2], [1, 128]])
                P.dma(fv(b1, 24576 + r * 2 * 128, [[128, 2], [1, 128]]), src, vdk, [("VD16", r)])
            qk_keys = [("QTd", j) for j in range(8)] + [("KTd", j) for j in range(8)]
            PTd = [fv(smb, i * 512, [[1, 512]]) for i in range(3)]
            diters = []
            for di, d in enumerate(DILS):
                for r in range(d):
                    for i in range(-1, (S // d) // 128):
                        diters.append((di, d, r, i))

            def d_info(it):
                di, d, r, i = diters[it]
                Ld = S // d
                ntile = Ld // 128
                q0 = 64 if i == -1 else 0
                q1 = 64 if i == ntile - 1 else 128
                hasA = i >= 0
                hasB = i + 1 <= ntile - 1
                tok0 = r + d * (128 * i + 64 + q0)
                if d == 1:
                    vkeys = [("Vnat", t) for t in range(NT)]
                elif d == 4:
                    vkeys = [("VD4", r)]
                else:
                    vkeys = [("VD16", r)]
                blocks = [(ab, jt) for ab, has, jt in ((0, hasA, i), (1, hasB, i + 1)) if has]
                return di, d, r, i, Ld, ntile, q0, q1, tok0, vkeys, blocks

            def d_S(it):
                di, d, r, i, Ld, ntile, q0, q1, tok0, vkeys, blocks = d_info(it)
                nq = q1 - q0
                for hh in range(2):
                    sbk = (it % 3) * 2 + hh
                    pr = slice(hh * 64, (hh + 1) * 64)
                    rhs_q = fv(b1[pr, :], tok0, [[d, nq]])
                    for ab, jt in blocks:
                        reg = bank(sbk)[:, ab * 128 + q0:ab * 128 + q1]
                        kcol = KTo[d] + r * Ld + jt * 128
                        mm(reg, fv(b1[pr, :], kcol, [[1, 128]]), rhs_q, True, True, qk_keys, pk(sbk))

            def d_E(it):
                di, d, r, i, Ld, ntile, q0, q1, tok0, vkeys, blocks = d_info(it)
                ptb = PTd[it % 3]
                ebase = 24576 + (hp * 2 * 6 + di * 2) * 128
                full = len(blocks) == 2 and q0 == 0 and q1 == 128
                for hh in range(2):
                    sbk = (it % 3) * 2 + hh
                    if full:
                        act(ptb[:, hh * 256:(hh + 1) * 256], bank(sbk)[:, 0:256], AF.Exp, pk(sbk), [f"PTd{it % 3}"], scale=0.125)
                    else:
                        for ab, jt in blocks:
                            col = (hh * 2 + ab) * 128
                            act(ptb[:, col + q0:col + q1], bank(sbk)[:, ab * 128 + q0:ab * 128 + q1], AF.Exp, pk(sbk), [f"PTd{it % 3}"], scale=0.125)
                if full:
                    ptv = fv(smb, (it % 3) * 512, [[256, 2], [128, 2], [1, 128]])
                    tt("dve", ptv, ptv, fv(b2, ebase, [[768, 2], [128, 2], [1, 128]]), ALU.mult, [f"PTd{it % 3}", "B8"], [f"PTd{it % 3}"])
                else:
                    for hh in range(2):
                        for ab, jt in blocks:
                            col = (hh * 2 + ab) * 128
                            tt("dve", ptb[:, col + q0:col + q1], ptb[:, col + q0:col + q1],
                               fv(b2, ebase + hh * 768 + ab * 128 + q0, [[1, q1 - q0]]), ALU.mult, [f"PTd{it % 3}", "B8"], [f"PTd{it % 3}"])

            def d_P(it):
                di, d, r, i, Ld, ntile, q0, q1, tok0, vkeys, blocks = d_info(it)
                nq = q1 - q0
                obk = 6 + it % 2
                ptb = PTd[it % 3]
                for hh in range(2):
                    oreg = bank(obk)[:, hh * 128 + q0:hh * 128 + q1]
                    for bi, (ab, jt) in enumerate(blocks):
                        col = (hh * 2 + ab) * 128
                        vcol = VTo[d] + (r * ntile + jt) * 128 + hh * 64
                        mm(oreg[0:64, :], fv(b1, vcol, [[1, 64]]), ptb[:, col + q0:col + q1], bi == 0, bi == len(blocks) - 1,
                           [f"PTd{it % 3}"] + vkeys, pk(obk))
                    for bi, (ab, jt) in enumerate(blocks):
                        col = (hh * 2 + ab) * 128
                        mm(oreg[64:128, :], ones64[:], ptb[:, col + q0:col + q1], bi == 0, bi == len(blocks) - 1,
                           [f"PTd{it % 3}", "ones64"], pk(obk))
                for hh in range(2):
                    oreg = bank(obk)[:, hh * 128 + q0:hh * 128 + q1]
                    dst = fv(big2[:], hh * 4096 + tok0, [[d, nq]])
                    if di == 0:
                        cp("dve", dst, oreg, pk(obk), [("accD", hh)])
                    else:
                        tt("dve", dst, oreg, dst, ALU.add, pk(obk) + [("accD", hh)], [("accD", hh)])

            d_S(0)
            d_S(1)
            for it in range(len(diters)):
                if it + 2 < len(diters):
                    d_S(it + 2)
                d_E(it)
                d_P(it)
            for hh in range(2):
                cp("dve", recD[0:64, :], accD[64:128, hh, :], [("accD", hh)], ["recD"])
                act(recD[0:64, :], recD[0:64, :], AF.Ln, ["recD"], ["recD"])
                act(recD[0:64, :], recD[0:64, :], AF.Exp, ["recD"], ["recD"], scale=-1.0)
                tt("dve", yD[hh * 64:(hh + 1) * 64, :], accD[0:64, hh, :], recD[0:64, :], ALU.mult, [("accD", hh), "recD"], ["yD"])
            P.dma(YTv[:, 6 + hp, :], yD, ["yD"], [("YT", 6 + hp)], q="pool")

        P.barrier(lambda: nc.gpsimd.memset(bar_t[:], 0.0))

        P.skip = not run_phase("O")
        W1 = fv(b1, 0, [[4096, 8], [1, 4096]])
        W2 = fv(b2, 0, [[1024, 32], [1, 1024]])
        w1v = w_ff1[l].rearrange("(c p) n -> p c n", p=128)
        w2v = w_ff2[l].rearrange("(c p) n -> p c n", p=128)
        wq = []
        for c in range(8):
            for hf in range(2):
                wq.append((W1[:, c, hf * 2048:(hf + 1) * 2048], w1v[:, c, hf * 2048:(hf + 1) * 2048], "W1"))
        for c in range(32):
            wq.append((W2[:, c, :], w2v[:, c, :], "W2"))
        ytk = [("YT", k) for k in range(8)]

        def load_yT(j, buf):
            P.dma(xTc[buf][:], YTv[:, :, j * 512:(j + 1) * 512], ytk, [f"xTc{buf}"])

        def o_mm(tp):
            j = tp // 2
            if tp % 2 == 0 and j + 1 < 8:
                load_yT(j + 1, (j + 1) % 2)
            yc = xTc[j % 2]
            for k in range(2):
                t = 2 * tp + k
                tl = (t % 4) * 128
                P.dma(xt_in[k][:], X[t * 128:(t + 1) * 128, :], [("X", t)], [f"xin{k}"])
                pb = 2 * k
                for hf in range(2):
                    for c in range(8):
                        mm(bank(pb + hf), yc[:, c, tl:tl + 128], WA[:, c, hf * 512:(hf + 1) * 512], c == 0, c == 7, ["WA", f"xTc{j % 2}"], pk(pb + hf))

        def o_ln(tp):
            for k in range(2):
                pb = 2 * k
                stt(rt[k][:], xt_in[k][:], ALPHA, PS[:, pb * 512:pb * 512 + 1024], ALU.mult, ALU.add, [f"xin{k}"] + pk(pb, 2), [f"rt{k}"])
            ln_pair([rt[0][:], rt[1][:]], ["rt0", "rt1"], [rt[0][:], rt[1][:]], ["rt0", "rt1"])
            for k in range(2):
                t = 2 * tp + k
                P.dma(X[t * 128:(t + 1) * 128, :], rt[k][:], [f"rt{k}"], [("X", t)], q="pool")

        def o_tail(tp):
            for k in range(2):
                transpose_tile(rt[k][:], f"rt{k}", xTs[0][:, :, k * 128:(k + 1) * 128], "xTs0", k)
            P.dma(XTv[:, :, tp * 256:(tp + 1) * 256], xTs[0][:], ["xTs0"], [("XT1", tp)], q="pool")
            for _ in range(3):
                if wq:
                    wd, ws, wk = wq.pop(0)
                    wload(wd, ws, [], [wk])

        load_yT(0, 0)
        o_mm(0)
        for tp in range(NT // 2):
            o_ln(tp)
            if tp + 1 < NT // 2:
                o_mm(tp + 1)
            o_tail(tp)

        P.skip = not run_phase("F")
        load_ln_params(ln2_g[l], ln2_b[l])
        last = (l == n_layers - 1)
        hT = fv(WA[:], 0, [[256, 32], [1, 256]])

        def load_x1T(jc, buf):
            P.dma(xTc[buf][:, :, 0:256], XTv[:, :, jc * 256:(jc + 1) * 256], [("XT1", jc)], [f"xTc{buf}"])

        def f_ffn1(jc, h0, h1):
            if h0 == 0 and jc + 1 < 16:
                load_x1T(jc + 1, (jc + 1) % 2)
            xc = xTc[jc % 2]
            for hc in range(h0, h1):
                pb = hc % 3
                for c in range(8):
                    mm(bank(pb)[:, 0:256], W1[:, c, hc * 128:(hc + 1) * 128], xc[:, c, 0:256], c == 0, c == 7, ["W1", f"xTc{jc % 2}"], pk(pb))
                rl = fv(small[:], (hc % 2) * 256, [[1, 256]])
                act(rl, bank(pb)[:, 0:256], AF.Relu, pk(pb), [f"relu{hc % 2}"])
                tt("pool", hT[:, hc, :], rl, rl, ALU.mult, [f"relu{hc % 2}"], [("hT", hc)])

        def f_ffn2(jc):
            for k in range(2):
                t = jc * 2 + k
                P.dma(xt_in[k][:], X[t * 128:(t + 1) * 128, :], [("X", t)], [f"xin{k}"])
                pb = 3 + 2 * k
                for hf in range(2):
                    for hc in range(32):
                        mm(bank(pb + hf), hT[:, hc, k * 128:(k + 1) * 128], W2[:, hc, hf * 512:(hf + 1) * 512], hc == 0, hc == 31, [("hT", hc), "W2"], pk(pb + hf))

        def f_ln_a(jc):
            for k in range(2):
                pb = 3 + 2 * k
                stt(rt[k][:], xt_in[k][:], ALPHA, PS[:, pb * 512:pb * 512 + 1024], ALU.mult, ALU.add, [f"xin{k}"] + pk(pb, 2), [f"rt{k}"])
            ln_pair([rt[0][:], rt[1][:]], ["rt0", "rt1"], [rt[0][:], rt[1][:]], ["rt0", "rt1"], rstd_on_pool=True, part="a")

        def f_ln_b(jc):
            ln_pair([rt[0][:], rt[1][:]], ["rt0", "rt1"], [rt[0][:], rt[1][:]], ["rt0", "rt1"], rstd_on_pool=True, part="b")
            for k in range(2):
                t = jc * 2 + k
                if last:
                    P.dma(out[t * 128:(t + 1) * 128, :], rt[k][:], [f"rt{k}"], ["out"], q="sp")
                else:
                    P.dma(X[t * 128:(t + 1) * 128, :], rt[k][:], [f"rt{k}"], [("X", t)], q="sp")

        def f_tail(jc):
            if last:
                return
            for k in range(2):
                transpose_tile(rt[k][:], f"rt{k}", xTs[0][:, :, k * 128:(k + 1) * 128], "xTs0", k)
            P.dma(XTv[:, :, jc * 256:(jc + 1) * 256], xTs[0][:], ["xTs0"], [("XT", jc // 2)], q="sp")

        load_x1T(0, 0)
        for jc in range(16):
            f_ffn1(jc, 0, 8)
            if jc > 0:
                f_ln_a(jc - 1)
            f_ffn1(jc, 8, 16)
            if jc > 0:
                f_ln_b(jc - 1)
            f_ffn1(jc, 16, 32)
            if jc > 0:
                f_tail(jc - 1)
            f_ffn2(jc)
        f_ln_a(15)
        f_ln_b(15)
        f_tail(15)
        P.barrier(lambda: nc.gpsimd.memset(bar_t[:], 0.0))

    P.skip = False
    fk = ["out"]
    if stop_after is not None and stop_after != "F":
        P.barrier(lambda: nc.gpsimd.memset(bar_t[:], 0.0))
        P.dma(out, X, [], ["out"])
    if debug:
        P.barrier(lambda: nc.gpsimd.memset(bar_t[:], 0.0))
        P.dma(dbg_yt, YT, [], ["dbg_yt"])
        P.dma(dbg_x1, X, [], ["dbg_x1"])
        fk += ["dbg_yt", "dbg_x1"]
    stats = P.emit(final_wait_keys=fk)
    return nc, stats


def _t5_bucket_np(rel):
    nb = 16
    max_exact = 8
    ret = np.where(rel > 0, nb, 0)
    n = np.abs(rel)
    nf = np.maximum(n, 1).astype(np.float32)
    large = max_exact + (np.log(nf / np.float32(max_exact)) / np.float32(math.log(1024 / max_exact))
                         * np.float32(nb - max_exact)).astype(np.int32)
    large = np.minimum(large, nb - 1)
    return ret + np.where(n < max_exact, n, large)


def _constants():
    c = {}
    c["c_ident"] = np.eye(128, dtype=np.float32)
    nf = 16
    inv = (10000.0 ** (-np.arange(nf, dtype=np.float32) / nf)).astype(np.float32)
    t = np.arange(S)
    row = (t // 64).astype(np.float32)
    col = (t % 64).astype(np.float32)
    ang = np.concatenate([row[:, None] * inv, col[:, None] * inv], -1).astype(np.float32)
    c["c_rope"] = np.concatenate([np.cos(ang), np.sin(ang)], -1).astype(np.float32)
    k = np.arange(64)
    C64 = np.cos(2 * np.pi * np.outer(k, k) / 64)
    S64 = np.sin(2 * np.pi * np.outer(k, k) / 64)
    BC = np.kron(np.eye(4), C64) / 8
    BS = np.kron(np.eye(4), S64) / 8
    c["c_bcs"] = np.concatenate([BC, BS], 1).astype(np.float32)
    Rre = np.concatenate([C64, -S64], 0)
    Rim = np.concatenate([-S64, -C64], 0)
    c["c_r"] = np.concatenate([Rre, Rim], 1).astype(np.float32)
    s1 = np.arange(64)[:, None, None]
    k2 = np.arange(64)[None, :, None]
    k1 = np.arange(64)[None, None, :]
    th = 2 * np.pi * ((s1 * (64 * k1 + k2)) % 4096) / 4096
    T3 = np.concatenate([np.cos(th) / 64, np.sin(th) / 64], 0)
    c["c_t3"] = T3.reshape(128, 4096).astype(np.float32)
    p = np.arange(128)[:, None]
    q = np.arange(128)[None, :]
    mA = np.where((p - q >= 0) & (p - q <= 128), 0.0, MASKV)
    mB = np.where((p - q >= -128) & (p - q <= 0), 0.0, MASKV)
    c["c_dmask"] = np.stack([mA, mB]).astype(np.float32)
    return c


def _dbias(rel_bias):
    p = np.arange(128)[:, None]
    q = np.arange(128)[None, :]
    o = np.zeros((4, 3, 2, 128, 128), np.float32)
    for di, d in enumerate(DILS):
        for ab, off in ((0, -64), (1, 64)):
            rel = np.clip(p - q + off, -64, 64)
            idx = _t5_bucket_np(rel * d)
            for h in range(4):
                o[h, di, ab] = rel_bias[idx, h]
    return o


_CACHE = {}


def kernel(**inputs):
    inputs = {k: np.asarray(v) for k, v in inputs.items()}
    if "nc" not in _CACHE:
        _CACHE["nc"] = build_program()
    nc, _ = _CACHE["nc"]
    consts = _constants()
    shared = {k: np.ascontiguousarray(v, dtype=np.float32) for k, v in inputs.items() if k not in ("x", "rel_bias")}
    shared.update(consts)
    shared["dbias"] = _dbias(inputs["rel_bias"].astype(np.float32))
    x = inputs["x"].astype(np.float32)
    in_maps = []
    for c in range(8):
        m = dict(shared)
        m["x"] = np.ascontiguousarray(x[c % 4])
        in_maps.append(m)
    res = run_bass_kernel_spmd(nc, in_maps, core_ids=list(range(8)))
    return np.stack([res.results[c]["out"] for c in range(4)], 0).astype(np.float32)
```
